# Optimizing a Trainium2 kernel written in Bass

```python
import jax, jax.numpy as jnp
from jax import lax
import numpy as np

D_MODEL = 1024
BATCH = 8
SEQ = 2048
DEPTH = 1

N_MEM = 256
NORM_EPS = 1e-6
NEG_INF = -1e30

MIX_WIDTH = D_MODEL
MOBA_WIDTH = MIX_WIDTH // 2
RET_WIDTH = MIX_WIDTH - MOBA_WIDTH

MOBA_HEAD_DIM = 64
MOBA_HEADS = MOBA_WIDTH // MOBA_HEAD_DIM
MOBA_BLOCK = 256
MOBA_TOPK = 3
MOBA_Q_BLOCK = 32
ROPE_THETA = 500000.0
ROPE_DIM = MOBA_HEAD_DIM // 4

RET_HEADS = 4
RET_V_DIM = RET_WIDTH // RET_HEADS
RET_QK_DIM = RET_V_DIM // 2
RET_CHUNK = 128
RET_THETA = 10000.0

CROSS_HEADS = 4
CROSS_HEAD_DIM = D_MODEL // CROSS_HEADS

D_FF = -(-8 * D_MODEL // (3 * 256)) * 256

W_MQ = MOBA_WIDTH
W_MK = MOBA_WIDTH
W_MV = MOBA_WIDTH
W_RQ = RET_HEADS * RET_QK_DIM
W_RK = RET_HEADS * RET_QK_DIM
W_RV = RET_WIDTH
W_RG = RET_WIDTH
IN_PROJ_WIDTH = W_MQ + W_MK + W_MV + W_RQ + W_RK + W_RV + W_RG
SPLIT_POINTS = [W_MQ, W_MQ + W_MK, W_MQ + W_MK + W_MV,
                W_MQ + W_MK + W_MV + W_RQ,
                W_MQ + W_MK + W_MV + W_RQ + W_RK,
                W_MQ + W_MK + W_MV + W_RQ + W_RK + W_RV]

kernel_name = "hymba_moba_retnet_sandwich_layer"


def rms_norm(x, g):
    xf = x.astype(jnp.float32)
    y = xf * lax.rsqrt(jnp.mean(xf * xf, axis=-1, keepdims=True) + NORM_EPS)
    return (y * g.astype(jnp.float32)).astype(x.dtype)


def rotary(x, inv_freq, rot_dim):
    S = x.shape[2]
    half = rot_dim // 2
    ang = jnp.arange(S, dtype=jnp.float32)[:, None] * inv_freq[None, :]
    cos, sin = jnp.cos(ang), jnp.sin(ang)
    xr = x[..., :rot_dim].astype(jnp.float32)
    x1, x2 = xr[..., :half], xr[..., half:]
    rot = jnp.concatenate([x1 * cos - x2 * sin, x2 * cos + x1 * sin], axis=-1).astype(x.dtype)
    return jnp.concatenate([rot, x[..., rot_dim:]], axis=-1)


def split_heads(t, n_heads):
    B, S, _ = t.shape
    return t.reshape(B, S, n_heads, -1).transpose(0, 2, 1, 3)


def merge_heads(t):
    B, H, S, d = t.shape
    return t.transpose(0, 2, 1, 3).reshape(B, S, H * d)


def moba_attention(q, k, v):
    B, H, S, dh = q.shape
    L = MOBA_BLOCK
    nb = -(-S // L)
    pad = nb * L - S
    kp = jnp.pad(k, ((0, 0), (0, 0), (0, pad), (0, 0)))
    vp = jnp.pad(v, ((0, 0), (0, 0), (0, pad), (0, 0)))
    k_blocks = kp.reshape(B, H, nb, L, dh)
    v_blocks = vp.reshape(B, H, nb, L, dh)
    k_mean = jnp.mean(k_blocks.astype(jnp.float32), axis=3).astype(q.dtype)
    n_sel = min(MOBA_TOPK, nb)
    scale = dh ** -0.5
    QB = MOBA_Q_BLOCK
    nq = S // QB
    q_blocks = q.reshape(B, H, nq, QB, dh).transpose(2, 0, 1, 3, 4)
    b_idx = jnp.arange(B)[:, None, None, None]
    h_idx = jnp.arange(H)[None, :, None, None]
    block_ids = jnp.arange(nb)
    slot_ids = jnp.arange(n_sel)
    key_offsets = jnp.arange(L)

    def one_query_block(args):
        qb, qi = args
        q_pos = qi * QB + jnp.arange(QB)
        cur = (qi * QB) // L
        gate = jnp.einsum('bhqd,bhnd->bhqn', qb, k_mean).astype(jnp.float32)
        gate = jnp.where((block_ids < cur)[None, None, None, :], gate, NEG_INF)
        _, sel = lax.top_k(gate, n_sel)
        slot_valid = slot_ids < cur
        k_sel = k_blocks[b_idx, h_idx, sel]
        v_sel = v_blocks[b_idx, h_idx, sel]
        s_sel = jnp.einsum('bhqd,bhqnld->bhqnl', qb, k_sel).astype(jnp.float32) * scale
        s_sel = jnp.where(slot_valid[:, None], s_sel, NEG_INF)
        k_own = lax.dynamic_index_in_dim(k_blocks, cur, axis=2, keepdims=False)
        v_own = lax.dynamic_index_in_dim(v_blocks, cur, axis=2, keepdims=False)
        s_own = jnp.einsum('bhqd,bhld->bhql', qb, k_own).astype(jnp.float32) * scale
        own_pos = cur * L + key_offsets
        s_own = jnp.where(own_pos[None, :] <= q_pos[:, None], s_own, NEG_INF)
        logits = jnp.concatenate([s_sel.reshape(B, H, QB, n_sel * L), s_own], axis=-1)
        p = jax.nn.softmax(logits, axis=-1)
        p_sel = p[..., :n_sel * L].reshape(B, H, QB, n_sel, L).astype(v.dtype)
        p_own = p[..., n_sel * L:].astype(v.dtype)
        return (jnp.einsum('bhqnl,bhqnld->bhqd', p_sel, v_sel)
                + jnp.einsum('bhql,bhld->bhqd', p_own, v_own))

    out = lax.map(one_query_block, (q_blocks, jnp.arange(nq)))
    return out.transpose(1, 2, 0, 3, 4).reshape(B, H, S, dh)


def retention(q, k, v):
    B, H, S, dk = q.shape
    dv = v.shape[-1]
    C = RET_CHUNK
    nc = S // C
    log_g = jnp.log(1.0 - jnp.power(2.0, -5.0 - jnp.arange(H, dtype=jnp.float32)))
    idx = jnp.arange(C, dtype=jnp.float32)
    diff = idx[:, None] - idx[None, :]
    inner_decay = jnp.where(diff >= 0, jnp.exp(log_g[:, None, None] * jnp.maximum(diff, 0.0)), 0.0)
    q_decay = jnp.exp(log_g[:, None] * (idx + 1.0))[None, :, :, None]
    k_decay = jnp.exp(log_g[:, None] * (C - 1.0 - idx))[None, :, :, None]
    chunk_decay = jnp.exp(log_g * C)[None, :, None, None]

    def to_chunks(t):
        return t.astype(jnp.float32).reshape(B, H, nc, C, t.shape[-1]).transpose(2, 0, 1, 3, 4)

    def step(state, inp):
        qc, kc, vc = inp
        attn = jnp.einsum('bhid,bhjd->bhij', qc, kc) * inner_decay[None]
        inner = jnp.einsum('bhij,bhjv->bhiv', attn, vc)
        cross = jnp.einsum('bhid,bhdv->bhiv', qc * q_decay, state)
        new_state = state * chunk_decay + jnp.einsum('bhjd,bhjv->bhdv', kc * k_decay, vc)
        return new_state, inner + cross

    state0 = jnp.zeros((B, H, dk, dv), jnp.float32)
    _, out = lax.scan(step, state0, (to_chunks(q), to_chunks(k), to_chunks(v)))
    return out.transpose(1, 2, 0, 3, 4).reshape(B, H, S, dv)


def cross_attention(h, mem_n, w_cq, w_ckv, w_co):
    q = split_heads(h @ w_cq, CROSS_HEADS)
    kv = mem_n @ w_ckv
    k = split_heads(kv[..., :D_MODEL], CROSS_HEADS)
    v = split_heads(kv[..., D_MODEL:], CROSS_HEADS)
    s = jnp.einsum('bhsd,bhmd->bhsm', q, k).astype(jnp.float32) * (CROSS_HEAD_DIM ** -0.5)
    p = jax.nn.softmax(s, axis=-1).astype(v.dtype)
    o = jnp.einsum('bhsm,bhmd->bhsd', p, v)
    return merge_heads(o) @ w_co


def setup_inputs(seed: int = 0) -> dict:
    key = jax.random.key(seed)
    ks = jax.random.split(key, 16)

    def w(k, shape, fan_in):
        return jax.random.normal(k, shape, jnp.float32) * (fan_in ** -0.5)

    def gain(k):
        return 1.0 + 0.05 * jax.random.normal(k, (DEPTH, D_MODEL), jnp.float32)

    return {
        "x": jax.random.normal(ks[0], (BATCH, SEQ, D_MODEL), jnp.float32),
        "mem": jax.random.normal(ks[1], (BATCH, N_MEM, D_MODEL), jnp.float32),
        "g_pre_mix": gain(ks[2]),
        "w_in": w(ks[3], (DEPTH, D_MODEL, IN_PROJ_WIDTH), D_MODEL),
        "w_out": w(ks[4], (DEPTH, MIX_WIDTH, D_MODEL), MIX_WIDTH),
        "g_post_mix": gain(ks[5]),
        "g_pre_cross": gain(ks[6]),
        "g_mem": gain(ks[7]),
        "w_cq": w(ks[8], (DEPTH, D_MODEL, D_MODEL), D_MODEL),
        "w_ckv": w(ks[9], (DEPTH, D_MODEL, 2 * D_MODEL), D_MODEL),
        "w_co": w(ks[10], (DEPTH, D_MODEL, D_MODEL), D_MODEL),
        "g_post_cross": gain(ks[11]),
        "g_pre_ffn": gain(ks[12]),
        "w_gate_up": w(ks[13], (DEPTH, D_MODEL, 2 * D_FF), D_MODEL),
        "w_down": w(ks[14], (DEPTH, D_FF, D_MODEL), D_FF),
        "g_post_ffn": gain(ks[15]),
    }


def reference(x, mem, g_pre_mix, w_in, w_out, g_post_mix, g_pre_cross, g_mem, w_cq, w_ckv,
              w_co, g_post_cross, g_pre_ffn, w_gate_up, w_down, g_post_ffn):
    moba_inv = jnp.power(ROPE_THETA, -jnp.arange(ROPE_DIM // 2, dtype=jnp.float32) * 2.0 / ROPE_DIM)
    ret_inv = 1.0 / jnp.power(RET_THETA, jnp.linspace(0.0, 1.0, RET_QK_DIM // 2, dtype=jnp.float32))
    for l in range(DEPTH):
        h = rms_norm(x, g_pre_mix[l])
        proj = h @ w_in[l]
        mq, mk, mv, rq, rk, rv, rg = jnp.split(proj, SPLIT_POINTS, axis=-1)
        mq = rotary(split_heads(mq, MOBA_HEADS), moba_inv, ROPE_DIM)
        mk = rotary(split_heads(mk, MOBA_HEADS), moba_inv, ROPE_DIM)
        mo = merge_heads(moba_attention(mq, mk, split_heads(mv, MOBA_HEADS)))
        rq = rotary(split_heads(rq, RET_HEADS), ret_inv, RET_QK_DIM)
        rk = rotary(split_heads(rk, RET_HEADS), ret_inv, RET_QK_DIM) * (RET_QK_DIM ** -0.5)
        ro = retention(rq, rk, split_heads(rv, RET_HEADS))
        ro = ro * lax.rsqrt(jnp.mean(ro * ro, axis=-1, keepdims=True) + NORM_EPS)
        ro = jax.nn.silu(rg) * merge_heads(ro).astype(x.dtype)
        mix = jnp.concatenate([mo, ro], axis=-1) @ w_out[l]
        x = x + rms_norm(mix, g_post_mix[l])
        h = rms_norm(x, g_pre_cross[l])
        mem_n = rms_norm(mem, g_mem[l])
        c = cross_attention(h, mem_n, w_cq[l], w_ckv[l], w_co[l])
        x = x + rms_norm(c, g_post_cross[l])
        h = rms_norm(x, g_pre_ffn[l])
        gu = h @ w_gate_up[l]
        f = (jax.nn.silu(gu[..., :D_FF]) * gu[..., D_FF:]) @ w_down[l]
        x = x + rms_norm(f, g_post_ffn[l])
    return x
```

```python
import math
import os
from collections import deque
from contextlib import ExitStack

import numpy as np
import ml_dtypes

import concourse.bass as bass
import concourse.mybir as mybir
from concourse.bass_utils import run_bass_kernel_spmd

F32 = mybir.dt.float32
BF16 = mybir.dt.bfloat16
AF = mybir.ActivationFunctionType
ALU = mybir.AluOpType
AX = mybir.AxisListType
NPBF = ml_dtypes.bfloat16

S = 2048
D = 1024
NT = S // 128
NMEM = 256
DFF = 2816
NFC = DFF // 128
EPS = 1e-6
BIGK = 30000.0
ARENA_BYTES = 210400

ENGS = ['pe', 'act', 'dve', 'pool', 'sp']
NDS = 8


class Ins:
    __slots__ = ('eng', 'fn', 'deps', 'marked', 'sem', 'semval', 'isdma', 'idx')


class Prog:
    def __init__(self):
        self.streams = {e: [] for e in ENGS}
        self.lastw = {}
        self.readers = {}
        self.ndma = {e: 0 for e in ENGS}
        self.dma_hist = {e: [] for e in ENGS}
        self.bank_last = {}

    def add(self, eng, fn, reads=(), writes=(), dma=False):
        ins = Ins()
        ins.eng = eng; ins.fn = fn; ins.isdma = dma; ins.marked = False
        ins.sem = None; ins.semval = None
        deps = []
        for k in reads:
            w = self.lastw.get(k)
            if w is not None:
                deps.append(w)
        for k in writes:
            w = self.lastw.get(k)
            if w is not None:
                deps.append(w)
            deps.extend(self.readers.get(k, ()))
        if dma:
            i = self.ndma[eng]
            ins.sem = ('dma', eng, i % NDS)
            ins.semval = 16 * (i // NDS + 1)
            if i >= NDS:
                deps.append(self.dma_hist[eng][i - NDS])
            self.ndma[eng] += 1
            self.dma_hist[eng].append(ins)
        for k in list(reads) + list(writes):
            if isinstance(k, tuple) and k and k[0] == 'B':
                la = self.bank_last.setdefault(k, {})
                for e2, d in la.items():
                    if e2 != eng:
                        deps.append(d)
        seen = set(); best = {}; dd = []
        for d in deps:
            if id(d) in seen or d is ins:
                continue
            seen.add(id(d))
            if d.isdma:
                dd.append(d)
                continue
            if (not dma) and d.eng == 'pe' and eng == 'pe':
                continue
            b = best.get(d.eng)
            if b is None or d.idx > b.idx:
                best[d.eng] = d
        dd.extend(best.values())
        for d in dd:
            d.marked = True
        ins.deps = dd
        for k in reads:
            self.readers.setdefault(k, []).append(ins)
        for k in writes:
            self.lastw[k] = ins
            self.readers[k] = []
        ins.idx = len(self.streams[eng])
        self.streams[eng].append(ins)
        for k in list(reads) + list(writes):
            if isinstance(k, tuple) and k and k[0] == 'B':
                self.bank_last.setdefault(k, {})[eng] = ins
        return ins

    def barrier(self):
        lasts = []
        for e in ENGS:
            for ins in reversed(self.streams[e]):
                if not ins.isdma:
                    lasts.append(ins)
                    break
        pend = []
        for e in ENGS:
            pend.extend(self.dma_hist[e][-NDS:])
        for e in ENGS:
            n = self.add(e, lambda eng: eng.nop())
            for d in lasts + pend:
                if d is n or d in n.deps:
                    continue
                if d.eng == 'pe' and e == 'pe' and not d.isdma:
                    continue
                n.deps.append(d)
                d.marked = True
        self.lastw = {}
        self.readers = {}
        self.bank_last = {}

    def finalize(self):
        for e in ENGS:
            c = 0
            for ins in self.streams[e]:
                if ins.isdma:
                    continue
                if ins.marked:
                    c += 1
                    ins.sem = ('eng', e); ins.semval = c
            assert c < 60000, (e, c)

    def sem_keys(self):
        ks = [('eng', e) for e in ENGS]
        for e in ENGS:
            for j in range(min(NDS, self.ndma[e])):
                ks.append(('dma', e, j))
        return ks

    def emit(self, nc, sems):
        engmap = {'pe': 'tensor', 'act': 'scalar', 'dve': 'vector', 'pool': 'gpsimd', 'sp': 'sync'}
        with nc.Block() as block:
            for e in ENGS:
                stream = self.streams[e]

                def body(eng, stream=stream):
                    known = {}
                    for ins in stream:
                        need = {}
                        for d in ins.deps:
                            if d.semval > need.get(d.sem, 0):
                                need[d.sem] = d.semval
                        for sk, sv in need.items():
                            if known.get(sk, 0) >= sv:
                                continue
                            eng.wait_ge(sems[sk], sv)
                            known[sk] = sv
                        bi = ins.fn(eng)
                        if ins.isdma:
                            bi.then_inc(sems[ins.sem], 16)
                        elif ins.marked:
                            bi.then_inc(sems[ins.sem], 1)
                getattr(block, engmap[e])(body)


def host_constants():
    c = {}
    c['ident'] = np.eye(128, dtype=np.float32).astype(NPBF)
    kk = np.arange(128)[:, None]; qq = np.arange(128)[None, :]
    c['maskbias'] = np.where(kk <= qq, 0.0, -BIGK).astype(np.float32).astype(NPBF)
    kind = np.zeros((8, S), np.float32)
    for n in range(8):
        kind[n, n * 256:(n + 1) * 256] = BIGK
    c['kind'] = kind.astype(NPBF)
    pos = (np.arange(NT)[None, :] * 128 + np.arange(128)[:, None]).astype(np.float64)
    moba_inv = np.power(np.float64(500000.0), -np.arange(8, dtype=np.float64) * 2.0 / 16.0)
    ang = pos[:, :, None] * moba_inv[None, None, :]
    cm = np.cos(ang); sm = np.sin(ang)
    c['csM'] = np.stack([np.concatenate([cm, cm], -1), np.concatenate([sm, sm], -1)], axis=2).astype(np.float32)\
        .reshape(128, NT * 32)
    ret_inv = 1.0 / np.power(np.float64(10000.0), np.linspace(0.0, 1.0, 32, dtype=np.float64))
    angr = pos[:, :, None] * ret_inv[None, None, :]
    cr = np.cos(angr); sr = np.sin(angr)
    ksc = 64.0 ** -0.5
    cr2 = np.concatenate([cr, cr], -1); sr2 = np.concatenate([sr, sr], -1)
    c['csR'] = np.stack([np.stack([cr2, sr2], axis=2), np.stack([cr2 * ksc, sr2 * ksc], axis=2)], axis=2)\
        .astype(np.float32).reshape(128, NT * 256)
    hh = np.arange(4, dtype=np.float64)
    log_g = np.log(1.0 - np.power(2.0, -5.0 - hh))
    idx = np.arange(128, dtype=np.float64)
    qdec = np.exp(log_g[None, :] * (idx[:, None] + 1.0))
    kdec = np.exp(log_g[None, :] * (127.0 - idx[:, None]))
    c['dec'] = np.concatenate([qdec, kdec], axis=1).astype(np.float32)
    diff = idx[None, :] - idx[:, None]
    mt = np.where(diff[:, None, :] >= 0, np.exp(log_g[None, :, None] * np.maximum(diff[:, None, :], 0.0)), 0.0)
    c['maskT'] = mt.astype(np.float32).reshape(128, 4 * 128)
    c['cd'] = [float(np.exp(log_g[h] * 128.0)) for h in range(4)]
    return c


_CONST = host_constants()


def build_program(level=5, ngroups=8, dumps=()):
    nc = bass.Bass("TRN2", target_bir_lowering=False)
    P = Prog()
    REG = {}

    def din(name, shape, dt=F32):
        return nc.dram_tensor(name, list(shape), dt, kind="ExternalInput").ap()

    x_d = din("x", [S, D]); mem_d = din("mem", [NMEM, D])
    g_d = {n: din(n, [1, D]) for n in ["g_pre_mix", "g_post_mix", "g_pre_cross", "g_mem", "g_post_cross",
                                       "g_pre_ffn", "g_post_ffn"]}
    w_in = din("w_in", [D, 3072]); w_out = din("w_out", [D, D]); w_cq = din("w_cq", [D, D])
    w_ckv = din("w_ckv", [D, 2 * D]); w_co = din("w_co", [D, D])
    w_gu = din("w_gate_up", [D, 2 * DFF]); w_down = din("w_down", [DFF, D])
    c_ident = din("c_ident", [128, 128], BF16); c_maskbias = din("c_maskbias", [128, 128], BF16)
    c_kind = din("c_kind", [8, S], BF16)
    c_csM = din("c_csM", [128, NT * 32]); c_csR = din("c_csR", [128, NT * 256])
    c_dec = din("c_dec", [128, 8]); c_maskT = din("c_maskT", [128, 512])
    out_d = nc.dram_tensor("out", [S, D], F32, kind="ExternalOutput").ap()
    x2s = nc.dram_tensor("x2s", [S, D], F32, kind="Internal").ap()
    wb_in = nc.dram_tensor("wb_in", [D, 3072], BF16, kind="Internal").ap()
    wb_out = nc.dram_tensor("wb_out", [D, D], BF16, kind="Internal").ap()
    wb_cq = nc.dram_tensor("wb_cq", [D, D], BF16, kind="Internal").ap()
    wb_co = nc.dram_tensor("wb_co", [D, D], BF16, kind="Internal").ap()
    wb_ckv = nc.dram_tensor("wb_ckv", [D, 2 * D], BF16, kind="Internal").ap()
    wb_gu = nc.dram_tensor("wb_gu", [D, 2 * DFF], BF16, kind="Internal").ap()
    wb_dn = nc.dram_tensor("wb_dn", [DFF, D], BF16, kind="Internal").ap()
    CD = _CONST['cd']

    es = ExitStack()
    arena = es.enter_context(nc.sbuf_tensor("arena", [128, ARENA_BYTES // 2], BF16))
    stats = es.enter_context(nc.sbuf_tensor("stats", [128, 512], F32))
    bpairs = [es.enter_context(nc.psum_tensor(f"bpair{i}", [128, 1024], F32)) for i in range(4)]

    def V(off, nbytes, dt=BF16):
        assert off % 4 == 0 and off + nbytes <= ARENA_BYTES, (off, nbytes)
        a = arena[:, off // 2:(off + nbytes) // 2]
        if dt == F32:
            a = a.bitcast(F32)
        return a

    def Bk(i):
        return bpairs[i // 2][:, (i % 2) * 512:(i % 2 + 1) * 512]

    def Bb(i):
        return bpairs[i // 2][:, (i % 2) * 512:(i % 2 + 1) * 512].bitcast(BF16)

    def Bk2(p):
        return bpairs[p][:]

    BK = lambda i: ('B', i)

    def dma(out, in_, reads, writes):
        P.add('sp', lambda e: e.dma_start(out=out, in_=in_), reads, writes, dma=True)

    def mm(out, lhsT, rhs, start, stop, reads, writes):
        P.add('pe', lambda e: e.matmul(out, lhsT=lhsT, rhs=rhs, start=start, stop=stop), reads, writes)

    def act(out, in_, func, reads, writes, scale=1.0, accum_out=None):
        if accum_out is None:
            P.add('act', lambda e: e.activation(out=out, in_=in_, func=func, scale=scale), reads, writes)
        else:
            P.add('act', lambda e: e.activation(out=out, in_=in_, func=func, scale=scale, accum_out=accum_out),
                  reads, writes)

    def acopy(out, in_, reads, writes):
        P.add('act', lambda e: e.copy(out=out, in_=in_), reads, writes)

    def tt(eng, out, in0, in1, op, reads, writes):
        P.add(eng, lambda e: e.tensor_tensor(out=out, in0=in0, in1=in1, op=op), reads, writes)

    def ts(eng, out, in0, s1, s2, op0, op1, reads, writes):
        if op1 is None:
            P.add(eng, lambda e: e.tensor_scalar(out=out, in0=in0, scalar1=s1, scalar2=None, op0=op0), reads, writes)
        else:
            P.add(eng, lambda e: e.tensor_scalar(out=out, in0=in0, scalar1=s1, scalar2=s2, op0=op0, op1=op1),
                  reads, writes)

    def stt(out, in0, scalar, in1, op0, op1, reads, writes):
        P.add('dve', lambda e: e.scalar_tensor_tensor(out=out, in0=in0, scalar=scalar, in1=in1, op0=op0, op1=op1),
              reads, writes)

    def tcopy(eng, out, in_, reads, writes):
        P.add(eng, lambda e: e.tensor_copy(out=out, in_=in_), reads, writes)

    def memset(eng, ap, val, writes):
        P.add(eng, lambda e: e.memset(ap, val), (), writes)

    def finish():
        outk = []
        for name in dumps:
            ap, dt, shape = REG[name]
            d = nc.dram_tensor("dbg_" + name, list(shape), dt, kind="ExternalOutput").ap()
            dma(d, ap, [], [('dbg', name)])
            outk.append(('dbg', name))
        P.add('sp', lambda e: e.nop(), reads=[('out', T) for T in range(NT)] + outk)
        P.finalize()
        sems = {kk: es.enter_context(nc.semaphore("s_" + "_".join(map(str, kk)))) for kk in P.sem_keys()}
        P.emit(nc, sems)
        es.close()
        return nc

    stat_ctr = [0]

    def newstat():
        i = stat_ctr[0] % 508
        stat_ctr[0] += 1
        return stats[:, i:i + 1], ('st', stat_ctr[0])

    memset('pool', stats[:, 508:509], -0.5, ['mhalf'])
    MHALF = stats[:, 508:509]

    def rstd_from_ss(ss_ap, ss_key, mult, add):
        v, vk = newstat()
        ts('dve', v, ss_ap, mult, add, ALU.mult, ALU.add, [ss_key], [vk])
        r, rk = newstat()
        tt('pool', r, v, MHALF, ALU.pow, [vk, 'mhalf'], [rk])
        return r, rk

    O_MIX = 0
    O_HT = 32768
    O_WCO = 65536
    O_MEMT = 81920
    O_KCT = O_MEMT + 4096
    O_VC = O_KCT + 4096
    O_IDENT = 94336
    O_GAIN = 94592
    O_WG = 106880
    O_STAGE = 123264
    O_MOBA = 139648
    O_RET = 160256
    O_MC = 182784
    O_TMP = 194304

    mixtok = V(O_MIX, 32768).rearrange("p (t f) -> p t f", t=NT)
    hT = V(O_HT, 32768).rearrange("p (c t) -> p c t", c=8)
    wout_sb = V(O_HT, 16384).rearrange("p (c n) -> p c n", c=8)
    wcq_sb = V(O_HT + 16384, 16384).rearrange("p (c n) -> p c n", c=8)
    wco_sb = V(O_WCO, 16384).rearrange("p (c n) -> p c n", c=8)
    memT = V(O_MEMT, 4096).rearrange("p (c t) -> p c t", c=8)
    kcT = V(O_KCT, 4096).rearrange("p (c t) -> p c t", c=8)
    vc = V(O_VC, 4128).rearrange("p (m h d) -> p m h d", m=2, h=4)
    ident = V(O_IDENT, 256)
    gain = [V(O_GAIN + i * 4096, 4096, F32) for i in range(3)]
    wg = [V(O_WG + i * 8192, 8192).rearrange("p (c n) -> p c n", c=8) for i in range(2)]
    wg += [V(O_WCO + i * 8192, 8192).rearrange("p (c n) -> p c n", c=8) for i in range(2)]
    wslot = lambda g: (g % 2) if g < 4 else 2 + ((g - 4) % 2)
    stage = [V(O_STAGE + i * 8192, 8192, F32) for i in range(2)]

    stage_ctr = [0]
    cast_ctr = [0]
    CAST_SEQ = [['pool']]

    def load_piece(dst, srcs, shape, wkeys, extra_reads=()):
        s = stage_ctr[0] % 2
        stage_ctr[0] += 1
        n = int(np.prod(shape[1:]))
        assert n <= 2048
        for (sub, src_ap, seg) in srcs:
            dma(sub(stage[s]), src_ap, [], [('stage', s, seg)])
        stv = stage[s][:, 0:n]
        if len(shape) == 3:
            stv = stv.rearrange("p (a b) -> p a b", a=shape[1])
        elif len(shape) == 4:
            stv = stv.rearrange("p (a b c) -> p a b c", a=shape[1], b=shape[2])
        ce = CAST_SEQ[0][cast_ctr[0] % len(CAST_SEQ[0])]; cast_ctr[0] += 1
        if ce == 'act':
            acopy(dst, stv, [('stage', s, k) for k in range(4)] + list(extra_reads), wkeys)
        else:
            tcopy(ce, dst, stv, [('stage', s, k) for k in range(4)] + list(extra_reads), wkeys)

    def load_plain(dst3, w_ap, r0, nr, c0, ncol, wkeys, extra_reads=()):
        src = w_ap.rearrange("(c p) n -> p c n", p=128)[:, r0:r0 + nr, c0:c0 + ncol]
        load_piece(dst3, [(lambda st: st[:, 0:nr * ncol].rearrange("p (a b) -> p a b", a=nr), src, 0)],
                   [128, nr, ncol], wkeys, extra_reads)

    conv_jobs = []
    for (nm, src, dstb, ncols) in (("ckv", w_ckv, wb_ckv, 2 * D), ("co", w_co, wb_co, D), ("out", w_out, wb_out, D),
                                   ("cq", w_cq, wb_cq, D), ("gu", w_gu, wb_gu, 2 * DFF)):
        for c0 in range(0, ncols, 512):
            conv_jobs.append((nm, c0 // 512, dstb[:, c0:c0 + 512], src[:, c0:c0 + 512]))
    for r0 in range(0, DFF, 512):
        r1 = min(r0 + 512, DFF)
        conv_jobs.append(("dn", r0 // 512, wb_dn[r0:r1, :], w_down[r0:r1, :]))
    in_jobs = [("in", c0 // 512, wb_in[:, c0:c0 + 512], w_in[:, c0:c0 + 512]) for c0 in range(0, 3072, 512)]
    conv_jobs = in_jobs + conv_jobs
    conv_pos = [0]

    def emit_conv(n=1):
        for _ in range(n):
            if conv_pos[0] >= len(conv_jobs):
                return
            nm, i, o_ap, i_ap = conv_jobs[conv_pos[0]]
            conv_pos[0] += 1
            P.add('pool', lambda e, o_ap=o_ap, i_ap=i_ap: e.dma_start(out=o_ap, in_=i_ap), [], [('cv', nm, i)], dma=True)

    def load_b16(dst3, wb_ap, r0, nr, c0, ncol, wkeys, cvkeys):
        src = wb_ap.rearrange("(c p) n -> p c n", p=128)[:, r0:r0 + nr, c0:c0 + ncol]
        dma(dst3, src, cvkeys, wkeys)

    w_in4 = w_in.rearrange("(c p) (s n) -> p c s n", p=128, n=512)
    w_in3 = w_in.rearrange("(c p) n -> p c n", p=128)

    wb_in3 = wb_in.rearrange("(c p) n -> p c n", p=128)

    def load_group_weights(g):
        slot = wslot(g)
        if g < 4:
            j = g
            segs = [(128 * j, 128, 0), (512 + 128 * j, 128, 128), (1024 + 128 * j, 128, 256)]
        else:
            r = g - 4
            segs = [(1536 + 64 * r, 64, 0), (1792 + 64 * r, 64, 64), (2048 + 128 * r, 128, 128),
                    (2560 + 128 * r, 128, 256)]
        for si, (c0, wd, o0) in enumerate(segs):
            dma(wg[slot][:, :, o0:o0 + wd], wb_in3[:, :, c0:c0 + wd], [('cv', 'in', c0 // 512)], [('wg', slot, si)])

    def load_group_weights_direct(g):
        slot = wslot(g)
        if g < 4:
            segs = [(128 * g, 128, 0), (512 + 128 * g, 128, 128), (1024 + 128 * g, 128, 256)]
        else:
            r = g - 4
            segs = [(1536 + 64 * r, 64, 0), (1792 + 64 * r, 64, 64), (2048 + 128 * r, 128, 128),
                    (2560 + 128 * r, 128, 256)]
        for si, (c0, wd, o0) in enumerate(segs):
            P.add('pool', lambda e, o0=o0, wd=wd, c0=c0: e.dma_start(out=wg[slot][:, :, o0:o0 + wd],
                                                                     in_=w_in3[:, :, c0:c0 + wd]),
                  [], [('wg', slot, si)], dma=True)

    def wgk(slot):
        return [('wg', slot, si) for si in range(4)]

    csM2 = V(O_MC, 2048, F32)
    csM = csM2.rearrange("p (t c d) -> p t c d", t=NT, c=2)
    maskT = V(O_MC + 2048, 2048, F32).rearrange("p (h i) -> p h i", h=4)
    maskbias = V(O_MC + 4096, 256)
    csR2 = V(O_STAGE, 16384, F32)
    csR = csR2.rearrange("p (t a c d) -> p t a c d", t=NT, a=2, c=2)
    dec = stats[:, 500:508]
    dma(ident, c_ident, [], ['ident'])
    dma(csM2, c_csM, [], ['csM'])
    dma(gain[0], g_d["g_pre_mix"].partition_broadcast(128), [], [('gain', 0)])
    dma(gain[1], g_d["g_mem"].partition_broadcast(128), [], [('gain', 1)])

    NXB = 4
    xb = [V(O_RET + i * 4096, 4096, F32) for i in range(NXB)]
    xn = [V(O_RET + 16384 + i * 2048, 2048) for i in range(2)]
    junk = V(O_RET + 20480, 2048)
    A0_KEYS = [('xb', i) for i in range(NXB)] + [('xn', 0), ('xn', 1), 'junk']

    a0_ctr = [0]

    def a0_tile(src_ap, gidx, dstT, dkeys):
        i = a0_ctr[0]; a0_ctr[0] += 1
        s = i % 2
        sx = i % NXB
        dma(xb[sx], src_ap, [], [('xb', sx)])
        ss, ssk = newstat()
        act(junk, xb[sx], AF.Square, [('xb', sx)], [ssk, 'junk'], accum_out=ss)
        r, rk = rstd_from_ss(ss, ssk, 1.0 / D, EPS)
        stt(xn[s], xb[sx], r, gain[gidx], ALU.mult, ALU.mult, [('xb', sx), rk, ('gain', gidx)], [('xn', s)])
        def tail(i=i, s=s, dstT=dstT, dkeys=dkeys):
            b = 2 + (i % 2)
            pT = Bb(b)
            for c in range(8):
                P.add('pe', lambda e, c=c, pT=pT, s=s: e.transpose(out=pT[:, c * 128:(c + 1) * 128],
                                                                   in_=xn[s][:, c * 128:(c + 1) * 128], identity=ident),
                      [('xn', s), 'ident'], [BK(b)])
            acopy(dstT, pT.rearrange("p (c t) -> p c t", c=8), [BK(b)], dkeys)
        a0_pend.append(tail)
        while len(a0_pend) > 1:
            a0_pend.pop(0)()

    a0_pend = []

    load_group_weights_direct(0)
    load_group_weights_direct(4)
    for m in range(int(os.environ.get('DBG_NMEM', '2'))):
        a0_tile(mem_d[m * 128:(m + 1) * 128, :], 1, memT[:, :, m * 128:(m + 1) * 128], [('memT', m)])
    dma(csR2, c_csR, [], ['csR'])
    dma(maskT, c_maskT.rearrange("p (h i) -> p h i", h=4), [], ['maskT'])
    dma(maskbias, c_maskbias, [], ['maskbias'])
    dma(dec, c_dec, [], ['dec'])
    for T in range(int(os.environ.get('DBG_NX', str(NT)))):
        a0_tile(x_d[T * 128:(T + 1) * 128, :], 0, hT[:, :, T * 128:(T + 1) * 128], [('hT', T)])
    while a0_pend:
        a0_pend.pop(0)()

    REG['hT'] = (V(O_HT, 32768), BF16, [128, 16384]); REG['memT'] = (V(O_MEMT, 4096), BF16, [128, 2048])
    if level <= 1:
        P.barrier()
        return finish()

    qTa = V(O_MOBA, 8192).rearrange("p (h t) -> p h t", h=2)
    kTa = V(O_MOBA + 8192, 8192).rearrange("p (h t) -> p h t", h=2)
    va = V(O_MOBA + 16384, 4224).rearrange("p (t h d) -> p t h d", t=NT, h=2)
    rT = V(O_RET, 12288).rearrange("p (a t) -> p a t", a=3)
    rv = V(O_RET + 12288, 4096).rearrange("p (t d) -> p t d", t=NT)
    kd = V(O_RET + 16384, 2048).rearrange("p (t d) -> p t d", t=NT)
    sg = V(O_RET + 18432, 4096).rearrange("p (t d) -> p t d", t=NT)

    to = [O_TMP]

    def talloc(nbytes, dt=BF16):
        o = to[0]
        to[0] += (nbytes + 63) // 64 * 64
        return V(o, nbytes, dt)

    PTb = [talloc(1024) for _ in range(4)]
    qk_tok = [talloc(576).rearrange("p (h d) -> p h d", h=4) for _ in range(4)]
    tallb = [talloc(512, F32).rearrange("p (h c d) -> p h c d", h=4, c=2) for _ in range(2)]
    g8b = [talloc(64, F32).rearrange("p (h d) -> p h d", h=2) for _ in range(2)]
    m8b = [talloc(64, F32).rearrange("p (h d) -> p h d", h=2) for _ in range(2)]
    kmT = talloc(32).rearrange("p (h d) -> p h d", h=2)
    kms = talloc(8, F32)
    recb = [talloc(4, F32) for _ in range(4)]
    tallr = [talloc(1024, F32).rearrange("p (a c d) -> p a c d", a=2, c=2) for _ in range(2)]
    qkr = [talloc(512, F32).rearrange("p (a d) -> p a d", a=2) for _ in range(2)]
    rtok = [talloc(384).rearrange("p (a d) -> p a d", a=3) for _ in range(2)]
    attnb = [talloc(256) for _ in range(2)]
    thb = [talloc(512, F32) for _ in range(2)]
    state_f = talloc(512, F32)
    state_b = [talloc(256) for _ in range(2)]
    assert to[0] <= ARENA_BYTES, to[0]

    memset('pool', qTa[64:128, :, :], 0.0, [('qTb', T) for T in range(NT)])
    memset('pool', kTa[64:128, :, :], 0.0, [('kind', 0), ('kind', 1)])
    for h in range(2):
        dma(kTa[64:72, h, :], c_kind, [], [('kind', h)])
    memset('pool', va[:, :, :, 64:65], 1.0, [('va1',)])
    memset('pool', kmT[0:64, :, :], 0.0, ['kmT'])
    memset('pool', stats[:, 509:510], 0.0, A0_KEYS + ['ret_ok'])
    memset('pool', vc[:, :, :, 256:257], 1.0, [('vc1',)])

    sc_ctr = [0]
    o_ctr = [0]
    pt_ctr = [0]
    rec_ctr = [0]
    pvq = deque()
    PV_LAG = 2
    junk2 = [talloc(256), talloc(128)]
    j2c = [0]

    def proj(u, g, T):
        b = u % 2
        slot = wslot(g)
        for c in range(8):
            mm(Bk(b)[:, 0:384], hT[:, c, T * 128:(T + 1) * 128], wg[slot][:, c, 0:384], c == 0, c == 7,
               [('hT', T)] + wgk(slot), [BK(b)])

    def flush_pv(n_keep):
        while len(pvq) > n_keep:
            pvq.popleft()()

    def m1(u, j, T):
        if T == 0:
            flush_pv(0)
        b = u % 2
        k = T % 2
        pb = Bk(b); pk = BK(b)
        slab = pb[:, 0:256].rearrange("p (h d) -> p h d", h=4)
        qt = qk_tok[T % 4]; qk = T % 4
        acopy(va[:, T, :, 0:64], pb[:, 256:384].rearrange("p (h d) -> p h d", h=2), [pk, ('va1',)], [('va', T)])
        tcopy('dve', qt[:, :, 16:64], slab[:, :, 16:64], [pk], [('qkt', qk, 2)])
        in0 = slab[:, :, 0:16].unsqueeze(2).broadcast_to([128, 4, 2, 16])
        in1 = csM[:, T, :, :].unsqueeze(1).broadcast_to([128, 4, 2, 16])
        ta = tallb[k]
        tt('dve', ta, in0, in1, ALU.mult, [pk, 'csM'], [('tall', k)])
        tt('pool', qt[:, :, 0:8], ta[:, :, 0, 0:8], ta[:, :, 1, 8:16], ALU.subtract, [('tall', k)], [('qkt', qk, 0)])
        tt('pool', qt[:, :, 8:16], ta[:, :, 0, 8:16], ta[:, :, 1, 0:8], ALU.add, [('tall', k)], [('qkt', qk, 1)])

    def m2(u, j, T):
        qt = qk_tok[T % 4]; qk = T % 4
        trb = Bb(2)
        for i in range(4):
            P.add('pe', lambda e, i=i, qt=qt, trb=trb: e.transpose(out=trb[0:64, i * 128:(i + 1) * 128],
                                                                   in_=qt[:, i, 0:64], identity=ident),
                  [('qkt', qk, 0), ('qkt', qk, 1), ('qkt', qk, 2), 'ident'], [BK(2)])
        tcopy('dve', qTa[0:64, :, T * 128:(T + 1) * 128], trb[0:64, 0:256].rearrange("p (h t) -> p h t", h=2),
              [BK(2)], [('qTa', T)])
        tcopy('dve', kTa[0:64, :, T * 128:(T + 1) * 128], trb[0:64, 256:512].rearrange("p (h t) -> p h t", h=2),
              [BK(2)], [('kTa', T)])
        if T % 2 == 1 and T // 2 <= 6:
            bidx = T // 2
            P.add('dve', lambda e, bidx=bidx: e.tensor_reduce(out=kms[0:64, :], in_=kTa[0:64, :, bidx * 256:(bidx + 1) * 256],
                                                              axis=AX.X, op=ALU.add),
                  [('kTa', T - 1), ('kTa', T)], ['kms'])
            ts('dve', kmT[0:64, :, bidx], kms[0:64, :], 1.0 / 256.0, None, ALU.mult, None, ['kms'], ['kmT'])

    def m3(u, j, T):
        if T < 8:
            return
        k = T % 2
        qt = qk_tok[T % 4]; qk = T % 4
        cur = T // 2
        for h in range(2):
            mm(Bk(3)[:, h * 8:(h + 1) * 8], qTa[0:64, h, T * 128:(T + 1) * 128], kmT[0:64, h, :], True, True,
               [('qTa', T), 'kmT'], [BK(3)])
        memset('pool', g8b[k], -1e30, [('g8', k)])
        memset('pool', g8b[k][:, :, cur:cur + 1], 1e30, [('g8', k)])
        tcopy('dve', g8b[k][:, :, 0:cur], Bk(3)[:, 0:16].rearrange("p (h d) -> p h d", h=2)[:, :, 0:cur],
              [BK(3)], [('g8', k)])
        for h in range(2):
            P.add('dve', lambda e, h=h: e.max(out=m8b[k][:, h, :], in_=g8b[k][:, h, :]), [('g8', k)], [('m8', k, h)])
            ts('dve', qt[:, h, 64:72], g8b[k][:, h, :], m8b[k][:, h, 3:4], 1.0, ALU.is_ge, ALU.subtract,
               [('g8', k), ('m8', k, h)], [('qkt', qk, 3 + h)])

    def m4(u, j, T):
        if T < 8:
            return
        qt = qk_tok[T % 4]; qk = T % 4
        b3 = Bb(3)
        for h in range(2):
            P.add('pe', lambda e, h=h, qt=qt, b3=b3: e.transpose(out=b3[0:72, 512 + h * 128:512 + (h + 1) * 128],
                                                                 in_=qt[:, h, 0:72], identity=ident),
                  [('qkt', qk, 0), ('qkt', qk, 1), ('qkt', qk, 2), ('qkt', qk, 3 + h), 'ident'], [BK(3)])
        acopy(qTa[64:72, :, T * 128:(T + 1) * 128], b3[64:72, 512:768].rearrange("p (h t) -> p h t", h=2),
              [BK(3)], [('qTb', T)])

    def m5(u, j, T):
        for h in range(2):
            ob = 6 + (o_ctr[0] % 2); o_ctr[0] += 1
            chunks = list(range(T + 1))
            groups = [chunks[i:i + 4] for i in range(0, len(chunks), 4)]
            for gi, grp in enumerate(groups):
                sb_ = 4 + (sc_ctr[0] % 2); sc_ctr[0] += 1
                pi = pt_ctr[0] % 4; pt_ctr[0] += 1
                Sb = Bk(sb_)
                for s, kc in enumerate(grp):
                    rd = [('kTa', kc), ('qTa', T), ('qTb', T), ('kind', h)]
                    mm(Sb[:, s * 128:(s + 1) * 128], kTa[:, h, kc * 128:(kc + 1) * 128],
                       qTa[:, h, T * 128:(T + 1) * 128], True, kc != T, rd, [BK(sb_)])
                    if kc == T:
                        mm(Sb[:, s * 128:(s + 1) * 128], ident, maskbias, False, True, ['ident', 'maskbias'], [BK(sb_)])
                n = len(grp)
                act(PTb[pi][:, 0:n * 128], Sb[:, 0:n * 128], AF.Exp, [BK(sb_)], [('PT', pi)], scale=0.125)

                def pv(grp=grp, pi=pi, ob=ob, h=h, T=T, last=(gi == len(groups) - 1), j=j):
                    Ob = Bk(ob)
                    for s, kc in enumerate(grp):
                        mm(Ob[:, 0:65], PTb[pi][:, s * 128:(s + 1) * 128], va[:, kc, h, 0:65], kc == 0, kc == T,
                           [('PT', pi), ('va', kc), ('va1',)], [BK(ob)])
                    if last:
                        ri = rec_ctr[0] % 4; rec_ctr[0] += 1
                        P.add('dve', lambda e: e.reciprocal(out=recb[ri][:, 0:1], in_=Ob[:, 64:65]), [BK(ob)], [('rec', ri)])
                        hs = 2 * j + h
                        ts('dve', mixtok[:, T, hs * 64:(hs + 1) * 64], Ob[:, 0:64], recb[ri][:, 0:1], None, ALU.mult, None,
                           [BK(ob), ('rec', ri)], [('mix', T, hs)])
                pvq.append(pv)
                flush_pv(PV_LAG)

    def r1(u, r, T):
        b = u % 2
        k = T % 2
        pb = Bk(b); pk = BK(b)
        acopy(rv[:, T, :], pb[:, 128:256], [pk, 'ret_ok'], [('rv', T)])
        act(thb[k], pb[:, 256:384], AF.Tanh, [pk], [('th', k)], scale=0.5)
        stt(sg[:, T, :], thb[k], 1.0, pb[:, 256:384], ALU.add, ALU.mult, [('th', k), pk, 'ret_ok'], [('sg', T)])
        slab2 = pb[:, 0:128].rearrange("p (a d) -> p a d", a=2)
        tr_ = tallr[k]
        tt('dve', tr_, slab2.unsqueeze(2).broadcast_to([128, 2, 2, 64]), csR[:, T, :, :, :], ALU.mult, [pk, 'csR'],
           [('tallr', k)])
        tt('pool', qkr[k][:, :, 0:32], tr_[:, :, 0, 0:32], tr_[:, :, 1, 32:64], ALU.subtract, [('tallr', k)], [('qkr', k, 0)])
        tt('pool', qkr[k][:, :, 32:64], tr_[:, :, 0, 32:64], tr_[:, :, 1, 0:32], ALU.add, [('tallr', k)], [('qkr', k, 1)])
        qk_ = [('qkr', k, 0), ('qkr', k, 1)]
        tcopy('pool', rtok[k][:, 0:2, :], qkr[k], qk_, [('rtok', k, 0)])
        ts('pool', rtok[k][:, 2, :], qkr[k][:, 0, :], dec[:, r:r + 1], 1.0, ALU.mult, ALU.mult, qk_ + ['dec'], [('rtok', k, 1)])
        ts('pool', kd[:, T, :], qkr[k][:, 1, :], dec[:, 4 + r:5 + r], 1.0, ALU.mult, ALU.mult, qk_ + ['dec', 'ret_ok'], [('kd', T)])

    def r2(u, r, T):
        k = T % 2
        trb = Bb(2)
        for i in range(3):
            P.add('pe', lambda e, i=i, trb=trb: e.transpose(out=trb[0:64, i * 128:(i + 1) * 128], in_=rtok[k][:, i, :],
                                                            identity=ident),
                  [('rtok', k, 0), ('rtok', k, 1), 'ident'], [BK(2)])
        tcopy('dve', rT[0:64, :, T * 128:(T + 1) * 128], trb[0:64, 0:384].rearrange("p (a t) -> p a t", a=3), [BK(2), 'ret_ok'], [('rT', T)])

    def r3(u, r, T):
        k = T % 2
        sb_ = 4 + (sc_ctr[0] % 2); sc_ctr[0] += 1
        tsl = slice(T * 128, (T + 1) * 128)
        mm(Bk(sb_)[:, 0:128], rT[0:64, 1, tsl], rT[0:64, 0, tsl], True, True, [('rT', T)], [BK(sb_)])
        tt('dve', attnb[k], Bk(sb_)[:, 0:128], maskT[:, r, :], ALU.mult, [BK(sb_), 'maskT'], [('attn', k)])
        if T < NT - 1:
            mm(Bk(3)[0:64, 0:128], kd[:, T, :], rv[:, T, :], True, True, [('kd', T), ('rv', T)], [BK(3)])
            if T == 0:
                tcopy('dve', state_f[0:64, :], Bk(3)[0:64, 0:128], [BK(3)], ['state_f'])
            else:
                stt(state_f[0:64, :], state_f[0:64, :], CD[r], Bk(3)[0:64, 0:128], ALU.mult, ALU.add,
                    [BK(3), 'state_f'], ['state_f'])
            tcopy('pool', state_b[(T + 1) % 2][0:64, :], state_f[0:64, :], ['state_f'], [('state_b', (T + 1) % 2)])

    def r4(u, r, T):
        k = T % 2
        ob = 4 + (sc_ctr[0] % 2); sc_ctr[0] += 1
        tsl = slice(T * 128, (T + 1) * 128)
        mm(Bk(ob)[:, 0:128], attnb[k], rv[:, T, :], True, T == 0, [('attn', k), ('rv', T)], [BK(ob)])
        if T > 0:
            mm(Bk(ob)[:, 0:128], rT[0:64, 2, tsl], state_b[T % 2][0:64, :], False, True,
               [('rT', T), ('state_b', T % 2)], [BK(ob)])
        ss, ssk = newstat()
        act(junk2[0], Bk(ob)[:, 0:128], AF.Square, [BK(ob)], [ssk, 'junk2'], accum_out=ss)
        r4_state[(r, T)] = rstd_from_ss(ss, ssk, 4.0 / 128.0, 4.0 * EPS)
        tcopy('dve', outc[k], Bk(ob)[:, 0:128], [BK(ob)], [('outc', k)])

    def r5(u, r, T):
        k = T % 2
        rr, rk = r4_state[(r, T)]
        stt(mixtok[:, T, 512 + r * 128:512 + (r + 1) * 128], outc[k], rr, sg[:, T, :], ALU.mult, ALU.mult,
            [('outc', k), rk, ('sg', T)], [('mix', T, 8 + r)])

    r4_state = {}
    outc = [V(O_MC + 4352 + i * 512, 512, F32) for i in range(2)]

    MST = [None, m1, m2, m3, m4, m5]
    RST = [None, r1, r2, r3, r4, r5]

    PAD = 2
    mu = []
    for g in range(min(ngroups, 4)):
        mu += [(g, T) for T in range(NT)] + [None] * PAD
    ru = [(g, T) for g in range(4, ngroups) for T in range(NT)]
    units = []
    for i in range(max(len(mu), len(ru))):
        units.append(mu[i] if i < len(mu) else None)
        units.append(ru[i] if i < len(ru) else None)
    NU = len(units)
    DEPTH = 6
    HT_KEYS = [('hT', T) for T in range(NT)]

    def phaseC_weight_loads():
        emit_conv(len(conv_jobs))
        for hf in range(2):
            load_b16(wout_sb[:, :, hf * 512:(hf + 1) * 512], wb_out, 0, 8, hf * 512, 512, [('wout',)] + HT_KEYS,
                     [('cv', 'out', hf)])
        for hf in range(2):
            load_b16(wco_sb[:, :, hf * 512:(hf + 1) * 512], wb_co, 0, 8, hf * 512, 512, [('wco',)] + wgk(2) + wgk(3),
                     [('cv', 'co', hf)])
        for hf in range(2):
            load_b16(wcq_sb[:, :, hf * 512:(hf + 1) * 512], wb_cq, 0, 8, hf * 512, 512, [('wcq',)] + HT_KEYS,
                     [('cv', 'cq', hf)])
        for cg in range(2):
            load_b16(wg[cg][:, :, :], wb_ckv, 0, 8, cg * 512, 512, wgk(cg), [('cv', 'ckv', cg)])

    def run_stage(kst, p):
        if not (0 <= p < NU) or units[p] is None:
            return
        g, T = units[p]
        if g < 4:
            if kst < len(MST):
                MST[kst](p, g, T)
        else:
            if kst < len(RST):
                RST[kst](p, g - 4, T)

    LAST_PROJ = max(p for p in range(NU) if units[p] is not None)
    for step in range(NU + DEPTH):
        if step >= 2 and step % 2 == 0:
            emit_conv(1)
        if step == 16:
            load_group_weights(1)
            load_group_weights(5)
        run_stage(1, step - 1)
        if step < NU and units[step] is not None:
            g, T = units[step]
            if T == 0 and g in (1, 2):
                load_group_weights(g + 1)
            if T == 0 and g in (5, 6) and g + 1 < ngroups:
                load_group_weights(g + 1)
            proj(step, g, T)
        if step == LAST_PROJ + 1:
            phaseC_weight_loads()
        for kst in range(DEPTH - 1, 1, -1):
            run_stage(kst, step - kst)
    flush_pv(0)
    REG['mixtok'] = (V(O_MIX, 32768), BF16, [128, 16384])
    REG['qTa'] = (V(O_MOBA, 8192), BF16, [128, 4096]); REG['kTa'] = (V(O_MOBA + 8192, 8192), BF16, [128, 4096])
    REG['va'] = (V(O_MOBA + 16384, 4224), BF16, [128, 2112])
    if level <= 2:
        P.barrier()
        return finish()

    kvb = [0]
    for cg in range(4):
        slot = cg % 2
        if cg >= 2:
            load_b16(wg[slot][:, :, :], wb_ckv, 0, 8, cg * 512, 512, wgk(slot), [('cv', 'ckv', cg)])
        if cg < 2:
            for fl in range(4):
                fc = cg * 4 + fl
                b = kvb[0] % 2; kvb[0] += 1
                for c in range(8):
                    mm(Bk(b)[:, 0:256], wg[slot][:, c, fl * 128:(fl + 1) * 128], memT[:, c, :], c == 0, c == 7,
                       wgk(slot) + [('memT', 0), ('memT', 1)], [BK(b)])
                acopy(kcT[:, fc, :], Bk(b)[:, 0:256], [BK(b)], [('kcT',)])
        else:
            hv = cg - 2
            for m in range(2):
                b = kvb[0] % 2; kvb[0] += 1
                for c in range(8):
                    mm(Bk(b)[:, 0:512], memT[:, c, m * 128:(m + 1) * 128], wg[slot][:, c, :], c == 0, c == 7,
                       wgk(slot) + [('memT', m)], [BK(b)])
                acopy(vc[:, m, 2 * hv:2 * hv + 2, 0:256], Bk(b)[:, 0:512].rearrange("p (h d) -> p h d", h=2),
                      [BK(b), ('vc1',)], [('vc', m)])

    dma(gain[0], g_d["g_post_mix"].partition_broadcast(128), [], [('gain', 0)])
    dma(gain[1], g_d["g_pre_cross"].partition_broadcast(128), [], [('gain', 1)])
    dma(gain[2], g_d["g_post_cross"].partition_broadcast(128), [], [('gain', 2)])

    P.barrier()
    REG['kcT'] = (V(O_KCT, 4096), BF16, [128, 2048]); REG['vc'] = (V(O_VC, 4128), BF16, [128, 2064])
    if level <= 3:
        return finish()

    co = [O_MOBA]

    def calloc(nbytes, dt=BF16):
        o = co[0]
        co[0] += (nbytes + 63) // 64 * 64
        return V(o, nbytes, dt)

    h2T = V(O_WG, 8192).rearrange("p (c t) -> p c t", c=8)
    qcT = V(O_WG + 8192, 8192).rearrange("p (c t) -> p c t", c=8)
    x1 = calloc(16384, F32).rearrange("p (t f) -> p t f", t=4)
    xl = [calloc(4096, F32) for _ in range(4)]
    tmpf = [calloc(4096, F32) for _ in range(2)]
    xn2 = [calloc(2048) for _ in range(2)]
    junkcb = [calloc(2048), calloc(2048)]
    jcc = [0]

    def njunkc():
        i = jcc[0] % 2; jcc[0] += 1
        return junkcb[i], ('junkc', i)
    mixT = [calloc(2048).rearrange("p (c t) -> p c t", c=8) for _ in range(2)]
    PTc = [V(O_STAGE + 2048 + i * 1024, 1024) for i in range(8)]
    oc_tok = calloc(8192).rearrange("p (t f) -> p t f", t=4)
    ocT = [calloc(2048).rearrange("p (c t) -> p c t", c=8) for _ in range(2)]
    recc = [calloc(64, F32) for _ in range(4)]
    assert co[0] <= ARENA_BYTES, co[0]

    wb_ctr = [0]

    def workbank():
        b = 5 + (wb_ctr[0] % 3); wb_ctr[0] += 1
        return b

    def transposes8(src_fn, rkeys, dst, dkeys, eng='act', bank=4):
        pT = Bb(bank)
        for c in range(8):
            P.add('pe', lambda e, c=c: e.transpose(out=pT[:, c * 128:(c + 1) * 128], in_=src_fn(c), identity=ident),
                  list(rkeys) + ['ident'], [BK(bank)])
        if eng == 'act':
            acopy(dst, pT.rearrange("p (c t) -> p c t", c=8), [BK(bank)], dkeys)
        else:
            tcopy(eng, dst, pT.rearrange("p (c t) -> p c t", c=8), [BK(bank)], dkeys)

    def norm_res_two_banks(b0, b1, gidx, res_ap, res_keys, out_ap, out_keys, ti, var_mult, var_add):
        assert b1 == b0 + 1 and b0 % 2 == 0
        keys = [BK(b0), BK(b1)]
        acc = Bk2(b0 // 2)
        ss, ssk = newstat()
        jb, jk = njunkc()
        act(jb, acc, AF.Square, keys, [ssk, jk, (jk, 1)], accum_out=ss)
        r, rk = rstd_from_ss(ss, ssk, var_mult, var_add)
        tk = ('tmpf', ti)
        stt(tmpf[ti], acc, r, gain[gidx], ALU.mult, ALU.mult, keys + [rk, ('gain', gidx)], [tk])
        tt('dve', out_ap, tmpf[ti], res_ap, ALU.add, [tk] + list(res_keys), out_keys)

    wgu_pre = V(0, 90112).rearrange("p (c n) -> p c n", c=8)
    wb_gu3 = wb_gu.rearrange("(c p) n -> p c n", p=128)
    C1P = [(0, 1), (2, 3), (6, 7), (0, 1)]
    C3P = [(2, 3), (6, 7), (0, 1), (2, 3)]
    for s4 in range(4):
        def c1_a(Tl):
            T = s4 * 4 + Tl
            k = Tl % 2
            dma(xl[Tl], x_d[T * 128:(T + 1) * 128, :], [], [('xl', Tl)])
            transposes8(lambda c, T=T: mixtok[:, T, c * 128:(c + 1) * 128], [('mix', T, hs) for hs in range(12)],
                        mixT[k], [('mixT', k)], bank=4 + Tl % 2)

        def c1_b(Tl):
            k = Tl % 2
            bp = C1P[Tl]
            for half in range(2):
                for c in range(8):
                    mm(Bk(bp[half]), mixT[k][:, c, :], wout_sb[:, c, half * 512:(half + 1) * 512], c == 0, c == 7,
                       [('mixT', k), ('wout',)], [BK(bp[half])])

        def c1_ab(Tl):
            if Tl == 0:
                c1_a(0)
            if Tl + 1 < 4:
                c1_a(Tl + 1)
            c1_b(Tl)

        c1s = {}

        def c1_cA(Tl):
            bp = C1P[Tl]
            ss, ssk = newstat()
            jb, jk = njunkc()
            act(jb, Bk2(bp[0] // 2), AF.Square, [BK(bp[0]), BK(bp[1])], [ssk, jk, (jk, 1)], accum_out=ss)
            c1s[('r1', Tl)] = rstd_from_ss(ss, ssk, 1.0 / D, EPS)

        def c1_cB(Tl):
            k = Tl % 2
            bp = C1P[Tl]
            r, rk = c1s[('r1', Tl)]
            tk = ('tmpf', k)
            stt(tmpf[k], Bk2(bp[0] // 2), r, gain[0], ALU.mult, ALU.mult, [BK(bp[0]), BK(bp[1]), rk, ('gain', 0)], [tk])
            tt('dve', x1[:, Tl, :], tmpf[k], xl[Tl], ALU.add, [tk, ('xl', Tl)], [('x1', Tl)])

        def c1_cC(Tl):
            ss, ssk = newstat()
            jb, jk = njunkc()
            act(jb, x1[:, Tl, :], AF.Square, [('x1', Tl)], [ssk, jk, (jk, 1)], accum_out=ss)
            c1s[('r2', Tl)] = rstd_from_ss(ss, ssk, 1.0 / D, EPS)

        def c1_cD(Tl):
            k = Tl % 2
            r, rk = c1s[('r2', Tl)]
            stt(xn2[k], x1[:, Tl, :], r, gain[1], ALU.mult, ALU.mult, [('x1', Tl), rk, ('gain', 1)], [('xn2', k)])

        def c1_e(Tl):
            k = Tl % 2
            transposes8(lambda c, k=k: xn2[k][:, c * 128:(c + 1) * 128], [('xn2', k)],
                        h2T[:, :, Tl * 128:(Tl + 1) * 128], [('h2T', Tl)], bank=4 + Tl % 2)

        c1_ab(0); c1_ab(1); c1_cA(0)
        c1_ab(2); c1_cA(1); c1_cB(0)
        c1_ab(3); c1_cA(2); c1_cB(1); c1_cC(0)
        c1_cA(3); c1_cB(2); c1_cC(1); c1_cD(0)
        c1_cB(3); c1_cC(2); c1_cD(1); c1_e(0)
        c1_cC(3); c1_cD(2); c1_e(1)
        c1_cD(3); c1_e(2)
        c1_e(3)
        if s4 == 3:
            for c in range(4):
                dma(wgu_pre[:, c:c + 1, :], wb_gu3[:, c:c + 1, :], [],
                    [('wgu_pre', c), ('wout',)] + [('mix', T, hs) for T in range(NT) for hs in range(12)])
        H2K = [('h2T', t) for t in range(4)]
        for fc in range(8):
            b = workbank()
            for c in range(8):
                mm(Bk(b), wcq_sb[:, c, fc * 128:(fc + 1) * 128], h2T[:, c, :], c == 0, c == 7, H2K + [('wcq',)], [BK(b)])
            acopy(qcT[:, fc, :], Bk(b), [BK(b)], [('qcT', fc)])
        if s4 == 3:
            dma(wgu_pre[:, 4:5, :], wb_gu3[:, 4:5, :], [], [('wgu_pre', 4), ('wout',), ('wcq',)])
        for h in range(4):
            for m in range(2):
                b = workbank()
                for jj in range(2):
                    mm(Bk(b), kcT[:, 2 * h + jj, m * 128:(m + 1) * 128], qcT[:, 2 * h + jj, :], jj == 0, jj == 1,
                       [('kcT',), ('qcT', 2 * h + jj)], [BK(b)])
                pi = 2 * h + m
                act(PTc[pi], Bk(b), AF.Exp, [BK(b)], [('PTc', pi)], scale=1.0 / 16.0)
        for h in range(4):
            for Tl in range(4):
                b = workbank()
                for m in range(2):
                    mm(Bk(b)[:, 0:257], PTc[2 * h + m][:, Tl * 128:(Tl + 1) * 128], vc[:, m, h, 0:257], m == 0, m == 1,
                       [('PTc', 2 * h + m), ('vc', m), ('vc1',)], [BK(b)])
                ri = (h * 4 + Tl) % 4
                P.add('dve', lambda e, ri=ri, b=b: e.reciprocal(out=recc[ri][:, 0:1], in_=Bk(b)[:, 256:257]), [BK(b)],
                      [('recc', ri)])
                ts('dve', oc_tok[:, Tl, h * 256:(h + 1) * 256], Bk(b)[:, 0:256], recc[ri][:, 0:1], None, ALU.mult, None,
                   [BK(b), ('recc', ri)], [('oc', Tl, h)])
        c3s = {}

        def c3_a(Tl):
            k = Tl % 2
            transposes8(lambda c, Tl=Tl: oc_tok[:, Tl, c * 128:(c + 1) * 128], [('oc', Tl, h) for h in range(4)],
                        ocT[k], [('ocT', k)], bank=4 + Tl % 2)

        def c3_b(Tl):
            k = Tl % 2
            bp = C3P[Tl]
            for half in range(2):
                for c in range(8):
                    mm(Bk(bp[half]), ocT[k][:, c, :], wco_sb[:, c, half * 512:(half + 1) * 512], c == 0, c == 7,
                       [('ocT', k), ('wco',)], [BK(bp[half])])

        def c3_ab(Tl):
            if Tl == 0:
                c3_a(0)
            if Tl + 1 < 4:
                c3_a(Tl + 1)
            c3_b(Tl)

        def c3_cA(Tl):
            bp = C3P[Tl]
            ss, ssk = newstat()
            jb, jk = njunkc()
            act(jb, Bk2(bp[0] // 2), AF.Square, [BK(bp[0]), BK(bp[1])], [ssk, jk, (jk, 1)], accum_out=ss)
            c3s[Tl] = rstd_from_ss(ss, ssk, 1.0 / D, EPS)

        def c3_cB(Tl):
            T = s4 * 4 + Tl
            k = Tl % 2
            bp = C3P[Tl]
            r, rk = c3s[Tl]
            tk = ('tmpf', k)
            stt(tmpf[k], Bk2(bp[0] // 2), r, gain[2], ALU.mult, ALU.mult, [BK(bp[0]), BK(bp[1]), rk, ('gain', 2)], [tk])
            tt('dve', xl[Tl], tmpf[k], x1[:, Tl, :], ALU.add, [tk, ('x1', Tl)], [('xl', Tl)])
            dma(x2s[T * 128:(T + 1) * 128, :], xl[Tl], [('xl', Tl)], [('x2s', T)])

        c3_ab(0); c3_ab(1); c3_cA(0)
        c3_ab(2); c3_cA(1); c3_cB(0)
        c3_ab(3); c3_cA(2); c3_cB(1)
        c3_cA(3); c3_cB(2)
        c3_cB(3)

    P.barrier()
    if level <= 4:
        return finish()

    wgu_sb = V(0, 90112).rearrange("p (c n) -> p c n", c=8)
    wdn_sb = V(139648, 45056).rearrange("p (c n) -> p c n", c=NFC)
    actT = V(O_WG, 11264).rearrange("p (c t) -> p c t", c=NFC)
    h3Tb = [V(O_WG + 11264, 4096).rearrange("p (c t) -> p c t", c=8),
            V(O_STAGE + 12288, 4096).rearrange("p (c t) -> p c t", c=8)]
    do = [184704]

    def dalloc(nbytes, dt=BF16):
        o = do[0]
        do[0] += (nbytes + 63) // 64 * 64
        return V(o, nbytes, dt)

    x2l = [dalloc(4096, F32) for _ in range(4)]
    xn3 = [dalloc(2048) for _ in range(2)]
    sgate = [dalloc(1024, F32) for _ in range(3)]
    assert do[0] <= ARENA_BYTES, do[0]
    junkdb = [V(90112, 2048), V(92160, 2048)]
    jdc = [0]

    def njunkd():
        i = jdc[0] % 2; jdc[0] += 1
        return junkdb[i], ('junkd', i)
    obuf = [V(O_STAGE + i * 4096, 4096, F32) for i in range(2)]
    tmpD = V(O_STAGE + 8192, 4096, F32)

    dma(gain[0], g_d["g_pre_ffn"].partition_broadcast(128), [], [('gain', 0)])
    dma(gain[1], g_d["g_post_ffn"].partition_broadcast(128), [], [('gain', 1)])
    STAGE_KEYS = [('stage', s, k) for s in range(2) for k in range(4)]

    first_stage_overlay = [True]
    first_h3_overlay = [True]

    d1_state = {}

    def d1a(u8, Tl):
        T = u8 * 2 + Tl
        xi = (u8 % 2) * 2 + Tl
        dma(x2l[xi], x2s[T * 128:(T + 1) * 128, :], [('x2s', T)], [('x2l', xi)])
        ss, ssk = newstat()
        jb, jk = njunkd()
        act(jb, x2l[xi], AF.Square, [('x2l', xi)], [ssk, (jk, 0), (jk, 1)], accum_out=ss)
        d1_state[(u8, Tl)] = rstd_from_ss(ss, ssk, 1.0 / D, EPS)

    def d1b(u8, Tl):
        xi = (u8 % 2) * 2 + Tl
        r, rk = d1_state[(u8, Tl)]
        stt(xn3[Tl], x2l[xi], r, gain[0], ALU.mult, ALU.mult, [('x2l', xi), rk, ('gain', 0)], [('xn3', Tl)])

    def d1c(u8, Tl):
        h3T = h3Tb[u8 % 2]
        extra = []
        if u8 % 2 == 1 and first_h3_overlay[0]:
            extra = STAGE_KEYS
            first_h3_overlay[0] = False
        transposes8(lambda c, k=Tl: xn3[k][:, c * 128:(c + 1) * 128], [('xn3', Tl)],
                    h3T[:, :, Tl * 128:(Tl + 1) * 128], [('h3T', u8 % 2, Tl)] + extra, eng='dve')

    def d1(u8, tls=(0, 1)):
        for Tl in tls:
            d1a(u8, Tl); d1b(u8, Tl); d1c(u8, Tl)

    D1_SCHED = {2: [(d1a, 0)], 4: [(d1b, 0)], 6: [(d1c, 0)], 8: [(d1a, 1)], 10: [(d1b, 1)], 12: [(d1c, 1)]}

    d4_state = {}
    d4_pending = {}

    def d4a(u8):
        for Tl in range(2):
            jb, jk = njunkd()
            ss, ssk = newstat()
            act(jb, Bk2(Tl), AF.Square, [BK(2 * Tl), BK(2 * Tl + 1)], [ssk, (jk, 0), (jk, 1)], accum_out=ss)
            d4_state[(u8, Tl)] = rstd_from_ss(ss, ssk, 1.0 / D, EPS)

    def d4b(u8):
        extra = STAGE_KEYS if first_stage_overlay[0] else []
        first_stage_overlay[0] = False
        for Tl in range(2):
            r, rk = d4_state[(u8, Tl)]
            stt(obuf[Tl], Bk2(Tl), r, gain[1], ALU.mult, ALU.mult,
                [BK(2 * Tl), BK(2 * Tl + 1), rk, ('gain', 1)], [('obuf', Tl, 0), ('obuf', Tl, 1)] + extra)

    def d4c(u8):
        for Tl in range(2):
            T = u8 * 2 + Tl
            xi = (u8 % 2) * 2 + Tl
            tt('dve', obuf[Tl], obuf[Tl], x2l[xi], ALU.add, [('obuf', Tl, 0), ('obuf', Tl, 1), ('x2l', xi)],
               [('obuf', Tl, 0), ('obuf', Tl, 1)])
            dma(out_d[T * 128:(T + 1) * 128, :], obuf[Tl], [('obuf', Tl, 0), ('obuf', Tl, 1)], [('out', T)])

    DLAG = 4
    d1(0)
    for gq in range(6):
        nfc = 4 if gq < 5 else 2
        ncol = nfc * 128
        for (kind, base) in (('g', 0), ('u', DFF)):
            c0 = base + gq * 512
            load_b16(wgu_sb[:, 5:8, c0:c0 + ncol], wb_gu, 5, 3, c0, ncol, [('wgu', kind, gq)], [])
        load_b16(wdn_sb[:, gq * 4:gq * 4 + nfc, :], wb_dn, gq * 4, nfc, 0, 1024,
                 [('wdn', rb) for rb in range(gq * 2, gq * 2 + nfc // 2)], [])
    for u8 in range(8):
        h3T = h3Tb[u8 % 2]
        H3K = [('h3T', u8 % 2, 0), ('h3T', u8 % 2, 1)]

        def down(fc):
            for Tl in range(2):
                for half in range(2):
                    b = Tl * 2 + half
                    mm(Bk(b), actT[:, fc, Tl * 128:(Tl + 1) * 128], wdn_sb[:, fc, half * 512:(half + 1) * 512],
                       fc == 0, fc == NFC - 1, [('actT', fc), ('wdn', fc // 2)], [BK(b)])

        for fc in range(NFC):
            b = 5 + (fc % 3)
            for c in range(8):
                mm(Bk(b)[:, 0:256], wgu_sb[:, c, fc * 128:(fc + 1) * 128], h3T[:, c, :], c == 0, c == 7,
                   H3K + [('wgu', 'g', fc // 4)], [BK(b)])
            for c in range(8):
                mm(Bk(b)[:, 256:512], wgu_sb[:, c, DFF + fc * 128:DFF + (fc + 1) * 128], h3T[:, c, :], c == 0, c == 7,
                   H3K + [('wgu', 'u', fc // 4)], [BK(b)])
            k = fc % 3
            act(sgate[k], Bk(b)[:, 0:256], AF.Silu, [BK(b)], [('sgate', k)])
            tt('dve', actT[:, fc, :], sgate[k], Bk(b)[:, 256:512], ALU.mult, [('sgate', k), BK(b)], [('actT', fc)])
            if fc >= DLAG:
                down(fc - DLAG)
            if fc in d4_pending:
                fn_, u_ = d4_pending.pop(fc)
                fn_(u_)
            if u8 + 1 < 8:
                for (fn, Tl) in D1_SCHED.get(fc, ()):
                    fn(u8 + 1, Tl)
        for fc in range(NFC - DLAG, NFC):
            down(fc)
        d4a(u8)
        if u8 == 7:
            d4b(u8); d4c(u8)
        else:
            d4_pending[1] = (d4b, u8)
            d4_pending[2] = (d4c, u8)

    return finish()


_NC_CACHE = {}


def kernel(x, mem, g_pre_mix, w_in, w_out, g_post_mix, g_pre_cross, g_mem, w_cq, w_ckv, w_co,
           g_post_cross, g_pre_ffn, w_gate_up, w_down, g_post_ffn):
    f32 = lambda a: np.ascontiguousarray(np.asarray(a, dtype=np.float32))
    x = f32(x); mem = f32(mem)
    B = x.shape[0]
    if 'nc' not in _NC_CACHE:
        _NC_CACHE['nc'] = build_program()
    nc = _NC_CACHE['nc']
    shared = {
        "g_pre_mix": f32(g_pre_mix).reshape(1, D), "g_post_mix": f32(g_post_mix).reshape(1, D),
        "g_pre_cross": f32(g_pre_cross).reshape(1, D), "g_mem": f32(g_mem).reshape(1, D),
        "g_post_cross": f32(g_post_cross).reshape(1, D), "g_pre_ffn": f32(g_pre_ffn).reshape(1, D),
        "g_post_ffn": f32(g_post_ffn).reshape(1, D),
        "w_in": f32(w_in).reshape(D, 3072), "w_out": f32(w_out).reshape(D, D), "w_cq": f32(w_cq).reshape(D, D),
        "w_ckv": f32(w_ckv).reshape(D, 2 * D), "w_co": f32(w_co).reshape(D, D),
        "w_gate_up": f32(w_gate_up).reshape(D, 2 * DFF), "w_down": f32(w_down).reshape(DFF, D),
        "c_ident": _CONST['ident'], "c_maskbias": _CONST['maskbias'], "c_kind": _CONST['kind'],
        "c_csM": _CONST['csM'], "c_csR": _CONST['csR'],
        "c_dec": _CONST['dec'], "c_maskT": _CONST['maskT'],
    }
    in_maps = []
    for b in range(B):
        m = dict(shared)
        m["x"] = x[b]
        m["mem"] = mem[b]
        in_maps.append(m)
    res = run_bass_kernel_spmd(nc, in_maps, core_ids=list(range(B)))
    return np.stack([np.asarray(r["out"], dtype=np.float32) for r in res.results], axis=0)
```

```python
import math
import os
from collections import deque
from contextlib import ExitStack

import numpy as np
import ml_dtypes

import concourse.bass as bass
import concourse.mybir as mybir
from concourse.bass_utils import run_bass_kernel_spmd

F32 = mybir.dt.float32
BF16 = mybir.dt.bfloat16
AF = mybir.ActivationFunctionType
ALU = mybir.AluOpType
AX = mybir.AxisListType
NPBF = ml_dtypes.bfloat16

S = 2048
D = 1024
NT = S // 128
NMEM = 256
DFF = 2816
NFC = DFF // 128
EPS = 1e-6
BIGK = 30000.0
ARENA_BYTES = 210400

ENGS = ['pe', 'act', 'dve', 'pool', 'sp']
NDS = 8


class Ins:
    __slots__ = ('eng', 'fn', 'deps', 'marked', 'sem', 'semval', 'isdma', 'idx')


class Prog:
    def __init__(self):
        self.streams = {e: [] for e in ENGS}
        self.lastw = {}
        self.readers = {}
        self.ndma = {e: 0 for e in ENGS}
        self.dma_hist = {e: [] for e in ENGS}
        self.bank_last = {}

    def add(self, eng, fn, reads=(), writes=(), dma=False):
        ins = Ins()
        ins.eng = eng; ins.fn = fn; ins.isdma = dma; ins.marked = False
        ins.sem = None; ins.semval = None
        deps = []
        for k in reads:
            w = self.lastw.get(k)
            if w is not None:
                deps.append(w)
        for k in writes:
            w = self.lastw.get(k)
            if w is not None:
                deps.append(w)
            deps.extend(self.readers.get(k, ()))
        if dma:
            i = self.ndma[eng]
            ins.sem = ('dma', eng, i % NDS)
            ins.semval = 16 * (i // NDS + 1)
            if i >= NDS:
                deps.append(self.dma_hist[eng][i - NDS])
            self.ndma[eng] += 1
            self.dma_hist[eng].append(ins)
        for k in list(reads) + list(writes):
            if isinstance(k, tuple) and k and k[0] == 'B':
                la = self.bank_last.setdefault(k, {})
                for e2, d in la.items():
                    if e2 != eng:
                        deps.append(d)
        seen = set(); best = {}; dd = []
        for d in deps:
            if id(d) in seen or d is ins:
                continue
            seen.add(id(d))
            if d.isdma:
                dd.append(d)
                continue
            if (not dma) and d.eng == 'pe' and eng == 'pe':
                continue
            b = best.get(d.eng)
            if b is None or d.idx > b.idx:
                best[d.eng] = d
        dd.extend(best.values())
        for d in dd:
            d.marked = True
        ins.deps = dd
        for k in reads:
            self.readers.setdefault(k, []).append(ins)
        for k in writes:
            self.lastw[k] = ins
            self.readers[k] = []
        ins.idx = len(self.streams[eng])
        self.streams[eng].append(ins)
        for k in list(reads) + list(writes):
            if isinstance(k, tuple) and k and k[0] == 'B':
                self.bank_last.setdefault(k, {})[eng] = ins
        return ins

    def barrier(self):
        lasts = []
        for e in ENGS:
            for ins in reversed(self.streams[e]):
                if not ins.isdma:
                    lasts.append(ins)
                    break
        pend = []
        for e in ENGS:
            pend.extend(self.dma_hist[e][-NDS:])
        for e in ENGS:
            n = self.add(e, lambda eng: eng.nop())
            for d in lasts + pend:
                if d is n or d in n.deps:
                    continue
                if d.eng == 'pe' and e == 'pe' and not d.isdma:
                    continue
                n.deps.append(d)
                d.marked = True
        self.lastw = {}
        self.readers = {}
        self.bank_last = {}

    def finalize(self):
        for e in ENGS:
            c = 0
            for ins in self.streams[e]:
                if ins.isdma:
                    continue
                if ins.marked:
                    c += 1
                    ins.sem = ('eng', e); ins.semval = c
            assert c < 60000, (e, c)

    def sem_keys(self):
        ks = [('eng', e) for e in ENGS]
        for e in ENGS:
            for j in range(min(NDS, self.ndma[e])):
                ks.append(('dma', e, j))
        return ks

    def emit(self, nc, sems):
        engmap = {'pe': 'tensor', 'act': 'scalar', 'dve': 'vector', 'pool': 'gpsimd', 'sp': 'sync'}
        with nc.Block() as block:
            for e in ENGS:
                stream = self.streams[e]

                def body(eng, stream=stream):
                    known = {}
                    for ins in stream:
                        need = {}
                        for d in ins.deps:
                            if d.semval > need.get(d.sem, 0):
                                need[d.sem] = d.semval
                        for sk, sv in need.items():
                            if known.get(sk, 0) >= sv:
                                continue
                            eng.wait_ge(sems[sk], sv)
                            known[sk] = sv
                        bi = ins.fn(eng)
                        if ins.isdma:
                            bi.then_inc(sems[ins.sem], 16)
                        elif ins.marked:
                            bi.then_inc(sems[ins.sem], 1)
                getattr(block, engmap[e])(body)


def host_constants():
    c = {}
    c['ident'] = np.eye(128, dtype=np.float32).astype(NPBF)
    kk = np.arange(128)[:, None]; qq = np.arange(128)[None, :]
    c['maskbias'] = np.where(kk <= qq, 0.0, -BIGK).astype(np.float32).astype(NPBF)
    kind = np.zeros((8, S), np.float32)
    for n in range(8):
        kind[n, n * 256:(n + 1) * 256] = BIGK
    c['kind'] = kind.astype(NPBF)
    pos = (np.arange(NT)[None, :] * 128 + np.arange(128)[:, None]).astype(np.float64)
    moba_inv = np.power(np.float64(500000.0), -np.arange(8, dtype=np.float64) * 2.0 / 16.0)
    ang = pos[:, :, None] * moba_inv[None, None, :]
    cm = np.cos(ang); sm = np.sin(ang)
    c['csM'] = np.stack([np.concatenate([cm, cm], -1), np.concatenate([sm, sm], -1)], axis=2).astype(np.float32)\
        .reshape(128, NT * 32)
    ret_inv = 1.0 / np.power(np.float64(10000.0), np.linspace(0.0, 1.0, 32, dtype=np.float64))
    angr = pos[:, :, None] * ret_inv[None, None, :]
    cr = np.cos(angr); sr = np.sin(angr)
    ksc = 64.0 ** -0.5
    cr2 = np.concatenate([cr, cr], -1); sr2 = np.concatenate([sr, sr], -1)
    c['csR'] = np.stack([np.stack([cr2, sr2], axis=2), np.stack([cr2 * ksc, sr2 * ksc], axis=2)], axis=2)\
        .astype(np.float32).reshape(128, NT * 256)
    hh = np.arange(4, dtype=np.float64)
    log_g = np.log(1.0 - np.power(2.0, -5.0 - hh))
    idx = np.arange(128, dtype=np.float64)
    qdec = np.exp(log_g[None, :] * (idx[:, None] + 1.0))
    kdec = np.exp(log_g[None, :] * (127.0 - idx[:, None]))
    c['dec'] = np.concatenate([qdec, kdec], axis=1).astype(np.float32)
    diff = idx[None, :] - idx[:, None]
    mt = np.where(diff[:, None, :] >= 0, np.exp(log_g[None, :, None] * np.maximum(diff[:, None, :], 0.0)), 0.0)
    c['maskT'] = mt.astype(np.float32).reshape(128, 4 * 128)
    c['cd'] = [float(np.exp(log_g[h] * 128.0)) for h in range(4)]
    return c


_CONST = host_constants()


def build_program(level=5, ngroups=8, dumps=()):
    nc = bass.Bass("TRN2", target_bir_lowering=False)
    P = Prog()
    REG = {}

    def din(name, shape, dt=F32):
        return nc.dram_tensor(name, list(shape), dt, kind="ExternalInput").ap()

    x_d = din("x", [S, D]); mem_d = din("mem", [NMEM, D])
    g_d = {n: din(n, [1, D]) for n in ["g_pre_mix", "g_post_mix", "g_pre_cross", "g_mem", "g_post_cross",
                                       "g_pre_ffn", "g_post_ffn"]}
    w_in = din("w_in", [D, 3072]); w_out = din("w_out", [D, D]); w_cq = din("w_cq", [D, D])
    w_ckv = din("w_ckv", [D, 2 * D]); w_co = din("w_co", [D, D])
    w_gu = din("w_gate_up", [D, 2 * DFF]); w_down = din("w_down", [DFF, D])
    c_ident = din("c_ident", [128, 128], BF16); c_maskbias = din("c_maskbias", [128, 128], BF16)
    c_kind = din("c_kind", [8, S], BF16)
    c_csM = din("c_csM", [128, NT * 32]); c_csR = din("c_csR", [128, NT * 256])
    c_dec = din("c_dec", [128, 8]); c_maskT = din("c_maskT", [128, 512])
    out_d = nc.dram_tensor("out", [S, D], F32, kind="ExternalOutput").ap()
    x2s = nc.dram_tensor("x2s", [S, D], F32, kind="Internal").ap()
    wb_in = nc.dram_tensor("wb_in", [D, 3072], BF16, kind="Internal").ap()
    wb_out = nc.dram_tensor("wb_out", [D, D], BF16, kind="Internal").ap()
    wb_cq = nc.dram_tensor("wb_cq", [D, D], BF16, kind="Internal").ap()
    wb_co = nc.dram_tensor("wb_co", [D, D], BF16, kind="Internal").ap()
    wb_ckv = nc.dram_tensor("wb_ckv", [D, 2 * D], BF16, kind="Internal").ap()
    wb_gu = nc.dram_tensor("wb_gu", [D, 2 * DFF], BF16, kind="Internal").ap()
    wb_dn = nc.dram_tensor("wb_dn", [DFF, D], BF16, kind="Internal").ap()
    CD = _CONST['cd']

    es = ExitStack()
    arena = es.enter_context(nc.sbuf_tensor("arena", [128, ARENA_BYTES // 2], BF16))
    stats = es.enter_context(nc.sbuf_tensor("stats", [128, 512], F32))
    bpairs = [es.enter_context(nc.psum_tensor(f"bpair{i}", [128, 1024], F32)) for i in range(4)]

    def V(off, nbytes, dt=BF16):
        assert off % 4 == 0 and off + nbytes <= ARENA_BYTES, (off, nbytes)
        a = arena[:, off // 2:(off + nbytes) // 2]
        if dt == F32:
            a = a.bitcast(F32)
        return a

    def Bk(i):
        return bpairs[i // 2][:, (i % 2) * 512:(i % 2 + 1) * 512]

    def Bb(i):
        return bpairs[i // 2][:, (i % 2) * 512:(i % 2 + 1) * 512].bitcast(BF16)

    def Bk2(p):
        return bpairs[p][:]

    BK = lambda i: ('B', i)

    def dma(out, in_, reads, writes):
        P.add('sp', lambda e: e.dma_start(out=out, in_=in_), reads, writes, dma=True)

    def mm(out, lhsT, rhs, start, stop, reads, writes):
        P.add('pe', lambda e: e.matmul(out, lhsT=lhsT, rhs=rhs, start=start, stop=stop), reads, writes)

    def act(out, in_, func, reads, writes, scale=1.0, accum_out=None):
        if accum_out is None:
            P.add('act', lambda e: e.activation(out=out, in_=in_, func=func, scale=scale), reads, writes)
        else:
            P.add('act', lambda e: e.activation(out=out, in_=in_, func=func, scale=scale, accum_out=accum_out),
                  reads, writes)

    def acopy(out, in_, reads, writes):
        P.add('act', lambda e: e.copy(out=out, in_=in_), reads, writes)

    def tt(eng, out, in0, in1, op, reads, writes):
        P.add(eng, lambda e: e.tensor_tensor(out=out, in0=in0, in1=in1, op=op), reads, writes)

    def ts(eng, out, in0, s1, s2, op0, op1, reads, writes):
        if op1 is None:
            P.add(eng, lambda e: e.tensor_scalar(out=out, in0=in0, scalar1=s1, scalar2=None, op0=op0), reads, writes)
        else:
            P.add(eng, lambda e: e.tensor_scalar(out=out, in0=in0, scalar1=s1, scalar2=s2, op0=op0, op1=op1),
                  reads, writes)

    def stt(out, in0, scalar, in1, op0, op1, reads, writes):
        P.add('dve', lambda e: e.scalar_tensor_tensor(out=out, in0=in0, scalar=scalar, in1=in1, op0=op0, op1=op1),
              reads, writes)

    def tcopy(eng, out, in_, reads, writes):
        P.add(eng, lambda e: e.tensor_copy(out=out, in_=in_), reads, writes)

    def memset(eng, ap, val, writes):
        P.add(eng, lambda e: e.memset(ap, val), (), writes)

    def finish():
        outk = []
        for name in dumps:
            ap, dt, shape = REG[name]
            d = nc.dram_tensor("dbg_" + name, list(shape), dt, kind="ExternalOutput").ap()
            dma(d, ap, [], [('dbg', name)])
            outk.append(('dbg', name))
        P.add('sp', lambda e: e.nop(), reads=[('out', T) for T in range(NT)] + outk)
        P.finalize()
        sems = {kk: es.enter_context(nc.semaphore("s_" + "_".join(map(str, kk)))) for kk in P.sem_keys()}
        P.emit(nc, sems)
        es.close()
        return nc

    stat_ctr = [0]

    def newstat():
        i = stat_ctr[0] % 508
        stat_ctr[0] += 1
        return stats[:, i:i + 1], ('st', stat_ctr[0])

    memset('pool', stats[:, 508:509], -0.5, ['mhalf'])
    MHALF = stats[:, 508:509]

    def rstd_from_ss(ss_ap, ss_key, mult, add):
        v, vk = newstat()
        ts('dve', v, ss_ap, mult, add, ALU.mult, ALU.add, [ss_key], [vk])
        r, rk = newstat()
        tt('pool', r, v, MHALF, ALU.pow, [vk, 'mhalf'], [rk])
        return r, rk

    O_MIX = 0
    O_HT = 32768
    O_WCO = 65536
    O_MEMT = 81920
    O_KCT = O_MEMT + 4096
    O_VC = O_KCT + 4096
    O_IDENT = 94336
    O_GAIN = 94592
    O_WG = 106880
    O_STAGE = 123264
    O_MOBA = 139648
    O_RET = 160256
    O_MC = 182784
    O_TMP = 194304

    mixtok = V(O_MIX, 32768).rearrange("p (t f) -> p t f", t=NT)
    hT = V(O_HT, 32768).rearrange("p (c t) -> p c t", c=8)
    wout_sb = V(O_HT, 16384).rearrange("p (c n) -> p c n", c=8)
    wcq_sb = V(O_HT + 16384, 16384).rearrange("p (c n) -> p c n", c=8)
    wco_sb = V(O_WCO, 16384).rearrange("p (c n) -> p c n", c=8)
    memT = V(O_MEMT, 4096).rearrange("p (c t) -> p c t", c=8)
    kcT = V(O_KCT, 4096).rearrange("p (c t) -> p c t", c=8)
    vc = V(O_VC, 4128).rearrange("p (m h d) -> p m h d", m=2, h=4)
    ident = V(O_IDENT, 256)
    gain = [V(O_GAIN + i * 4096, 4096, F32) for i in range(3)]
    wg = [V(O_WG + i * 8192, 8192).rearrange("p (c n) -> p c n", c=8) for i in range(2)]
    wg += [V(O_WCO + i * 8192, 8192).rearrange("p (c n) -> p c n", c=8) for i in range(2)]
    wslot = lambda g: (g % 2) if g < 4 else 2 + ((g - 4) % 2)
    stage = [V(O_STAGE + i * 8192, 8192, F32) for i in range(2)]

    stage_ctr = [0]
    cast_ctr = [0]
    CAST_SEQ = [['pool']]

    def load_piece(dst, srcs, shape, wkeys, extra_reads=()):
        s = stage_ctr[0] % 2
        stage_ctr[0] += 1
        n = int(np.prod(shape[1:]))
        assert n <= 2048
        for (sub, src_ap, seg) in srcs:
            dma(sub(stage[s]), src_ap, [], [('stage', s, seg)])
        stv = stage[s][:, 0:n]
        if len(shape) == 3:
            stv = stv.rearrange("p (a b) -> p a b", a=shape[1])
        elif len(shape) == 4:
            stv = stv.rearrange("p (a b c) -> p a b c", a=shape[1], b=shape[2])
        ce = CAST_SEQ[0][cast_ctr[0] % len(CAST_SEQ[0])]; cast_ctr[0] += 1
        if ce == 'act':
            acopy(dst, stv, [('stage', s, k) for k in range(4)] + list(extra_reads), wkeys)
        else:
            tcopy(ce, dst, stv, [('stage', s, k) for k in range(4)] + list(extra_reads), wkeys)

    def load_plain(dst3, w_ap, r0, nr, c0, ncol, wkeys, extra_reads=()):
        src = w_ap.rearrange("(c p) n -> p c n", p=128)[:, r0:r0 + nr, c0:c0 + ncol]
        load_piece(dst3, [(lambda st: st[:, 0:nr * ncol].rearrange("p (a b) -> p a b", a=nr), src, 0)],
                   [128, nr, ncol], wkeys, extra_reads)

    conv_jobs = []
    for (nm, src, dstb, ncols) in (("ckv", w_ckv, wb_ckv, 2 * D), ("co", w_co, wb_co, D), ("out", w_out, wb_out, D),
                                   ("cq", w_cq, wb_cq, D), ("gu", w_gu, wb_gu, 2 * DFF)):
        for c0 in range(0, ncols, 512):
            conv_jobs.append((nm, c0 // 512, dstb[:, c0:c0 + 512], src[:, c0:c0 + 512]))
    for r0 in range(0, DFF, 512):
        r1 = min(r0 + 512, DFF)
        conv_jobs.append(("dn", r0 // 512, wb_dn[r0:r1, :], w_down[r0:r1, :]))
    in_jobs = [("in", c0 // 512, wb_in[:, c0:c0 + 512], w_in[:, c0:c0 + 512]) for c0 in range(0, 3072, 512)]
    conv_jobs = in_jobs + conv_jobs
    conv_pos = [0]

    def emit_conv(n=1):
        for _ in range(n):
            if conv_pos[0] >= len(conv_jobs):
                return
            nm, i, o_ap, i_ap = conv_jobs[conv_pos[0]]
            conv_pos[0] += 1
            P.add('pool', lambda e, o_ap=o_ap, i_ap=i_ap: e.dma_start(out=o_ap, in_=i_ap), [], [('cv', nm, i)], dma=True)

    def load_b16(dst3, wb_ap, r0, nr, c0, ncol, wkeys, cvkeys):
        src = wb_ap.rearrange("(c p) n -> p c n", p=128)[:, r0:r0 + nr, c0:c0 + ncol]
        dma(dst3, src, cvkeys, wkeys)

    w_in4 = w_in.rearrange("(c p) (s n) -> p c s n", p=128, n=512)
    w_in3 = w_in.rearrange("(c p) n -> p c n", p=128)

    wb_in3 = wb_in.rearrange("(c p) n -> p c n", p=128)

    def load_group_weights(g):
        slot = wslot(g)
        if g < 4:
            j = g
            segs = [(128 * j, 128, 0), (512 + 128 * j, 128, 128), (1024 + 128 * j, 128, 256)]
        else:
            r = g - 4
            segs = [(1536 + 64 * r, 64, 0), (1792 + 64 * r, 64, 64), (2048 + 128 * r, 128, 128),
                    (2560 + 128 * r, 128, 256)]
        for si, (c0, wd, o0) in enumerate(segs):
            dma(wg[slot][:, :, o0:o0 + wd], wb_in3[:, :, c0:c0 + wd], [('cv', 'in', c0 // 512)], [('wg', slot, si)])

    def load_group_weights_direct(g):
        slot = wslot(g)
        if g < 4:
            segs = [(128 * g, 128, 0), (512 + 128 * g, 128, 128), (1024 + 128 * g, 128, 256)]
        else:
            r = g - 4
            segs = [(1536 + 64 * r, 64, 0), (1792 + 64 * r, 64, 64), (2048 + 128 * r, 128, 128),
                    (2560 + 128 * r, 128, 256)]
        for si, (c0, wd, o0) in enumerate(segs):
            P.add('pool', lambda e, o0=o0, wd=wd, c0=c0: e.dma_start(out=wg[slot][:, :, o0:o0 + wd],
                                                                     in_=w_in3[:, :, c0:c0 + wd]),
                  [], [('wg', slot, si)], dma=True)

    def wgk(slot):
        return [('wg', slot, si) for si in range(4)]

    csM2 = V(O_MC, 2048, F32)
    csM = csM2.rearrange("p (t c d) -> p t c d", t=NT, c=2)
    maskT = V(O_MC + 2048, 2048, F32).rearrange("p (h i) -> p h i", h=4)
    maskbias = V(O_MC + 4096, 256)
    csR2 = V(O_STAGE, 16384, F32)
    csR = csR2.rearrange("p (t a c d) -> p t a c d", t=NT, a=2, c=2)
    dec = stats[:, 500:508]
    dma(ident, c_ident, [], ['ident'])
    dma(csM2, c_csM, [], ['csM'])
    dma(gain[0], g_d["g_pre_mix"].partition_broadcast(128), [], [('gain', 0)])
    dma(gain[1], g_d["g_mem"].partition_broadcast(128), [], [('gain', 1)])

    NXB = 4
    xb = [V(O_RET + i * 4096, 4096, F32) for i in range(NXB)]
    xn = [V(O_RET + 16384 + i * 2048, 2048) for i in range(2)]
    junk = V(O_RET + 20480, 2048)
    A0_KEYS = [('xb', i) for i in range(NXB)] + [('xn', 0), ('xn', 1), 'junk']

    a0_ctr = [0]

    def a0_tile(src_ap, gidx, dstT, dkeys):
        i = a0_ctr[0]; a0_ctr[0] += 1
        s = i % 2
        sx = i % NXB
        dma(xb[sx], src_ap, [], [('xb', sx)])
        ss, ssk = newstat()
        act(junk, xb[sx], AF.Square, [('xb', sx)], [ssk, 'junk'], accum_out=ss)
        r, rk = rstd_from_ss(ss, ssk, 1.0 / D, EPS)
        stt(xn[s], xb[sx], r, gain[gidx], ALU.mult, ALU.mult, [('xb', sx), rk, ('gain', gidx)], [('xn', s)])
        def tail(i=i, s=s, dstT=dstT, dkeys=dkeys):
            b = 2 + (i % 2)
            pT = Bb(b)
            for c in range(8):
                P.add('pe', lambda e, c=c, pT=pT, s=s: e.transpose(out=pT[:, c * 128:(c + 1) * 128],
                                                                   in_=xn[s][:, c * 128:(c + 1) * 128], identity=ident),
                      [('xn', s), 'ident'], [BK(b)])
            acopy(dstT, pT.rearrange("p (c t) -> p c t", c=8), [BK(b)], dkeys)
        a0_pend.append(tail)
        while len(a0_pend) > 1:
            a0_pend.pop(0)()

    a0_pend = []

    load_group_weights_direct(0)
    load_group_weights_direct(4)
    for m in range(int(os.environ.get('DBG_NMEM', '2'))):
        a0_tile(mem_d[m * 128:(m + 1) * 128, :], 1, memT[:, :, m * 128:(m + 1) * 128], [('memT', m)])
    dma(csR2, c_csR, [], ['csR'])
    dma(maskT, c_maskT.rearrange("p (h i) -> p h i", h=4), [], ['maskT'])
    dma(maskbias, c_maskbias, [], ['maskbias'])
    dma(dec, c_dec, [], ['dec'])
    for T in range(int(os.environ.get('DBG_NX', str(NT)))):
        a0_tile(x_d[T * 128:(T + 1) * 128, :], 0, hT[:, :, T * 128:(T + 1) * 128], [('hT', T)])
    while a0_pend:
        a0_pend.pop(0)()

    REG['hT'] = (V(O_HT, 32768), BF16, [128, 16384]); REG['memT'] = (V(O_MEMT, 4096), BF16, [128, 2048])
    if level <= 1:
        P.barrier()
        return finish()

    qTa = V(O_MOBA, 8192).rearrange("p (h t) -> p h t", h=2)
    kTa = V(O_MOBA + 8192, 8192).rearrange("p (h t) -> p h t", h=2)
    va = V(O_MOBA + 16384, 4224).rearrange("p (t h d) -> p t h d", t=NT, h=2)
    rT = V(O_RET, 12288).rearrange("p (a t) -> p a t", a=3)
    rv = V(O_RET + 12288, 4096).rearrange("p (t d) -> p t d", t=NT)
    kd = V(O_RET + 16384, 2048).rearrange("p (t d) -> p t d", t=NT)
    sg = V(O_RET + 18432, 4096).rearrange("p (t d) -> p t d", t=NT)

    to = [O_TMP]

    def talloc(nbytes, dt=BF16):
        o = to[0]
        to[0] += (nbytes + 63) // 64 * 64
        return V(o, nbytes, dt)

    PTb = [talloc(1024) for _ in range(4)]
    qk_tok = [talloc(576).rearrange("p (h d) -> p h d", h=4) for _ in range(4)]
    tallb = [talloc(512, F32).rearrange("p (h c d) -> p h c d", h=4, c=2) for _ in range(2)]
    g8b = [talloc(64, F32).rearrange("p (h d) -> p h d", h=2) for _ in range(2)]
    m8b = [talloc(64, F32).rearrange("p (h d) -> p h d", h=2) for _ in range(2)]
    kmT = talloc(32).rearrange("p (h d) -> p h d", h=2)
    kms = talloc(8, F32)
    recb = [talloc(4, F32) for _ in range(4)]
    tallr = [talloc(1024, F32).rearrange("p (a c d) -> p a c d", a=2, c=2) for _ in range(2)]
    qkr = [talloc(512, F32).rearrange("p (a d) -> p a d", a=2) for _ in range(2)]
    rtok = [talloc(384).rearrange("p (a d) -> p a d", a=3) for _ in range(2)]
    attnb = [talloc(256) for _ in range(2)]
    thb = [talloc(512, F32) for _ in range(2)]
    state_f = talloc(512, F32)
    state_b = [talloc(256) for _ in range(2)]
    assert to[0] <= ARENA_BYTES, to[0]

    memset('pool', qTa[64:128, :, :], 0.0, [('qTb', T) for T in range(NT)])
    memset('pool', kTa[64:128, :, :], 0.0, [('kind', 0), ('kind', 1)])
    for h in range(2):
        dma(kTa[64:72, h, :], c_kind, [], [('kind', h)])
    memset('pool', va[:, :, :, 64:65], 1.0, [('va1',)])
    memset('pool', kmT[0:64, :, :], 0.0, ['kmT'])
    memset('pool', stats[:, 509:510], 0.0, A0_KEYS + ['ret_ok'])
    memset('pool', vc[:, :, :, 256:257], 1.0, [('vc1',)])

    sc_ctr = [0]
    o_ctr = [0]
    pt_ctr = [0]
    rec_ctr = [0]
    pvq = deque()
    PV_LAG = 2
    junk2 = [talloc(256), talloc(128)]
    j2c = [0]

    def proj(u, g, T):
        b = u % 2
        slot = wslot(g)
        for c in range(8):
            mm(Bk(b)[:, 0:384], hT[:, c, T * 128:(T + 1) * 128], wg[slot][:, c, 0:384], c == 0, c == 7,
               [('hT', T)] + wgk(slot), [BK(b)])

    def flush_pv(n_keep):
        while len(pvq) > n_keep:
            pvq.popleft()()

    def m1(u, j, T):
        if T == 0:
            flush_pv(0)
        b = u % 2
        k = T % 2
        pb = Bk(b); pk = BK(b)
        slab = pb[:, 0:256].rearrange("p (h d) -> p h d", h=4)
        qt = qk_tok[T % 4]; qk = T % 4
        acopy(va[:, T, :, 0:64], pb[:, 256:384].rearrange("p (h d) -> p h d", h=2), [pk, ('va1',)], [('va', T)])
        tcopy('dve', qt[:, :, 16:64], slab[:, :, 16:64], [pk], [('qkt', qk, 2)])
        in0 = slab[:, :, 0:16].unsqueeze(2).broadcast_to([128, 4, 2, 16])
        in1 = csM[:, T, :, :].unsqueeze(1).broadcast_to([128, 4, 2, 16])
        ta = tallb[k]
        tt('dve', ta, in0, in1, ALU.mult, [pk, 'csM'], [('tall', k)])
        tt('pool', qt[:, :, 0:8], ta[:, :, 0, 0:8], ta[:, :, 1, 8:16], ALU.subtract, [('tall', k)], [('qkt', qk, 0)])
        tt('pool', qt[:, :, 8:16], ta[:, :, 0, 8:16], ta[:, :, 1, 0:8], ALU.add, [('tall', k)], [('qkt', qk, 1)])

    def m2(u, j, T):
        qt = qk_tok[T % 4]; qk = T % 4
        trb = Bb(2)
        for i in range(4):
            P.add('pe', lambda e, i=i, qt=qt, trb=trb: e.transpose(out=trb[0:64, i * 128:(i + 1) * 128],
                                                                   in_=qt[:, i, 0:64], identity=ident),
                  [('qkt', qk, 0), ('qkt', qk, 1), ('qkt', qk, 2), 'ident'], [BK(2)])
        tcopy('dve', qTa[0:64, :, T * 128:(T + 1) * 128], trb[0:64, 0:256].rearrange("p (h t) -> p h t", h=2),
              [BK(2)], [('qTa', T)])
        tcopy('dve', kTa[0:64, :, T * 128:(T + 1) * 128], trb[0:64, 256:512].rearrange("p (h t) -> p h t", h=2),
              [BK(2)], [('kTa', T)])
        if T % 2 == 1 and T // 2 <= 6:
            bidx = T // 2
            P.add('dve', lambda e, bidx=bidx: e.tensor_reduce(out=kms[0:64, :], in_=kTa[0:64, :, bidx * 256:(bidx + 1) * 256],
                                                              axis=AX.X, op=ALU.add),
                  [('kTa', T - 1), ('kTa', T)], ['kms'])
            ts('dve', kmT[0:64, :, bidx], kms[0:64, :], 1.0 / 256.0, None, ALU.mult, None, ['kms'], ['kmT'])

    def m3(u, j, T):
        if T < 8:
            return
        k = T % 2
        qt = qk_tok[T % 4]; qk = T % 4
        cur = T // 2
        for h in range(2):
            mm(Bk(3)[:, h * 8:(h + 1) * 8], qTa[0:64, h, T * 128:(T + 1) * 128], kmT[0:64, h, :], True, True,
               [('qTa', T), 'kmT'], [BK(3)])
        memset('pool', g8b[k], -1e30, [('g8', k)])
        memset('pool', g8b[k][:, :, cur:cur + 1], 1e30, [('g8', k)])
        tcopy('dve', g8b[k][:, :, 0:cur], Bk(3)[:, 0:16].rearrange("p (h d) -> p h d", h=2)[:, :, 0:cur],
              [BK(3)], [('g8', k)])
        for h in range(2):
            P.add('dve', lambda e, h=h: e.max(out=m8b[k][:, h, :], in_=g8b[k][:, h, :]), [('g8', k)], [('m8', k, h)])
            ts('dve', qt[:, h, 64:72], g8b[k][:, h, :], m8b[k][:, h, 3:4], 1.0, ALU.is_ge, ALU.subtract,
               [('g8', k), ('m8', k, h)], [('qkt', qk, 3 + h)])

    def m4(u, j, T):
        if T < 8:
            return
        qt = qk_tok[T % 4]; qk = T % 4
        b3 = Bb(3)
        for h in range(2):
            P.add('pe', lambda e, h=h, qt=qt, b3=b3: e.transpose(out=b3[0:72, 512 + h * 128:512 + (h + 1) * 128],
                                                                 in_=qt[:, h, 0:72], identity=ident),
                  [('qkt', qk, 0), ('qkt', qk, 1), ('qkt', qk, 2), ('qkt', qk, 3 + h), 'ident'], [BK(3)])
        acopy(qTa[64:72, :, T * 128:(T + 1) * 128], b3[64:72, 512:768].rearrange("p (h t) -> p h t", h=2),
              [BK(3)], [('qTb', T)])

    def m5(u, j, T):
        for h in range(2):
            ob = 6 + (o_ctr[0] % 2); o_ctr[0] += 1
            chunks = list(range(T + 1))
            groups = [chunks[i:i + 4] for i in range(0, len(chunks), 4)]
            for gi, grp in enumerate(groups):
                sb_ = 4 + (sc_ctr[0] % 2); sc_ctr[0] += 1
                pi = pt_ctr[0] % 4; pt_ctr[0] += 1
                Sb = Bk(sb_)
                for s, kc in enumerate(grp):
                    rd = [('kTa', kc), ('qTa', T), ('qTb', T), ('kind', h)]
                    mm(Sb[:, s * 128:(s + 1) * 128], kTa[:, h, kc * 128:(kc + 1) * 128],
                       qTa[:, h, T * 128:(T + 1) * 128], True, kc != T, rd, [BK(sb_)])
                    if kc == T:
                        mm(Sb[:, s * 128:(s + 1) * 128], ident, maskbias, False, True, ['ident', 'maskbias'], [BK(sb_)])
                n = len(grp)
                act(PTb[pi][:, 0:n * 128], Sb[:, 0:n * 128], AF.Exp, [BK(sb_)], [('PT', pi)], scale=0.125)

                def pv(grp=grp, pi=pi, ob=ob, h=h, T=T, last=(gi == len(groups) - 1), j=j):
                    Ob = Bk(ob)
                    for s, kc in enumerate(grp):
                        mm(Ob[:, 0:65], PTb[pi][:, s * 128:(s + 1) * 128], va[:, kc, h, 0:65], kc == 0, kc == T,
                           [('PT', pi), ('va', kc), ('va1',)], [BK(ob)])
                    if last:
                        ri = rec_ctr[0] % 4; rec_ctr[0] += 1
                        P.add('dve', lambda e: e.reciprocal(out=recb[ri][:, 0:1], in_=Ob[:, 64:65]), [BK(ob)], [('rec', ri)])
                        hs = 2 * j + h
                        ts('dve', mixtok[:, T, hs * 64:(hs + 1) * 64], Ob[:, 0:64], recb[ri][:, 0:1], None, ALU.mult, None,
                           [BK(ob), ('rec', ri)], [('mix', T, hs)])
                pvq.append(pv)
                flush_pv(PV_LAG)

    def r1(u, r, T):
        b = u % 2
        k = T % 2
        pb = Bk(b); pk = BK(b)
        acopy(rv[:, T, :], pb[:, 128:256], [pk, 'ret_ok'], [('rv', T)])
        act(thb[k], pb[:, 256:384], AF.Tanh, [pk], [('th', k)], scale=0.5)
        stt(sg[:, T, :], thb[k], 1.0, pb[:, 256:384], ALU.add, ALU.mult, [('th', k), pk, 'ret_ok'], [('sg', T)])
        slab2 = pb[:, 0:128].rearrange("p (a d) -> p a d", a=2)
        tr_ = tallr[k]
        tt('dve', tr_, slab2.unsqueeze(2).broadcast_to([128, 2, 2, 64]), csR[:, T, :, :, :], ALU.mult, [pk, 'csR'],
           [('tallr', k)])
        tt('pool', qkr[k][:, :, 0:32], tr_[:, :, 0, 0:32], tr_[:, :, 1, 32:64], ALU.subtract, [('tallr', k)], [('qkr', k, 0)])
        tt('pool', qkr[k][:, :, 32:64], tr_[:, :, 0, 32:64], tr_[:, :, 1, 0:32], ALU.add, [('tallr', k)], [('qkr', k, 1)])
        qk_ = [('qkr', k, 0), ('qkr', k, 1)]
        tcopy('pool', rtok[k][:, 0:2, :], qkr[k], qk_, [('rtok', k, 0)])
        ts('pool', rtok[k][:, 2, :], qkr[k][:, 0, :], dec[:, r:r + 1], 1.0, ALU.mult, ALU.mult, qk_ + ['dec'], [('rtok', k, 1)])
        ts('pool', kd[:, T, :], qkr[k][:, 1, :], dec[:, 4 + r:5 + r], 1.0, ALU.mult, ALU.mult, qk_ + ['dec', 'ret_ok'], [('kd', T)])

    def r2(u, r, T):
        k = T % 2
        trb = Bb(2)
        for i in range(3):
            P.add('pe', lambda e, i=i, trb=trb: e.transpose(out=trb[0:64, i * 128:(i + 1) * 128], in_=rtok[k][:, i, :],
                                                            identity=ident),
                  [('rtok', k, 0), ('rtok', k, 1), 'ident'], [BK(2)])
        tcopy('dve', rT[0:64, :, T * 128:(T + 1) * 128], trb[0:64, 0:384].rearrange("p (a t) -> p a t", a=3), [BK(2), 'ret_ok'], [('rT', T)])

    def r3(u, r, T):
        k = T % 2
        sb_ = 4 + (sc_ctr[0] % 2); sc_ctr[0] += 1
        tsl = slice(T * 128, (T + 1) * 128)
        mm(Bk(sb_)[:, 0:128], rT[0:64, 1, tsl], rT[0:64, 0, tsl], True, True, [('rT', T)], [BK(sb_)])
        tt('dve', attnb[k], Bk(sb_)[:, 0:128], maskT[:, r, :], ALU.mult, [BK(sb_), 'maskT'], [('attn', k)])
        if T < NT - 1:
            mm(Bk(3)[0:64, 0:128], kd[:, T, :], rv[:, T, :], True, True, [('kd', T), ('rv', T)], [BK(3)])
            if T == 0:
                tcopy('dve', state_f[0:64, :], Bk(3)[0:64, 0:128], [BK(3)], ['state_f'])
            else:
                stt(state_f[0:64, :], state_f[0:64, :], CD[r], Bk(3)[0:64, 0:128], ALU.mult, ALU.add,
                    [BK(3), 'state_f'], ['state_f'])
            tcopy('pool', state_b[(T + 1) % 2][0:64, :], state_f[0:64, :], ['state_f'], [('state_b', (T + 1) % 2)])

    def r4(u, r, T):
        k = T % 2
        ob = 4 + (sc_ctr[0] % 2); sc_ctr[0] += 1
        tsl = slice(T * 128, (T + 1) * 128)
        mm(Bk(ob)[:, 0:128], attnb[k], rv[:, T, :], True, T == 0, [('attn', k), ('rv', T)], [BK(ob)])
        if T > 0:
            mm(Bk(ob)[:, 0:128], rT[0:64, 2, tsl], state_b[T % 2][0:64, :], False, True,
               [('rT', T), ('state_b', T % 2)], [BK(ob)])
        ss, ssk = newstat()
        act(junk2[0], Bk(ob)[:, 0:128], AF.Square, [BK(ob)], [ssk, 'junk2'], accum_out=ss)
        r4_state[(r, T)] = rstd_from_ss(ss, ssk, 4.0 / 128.0, 4.0 * EPS)
        tcopy('dve', outc[k], Bk(ob)[:, 0:128], [BK(ob)], [('outc', k)])

    def r5(u, r, T):
        k = T % 2
        rr, rk = r4_state[(r, T)]
        stt(mixtok[:, T, 512 + r * 128:512 + (r + 1) * 128], outc[k], rr, sg[:, T, :], ALU.mult, ALU.mult,
            [('outc', k), rk, ('sg', T)], [('mix', T, 8 + r)])

    r4_state = {}
    outc = [V(O_MC + 4352 + i * 512, 512, F32) for i in range(2)]

    MST = [None, m1, m2, m3, m4, m5]
    RST = [None, r1, r2, r3, r4, r5]

    PAD = 2
    mu = []
    for g in range(min(ngroups, 4)):
        mu += [(g, T) for T in range(NT)] + [None] * PAD
    ru = [(g, T) for g in range(4, ngroups) for T in range(NT)]
    units = []
    for i in range(max(len(mu), len(ru))):
        units.append(mu[i] if i < len(mu) else None)
        units.append(ru[i] if i < len(ru) else None)
    NU = len(units)
    DEPTH = 6
    HT_KEYS = [('hT', T) for T in range(NT)]

    def phaseC_weight_loads():
        emit_conv(len(conv_jobs))
        for hf in range(2):
            load_b16(wout_sb[:, :, hf * 512:(hf + 1) * 512], wb_out, 0, 8, hf * 512, 512, [('wout',)] + HT_KEYS,
                     [('cv', 'out', hf)])
        for hf in range(2):
            load_b16(wco_sb[:, :, hf * 512:(hf + 1) * 512], wb_co, 0, 8, hf * 512, 512, [('wco',)] + wgk(2) + wgk(3),
                     [('cv', 'co', hf)])
        for hf in range(2):
            load_b16(wcq_sb[:, :, hf * 512:(hf + 1) * 512], wb_cq, 0, 8, hf * 512, 512, [('wcq',)] + HT_KEYS,
                     [('cv', 'cq', hf)])
        for cg in range(2):
            load_b16(wg[cg][:, :, :], wb_ckv, 0, 8, cg * 512, 512, wgk(cg), [('cv', 'ckv', cg)])

    def run_stage(kst, p):
        if not (0 <= p < NU) or units[p] is None:
            return
        g, T = units[p]
        if g < 4:
            if kst < len(MST):
                MST[kst](p, g, T)
        else:
            if kst < len(RST):
                RST[kst](p, g - 4, T)

    LAST_PROJ = max(p for p in range(NU) if units[p] is not None)
    for step in range(NU + DEPTH):
        if step >= 3 and step % 2 == 1:
            emit_conv(1)
        if step == 16:
            load_group_weights(1)
            load_group_weights(5)
        run_stage(1, step - 1)
        if step < NU and units[step] is not None:
            g, T = units[step]
            if T == 0 and g in (1, 2):
                load_group_weights(g + 1)
            if T == 0 and g in (5, 6) and g + 1 < ngroups:
                load_group_weights(g + 1)
            proj(step, g, T)
        if step == LAST_PROJ + 1:
            phaseC_weight_loads()
        for kst in range(DEPTH - 1, 1, -1):
            run_stage(kst, step - kst)
    flush_pv(0)
    REG['mixtok'] = (V(O_MIX, 32768), BF16, [128, 16384])
    REG['qTa'] = (V(O_MOBA, 8192), BF16, [128, 4096]); REG['kTa'] = (V(O_MOBA + 8192, 8192), BF16, [128, 4096])
    REG['va'] = (V(O_MOBA + 16384, 4224), BF16, [128, 2112])
    if level <= 2:
        P.barrier()
        return finish()

    kvb = [0]
    for cg in range(4):
        slot = cg % 2
        if cg >= 2:
            load_b16(wg[slot][:, :, :], wb_ckv, 0, 8, cg * 512, 512, wgk(slot), [('cv', 'ckv', cg)])
        if cg < 2:
            for fl in range(4):
                fc = cg * 4 + fl
                b = kvb[0] % 2; kvb[0] += 1
                for c in range(8):
                    mm(Bk(b)[:, 0:256], wg[slot][:, c, fl * 128:(fl + 1) * 128], memT[:, c, :], c == 0, c == 7,
                       wgk(slot) + [('memT', 0), ('memT', 1)], [BK(b)])
                acopy(kcT[:, fc, :], Bk(b)[:, 0:256], [BK(b)], [('kcT',)])
        else:
            hv = cg - 2
            for m in range(2):
                b = kvb[0] % 2; kvb[0] += 1
                for c in range(8):
                    mm(Bk(b)[:, 0:512], memT[:, c, m * 128:(m + 1) * 128], wg[slot][:, c, :], c == 0, c == 7,
                       wgk(slot) + [('memT', m)], [BK(b)])
                acopy(vc[:, m, 2 * hv:2 * hv + 2, 0:256], Bk(b)[:, 0:512].rearrange("p (h d) -> p h d", h=2),
                      [BK(b), ('vc1',)], [('vc', m)])

    dma(gain[0], g_d["g_post_mix"].partition_broadcast(128), [], [('gain', 0)])
    dma(gain[1], g_d["g_pre_cross"].partition_broadcast(128), [], [('gain', 1)])
    dma(gain[2], g_d["g_post_cross"].partition_broadcast(128), [], [('gain', 2)])

    P.barrier()
    REG['kcT'] = (V(O_KCT, 4096), BF16, [128, 2048]); REG['vc'] = (V(O_VC, 4128), BF16, [128, 2064])
    if level <= 3:
        return finish()

    co = [O_MOBA]

    def calloc(nbytes, dt=BF16):
        o = co[0]
        co[0] += (nbytes + 63) // 64 * 64
        return V(o, nbytes, dt)

    h2T = V(O_WG, 8192).rearrange("p (c t) -> p c t", c=8)
    qcT = V(O_WG + 8192, 8192).rearrange("p (c t) -> p c t", c=8)
    x1 = calloc(16384, F32).rearrange("p (t f) -> p t f", t=4)
    xl = [calloc(4096, F32) for _ in range(4)]
    tmpf = [calloc(4096, F32) for _ in range(2)]
    xn2 = [calloc(2048) for _ in range(2)]
    junkcb = [calloc(2048), calloc(2048)]
    jcc = [0]

    def njunkc():
        i = jcc[0] % 2; jcc[0] += 1
        return junkcb[i], ('junkc', i)
    mixT = [calloc(2048).rearrange("p (c t) -> p c t", c=8) for _ in range(2)]
    PTc = [V(O_STAGE + 2048 + i * 1024, 1024) for i in range(8)]
    oc_tok = calloc(8192).rearrange("p (t f) -> p t f", t=4)
    ocT = [calloc(2048).rearrange("p (c t) -> p c t", c=8) for _ in range(2)]
    recc = [calloc(64, F32) for _ in range(4)]
    assert co[0] <= ARENA_BYTES, co[0]

    wb_ctr = [0]

    def workbank():
        b = 5 + (wb_ctr[0] % 3); wb_ctr[0] += 1
        return b

    def transposes8(src_fn, rkeys, dst, dkeys, eng='act', bank=4):
        pT = Bb(bank)
        for c in range(8):
            P.add('pe', lambda e, c=c: e.transpose(out=pT[:, c * 128:(c + 1) * 128], in_=src_fn(c), identity=ident),
                  list(rkeys) + ['ident'], [BK(bank)])
        if eng == 'act':
            acopy(dst, pT.rearrange("p (c t) -> p c t", c=8), [BK(bank)], dkeys)
        else:
            tcopy(eng, dst, pT.rearrange("p (c t) -> p c t", c=8), [BK(bank)], dkeys)

    def norm_res_two_banks(b0, b1, gidx, res_ap, res_keys, out_ap, out_keys, ti, var_mult, var_add):
        assert b1 == b0 + 1 and b0 % 2 == 0
        keys = [BK(b0), BK(b1)]
        acc = Bk2(b0 // 2)
        ss, ssk = newstat()
        jb, jk = njunkc()
        act(jb, acc, AF.Square, keys, [ssk, jk, (jk, 1)], accum_out=ss)
        r, rk = rstd_from_ss(ss, ssk, var_mult, var_add)
        tk = ('tmpf', ti)
        stt(tmpf[ti], acc, r, gain[gidx], ALU.mult, ALU.mult, keys + [rk, ('gain', gidx)], [tk])
        tt('dve', out_ap, tmpf[ti], res_ap, ALU.add, [tk] + list(res_keys), out_keys)

    wgu_pre = V(0, 90112).rearrange("p (c n) -> p c n", c=8)
    wb_gu3 = wb_gu.rearrange("(c p) n -> p c n", p=128)
    C1P = [(0, 1), (2, 3), (6, 7), (0, 1)]
    C3P = [(2, 3), (6, 7), (0, 1), (2, 3)]
    for s4 in range(4):
        def c1_a(Tl):
            T = s4 * 4 + Tl
            k = Tl % 2
            dma(xl[Tl], x_d[T * 128:(T + 1) * 128, :], [], [('xl', Tl)])
            transposes8(lambda c, T=T: mixtok[:, T, c * 128:(c + 1) * 128], [('mix', T, hs) for hs in range(12)],
                        mixT[k], [('mixT', k)], bank=4 + Tl % 2)

        def c1_b(Tl):
            k = Tl % 2
            bp = C1P[Tl]
            for half in range(2):
                for c in range(8):
                    mm(Bk(bp[half]), mixT[k][:, c, :], wout_sb[:, c, half * 512:(half + 1) * 512], c == 0, c == 7,
                       [('mixT', k), ('wout',)], [BK(bp[half])])

        def c1_ab(Tl):
            if Tl == 0:
                c1_a(0)
            if Tl + 1 < 4:
                c1_a(Tl + 1)
            c1_b(Tl)

        c1s = {}

        def c1_cA(Tl):
            bp = C1P[Tl]
            ss, ssk = newstat()
            jb, jk = njunkc()
            act(jb, Bk2(bp[0] // 2), AF.Square, [BK(bp[0]), BK(bp[1])], [ssk, jk, (jk, 1)], accum_out=ss)
            c1s[('r1', Tl)] = rstd_from_ss(ss, ssk, 1.0 / D, EPS)

        def c1_cB(Tl):
            k = Tl % 2
            bp = C1P[Tl]
            r, rk = c1s[('r1', Tl)]
            tk = ('tmpf', k)
            stt(tmpf[k], Bk2(bp[0] // 2), r, gain[0], ALU.mult, ALU.mult, [BK(bp[0]), BK(bp[1]), rk, ('gain', 0)], [tk])
            tt('dve', x1[:, Tl, :], tmpf[k], xl[Tl], ALU.add, [tk, ('xl', Tl)], [('x1', Tl)])

        def c1_cC(Tl):
            ss, ssk = newstat()
            jb, jk = njunkc()
            act(jb, x1[:, Tl, :], AF.Square, [('x1', Tl)], [ssk, jk, (jk, 1)], accum_out=ss)
            c1s[('r2', Tl)] = rstd_from_ss(ss, ssk, 1.0 / D, EPS)

        def c1_cD(Tl):
            k = Tl % 2
            r, rk = c1s[('r2', Tl)]
            stt(xn2[k], x1[:, Tl, :], r, gain[1], ALU.mult, ALU.mult, [('x1', Tl), rk, ('gain', 1)], [('xn2', k)])

        def c1_e(Tl):
            k = Tl % 2
            transposes8(lambda c, k=k: xn2[k][:, c * 128:(c + 1) * 128], [('xn2', k)],
                        h2T[:, :, Tl * 128:(Tl + 1) * 128], [('h2T', Tl)], bank=4 + Tl % 2)

        c1_ab(0); c1_ab(1); c1_cA(0)
        c1_ab(2); c1_cA(1); c1_cB(0)
        c1_ab(3); c1_cA(2); c1_cB(1); c1_cC(0)
        c1_cA(3); c1_cB(2); c1_cC(1); c1_cD(0)
        c1_cB(3); c1_cC(2); c1_cD(1); c1_e(0)
        c1_cC(3); c1_cD(2); c1_e(1)
        c1_cD(3); c1_e(2)
        c1_e(3)
        if s4 == 3:
            for c in range(4):
                dma(wgu_pre[:, c:c + 1, :], wb_gu3[:, c:c + 1, :], [],
                    [('wgu_pre', c), ('wout',)] + [('mix', T, hs) for T in range(NT) for hs in range(12)])
        H2K = [('h2T', t) for t in range(4)]
        for fc in range(8):
            b = workbank()
            for c in range(8):
                mm(Bk(b), wcq_sb[:, c, fc * 128:(fc + 1) * 128], h2T[:, c, :], c == 0, c == 7, H2K + [('wcq',)], [BK(b)])
            acopy(qcT[:, fc, :], Bk(b), [BK(b)], [('qcT', fc)])
        if s4 == 3:
            dma(wgu_pre[:, 4:5, :], wb_gu3[:, 4:5, :], [], [('wgu_pre', 4), ('wout',), ('wcq',)])
        for h in range(4):
            for m in range(2):
                b = workbank()
                for jj in range(2):
                    mm(Bk(b), kcT[:, 2 * h + jj, m * 128:(m + 1) * 128], qcT[:, 2 * h + jj, :], jj == 0, jj == 1,
                       [('kcT',), ('qcT', 2 * h + jj)], [BK(b)])
                pi = 2 * h + m
                act(PTc[pi], Bk(b), AF.Exp, [BK(b)], [('PTc', pi)], scale=1.0 / 16.0)
        for h in range(4):
            for Tl in range(4):
                b = workbank()
                for m in range(2):
                    mm(Bk(b)[:, 0:257], PTc[2 * h + m][:, Tl * 128:(Tl + 1) * 128], vc[:, m, h, 0:257], m == 0, m == 1,
                       [('PTc', 2 * h + m), ('vc', m), ('vc1',)], [BK(b)])
                ri = (h * 4 + Tl) % 4
                P.add('dve', lambda e, ri=ri, b=b: e.reciprocal(out=recc[ri][:, 0:1], in_=Bk(b)[:, 256:257]), [BK(b)],
                      [('recc', ri)])
                ts('dve', oc_tok[:, Tl, h * 256:(h + 1) * 256], Bk(b)[:, 0:256], recc[ri][:, 0:1], None, ALU.mult, None,
                   [BK(b), ('recc', ri)], [('oc', Tl, h)])
        c3s = {}

        def c3_a(Tl):
            k = Tl % 2
            transposes8(lambda c, Tl=Tl: oc_tok[:, Tl, c * 128:(c + 1) * 128], [('oc', Tl, h) for h in range(4)],
                        ocT[k], [('ocT', k)], bank=4 + Tl % 2)

        def c3_b(Tl):
            k = Tl % 2
            bp = C3P[Tl]
            for half in range(2):
                for c in range(8):
                    mm(Bk(bp[half]), ocT[k][:, c, :], wco_sb[:, c, half * 512:(half + 1) * 512], c == 0, c == 7,
                       [('ocT', k), ('wco',)], [BK(bp[half])])

        def c3_ab(Tl):
            if Tl == 0:
                c3_a(0)
            if Tl + 1 < 4:
                c3_a(Tl + 1)
            c3_b(Tl)

        def c3_cA(Tl):
            bp = C3P[Tl]
            ss, ssk = newstat()
            jb, jk = njunkc()
            act(jb, Bk2(bp[0] // 2), AF.Square, [BK(bp[0]), BK(bp[1])], [ssk, jk, (jk, 1)], accum_out=ss)
            c3s[Tl] = rstd_from_ss(ss, ssk, 1.0 / D, EPS)

        def c3_cB(Tl):
            T = s4 * 4 + Tl
            k = Tl % 2
            bp = C3P[Tl]
            r, rk = c3s[Tl]
            tk = ('tmpf', k)
            stt(tmpf[k], Bk2(bp[0] // 2), r, gain[2], ALU.mult, ALU.mult, [BK(bp[0]), BK(bp[1]), rk, ('gain', 2)], [tk])
            tt('dve', xl[Tl], tmpf[k], x1[:, Tl, :], ALU.add, [tk, ('x1', Tl)], [('xl', Tl)])
            dma(x2s[T * 128:(T + 1) * 128, :], xl[Tl], [('xl', Tl)], [('x2s', T)])

        c3_ab(0); c3_ab(1); c3_cA(0)
        c3_ab(2); c3_cA(1); c3_cB(0)
        c3_ab(3); c3_cA(2); c3_cB(1)
        c3_cA(3); c3_cB(2)
        c3_cB(3)

    P.barrier()
    if level <= 4:
        return finish()

    wgu_sb = V(0, 90112).rearrange("p (c n) -> p c n", c=8)
    wdn_sb = V(139648, 45056).rearrange("p (c n) -> p c n", c=NFC)
    actT = V(O_WG, 11264).rearrange("p (c t) -> p c t", c=NFC)
    h3Tb = [V(O_WG + 11264, 4096).rearrange("p (c t) -> p c t", c=8),
            V(O_STAGE + 12288, 4096).rearrange("p (c t) -> p c t", c=8)]
    do = [184704]

    def dalloc(nbytes, dt=BF16):
        o = do[0]
        do[0] += (nbytes + 63) // 64 * 64
        return V(o, nbytes, dt)

    x2l = [dalloc(4096, F32) for _ in range(4)]
    xn3 = [dalloc(2048) for _ in range(2)]
    sgate = [dalloc(1024, F32) for _ in range(3)]
    assert do[0] <= ARENA_BYTES, do[0]
    junkdb = [V(90112, 2048), V(92160, 2048)]
    jdc = [0]

    def njunkd():
        i = jdc[0] % 2; jdc[0] += 1
        return junkdb[i], ('junkd', i)
    obuf = [V(O_STAGE + i * 4096, 4096, F32) for i in range(2)]
    tmpD = V(O_STAGE + 8192, 4096, F32)

    dma(gain[0], g_d["g_pre_ffn"].partition_broadcast(128), [], [('gain', 0)])
    dma(gain[1], g_d["g_post_ffn"].partition_broadcast(128), [], [('gain', 1)])
    STAGE_KEYS = [('stage', s, k) for s in range(2) for k in range(4)]

    first_stage_overlay = [True]
    first_h3_overlay = [True]

    d1_state = {}

    def d1a(u8, Tl):
        T = u8 * 2 + Tl
        xi = (u8 % 2) * 2 + Tl
        dma(x2l[xi], x2s[T * 128:(T + 1) * 128, :], [('x2s', T)], [('x2l', xi)])
        ss, ssk = newstat()
        jb, jk = njunkd()
        act(jb, x2l[xi], AF.Square, [('x2l', xi)], [ssk, (jk, 0), (jk, 1)], accum_out=ss)
        d1_state[(u8, Tl)] = rstd_from_ss(ss, ssk, 1.0 / D, EPS)

    def d1b(u8, Tl):
        xi = (u8 % 2) * 2 + Tl
        r, rk = d1_state[(u8, Tl)]
        stt(xn3[Tl], x2l[xi], r, gain[0], ALU.mult, ALU.mult, [('x2l', xi), rk, ('gain', 0)], [('xn3', Tl)])

    def d1c(u8, Tl):
        h3T = h3Tb[u8 % 2]
        extra = []
        if u8 % 2 == 1 and first_h3_overlay[0]:
            extra = STAGE_KEYS
            first_h3_overlay[0] = False
        transposes8(lambda c, k=Tl: xn3[k][:, c * 128:(c + 1) * 128], [('xn3', Tl)],
                    h3T[:, :, Tl * 128:(Tl + 1) * 128], [('h3T', u8 % 2, Tl)] + extra, eng='dve')

    def d1(u8, tls=(0, 1)):
        for Tl in tls:
            d1a(u8, Tl); d1b(u8, Tl); d1c(u8, Tl)

    D1_SCHED = {2: [(d1a, 0)], 4: [(d1b, 0)], 6: [(d1c, 0)], 8: [(d1a, 1)], 10: [(d1b, 1)], 12: [(d1c, 1)]}

    d4_state = {}
    d4_pending = {}

    def d4a(u8):
        for Tl in range(2):
            jb, jk = njunkd()
            ss, ssk = newstat()
            act(jb, Bk2(Tl), AF.Square, [BK(2 * Tl), BK(2 * Tl + 1)], [ssk, (jk, 0), (jk, 1)], accum_out=ss)
            d4_state[(u8, Tl)] = rstd_from_ss(ss, ssk, 1.0 / D, EPS)

    def d4b(u8):
        extra = STAGE_KEYS if first_stage_overlay[0] else []
        first_stage_overlay[0] = False
        for Tl in range(2):
            r, rk = d4_state[(u8, Tl)]
            stt(obuf[Tl], Bk2(Tl), r, gain[1], ALU.mult, ALU.mult,
                [BK(2 * Tl), BK(2 * Tl + 1), rk, ('gain', 1)], [('obuf', Tl, 0), ('obuf', Tl, 1)] + extra)

    def d4c(u8):
        for Tl in range(2):
            T = u8 * 2 + Tl
            xi = (u8 % 2) * 2 + Tl
            tt('dve', obuf[Tl], obuf[Tl], x2l[xi], ALU.add, [('obuf', Tl, 0), ('obuf', Tl, 1), ('x2l', xi)],
               [('obuf', Tl, 0), ('obuf', Tl, 1)])
            dma(out_d[T * 128:(T + 1) * 128, :], obuf[Tl], [('obuf', Tl, 0), ('obuf', Tl, 1)], [('out', T)])

    DLAG = 3
    d1(0)
    for gq in range(6):
        nfc = 4 if gq < 5 else 2
        ncol = nfc * 128
        for (kind, base) in (('g', 0), ('u', DFF)):
            c0 = base + gq * 512
            load_b16(wgu_sb[:, 5:8, c0:c0 + ncol], wb_gu, 5, 3, c0, ncol, [('wgu', kind, gq)], [])
        load_b16(wdn_sb[:, gq * 4:gq * 4 + nfc, :], wb_dn, gq * 4, nfc, 0, 1024,
                 [('wdn', rb) for rb in range(gq * 2, gq * 2 + nfc // 2)], [])
    for u8 in range(8):
        h3T = h3Tb[u8 % 2]
        H3K = [('h3T', u8 % 2, 0), ('h3T', u8 % 2, 1)]

        def down(fc):
            for Tl in range(2):
                for half in range(2):
                    b = Tl * 2 + half
                    mm(Bk(b), actT[:, fc, Tl * 128:(Tl + 1) * 128], wdn_sb[:, fc, half * 512:(half + 1) * 512],
                       fc == 0, fc == NFC - 1, [('actT', fc), ('wdn', fc // 2)], [BK(b)])

        for fc in range(NFC):
            b = 5 + (fc % 3)
            for c in range(8):
                mm(Bk(b)[:, 0:256], wgu_sb[:, c, fc * 128:(fc + 1) * 128], h3T[:, c, :], c == 0, c == 7,
                   H3K + [('wgu', 'g', fc // 4)], [BK(b)])
            for c in range(8):
                mm(Bk(b)[:, 256:512], wgu_sb[:, c, DFF + fc * 128:DFF + (fc + 1) * 128], h3T[:, c, :], c == 0, c == 7,
                   H3K + [('wgu', 'u', fc // 4)], [BK(b)])
            k = fc % 3
            act(sgate[k], Bk(b)[:, 0:256], AF.Silu, [BK(b)], [('sgate', k)])
            tt('dve', actT[:, fc, :], sgate[k], Bk(b)[:, 256:512], ALU.mult, [('sgate', k), BK(b)], [('actT', fc)])
            if fc >= DLAG:
                down(fc - DLAG)
            if fc in d4_pending:
                fn_, u_ = d4_pending.pop(fc)
                fn_(u_)
            if u8 + 1 < 8:
                for (fn, Tl) in D1_SCHED.get(fc, ()):
                    fn(u8 + 1, Tl)
        for fc in range(NFC - DLAG, NFC):
            down(fc)
        d4a(u8)
        if u8 == 7:
            d4b(u8); d4c(u8)
        else:
            d4_pending[1] = (d4b, u8)
            d4_pending[2] = (d4c, u8)

    return finish()


_NC_CACHE = {}


def kernel(x, mem, g_pre_mix, w_in, w_out, g_post_mix, g_pre_cross, g_mem, w_cq, w_ckv, w_co,
           g_post_cross, g_pre_ffn, w_gate_up, w_down, g_post_ffn):
    f32 = lambda a: np.ascontiguousarray(np.asarray(a, dtype=np.float32))
    x = f32(x); mem = f32(mem)
    B = x.shape[0]
    if 'nc' not in _NC_CACHE:
        _NC_CACHE['nc'] = build_program()
    nc = _NC_CACHE['nc']
    shared = {
        "g_pre_mix": f32(g_pre_mix).reshape(1, D), "g_post_mix": f32(g_post_mix).reshape(1, D),
        "g_pre_cross": f32(g_pre_cross).reshape(1, D), "g_mem": f32(g_mem).reshape(1, D),
        "g_post_cross": f32(g_post_cross).reshape(1, D), "g_pre_ffn": f32(g_pre_ffn).reshape(1, D),
        "g_post_ffn": f32(g_post_ffn).reshape(1, D),
        "w_in": f32(w_in).reshape(D, 3072), "w_out": f32(w_out).reshape(D, D), "w_cq": f32(w_cq).reshape(D, D),
        "w_ckv": f32(w_ckv).reshape(D, 2 * D), "w_co": f32(w_co).reshape(D, D),
        "w_gate_up": f32(w_gate_up).reshape(D, 2 * DFF), "w_down": f32(w_down).reshape(DFF, D),
        "c_ident": _CONST['ident'], "c_maskbias": _CONST['maskbias'], "c_kind": _CONST['kind'],
        "c_csM": _CONST['csM'], "c_csR": _CONST['csR'],
        "c_dec": _CONST['dec'], "c_maskT": _CONST['maskT'],
    }
    in_maps = []
    for b in range(B):
        m = dict(shared)
        m["x"] = x[b]
        m["mem"] = mem[b]
        in_maps.append(m)
    res = run_bass_kernel_spmd(nc, in_maps, core_ids=list(range(B)))
    return np.stack([np.asarray(r["out"], dtype=np.float32) for r in res.results], axis=0)
```

```python
import math
import os
from collections import deque
from contextlib import ExitStack

import numpy as np
import ml_dtypes

import concourse.bass as bass
import concourse.mybir as mybir
from concourse.bass_utils import run_bass_kernel_spmd

F32 = mybir.dt.float32
BF16 = mybir.dt.bfloat16
AF = mybir.ActivationFunctionType
ALU = mybir.AluOpType
AX = mybir.AxisListType
NPBF = ml_dtypes.bfloat16

S = 2048
D = 1024
NT = S // 128
NMEM = 256
DFF = 2816
NFC = DFF // 128
EPS = 1e-6
BIGK = 30000.0
ARENA_BYTES = 210400

ENGS = ['pe', 'act', 'dve', 'pool', 'sp']
NDS = 8


class Ins:
    __slots__ = ('eng', 'fn', 'deps', 'marked', 'sem', 'semval', 'isdma', 'idx')


class Prog:
    def __init__(self):
        self.streams = {e: [] for e in ENGS}
        self.lastw = {}
        self.readers = {}
        self.ndma = {e: 0 for e in ENGS}
        self.dma_hist = {e: [] for e in ENGS}
        self.bank_last = {}

    def add(self, eng, fn, reads=(), writes=(), dma=False):
        ins = Ins()
        ins.eng = eng; ins.fn = fn; ins.isdma = dma; ins.marked = False
        ins.sem = None; ins.semval = None
        deps = []
        for k in reads:
            w = self.lastw.get(k)
            if w is not None:
                deps.append(w)
        for k in writes:
            w = self.lastw.get(k)
            if w is not None:
                deps.append(w)
            deps.extend(self.readers.get(k, ()))
        if dma:
            i = self.ndma[eng]
            ins.sem = ('dma', eng, i % NDS)
            ins.semval = 16 * (i // NDS + 1)
            if i >= NDS:
                deps.append(self.dma_hist[eng][i - NDS])
            self.ndma[eng] += 1
            self.dma_hist[eng].append(ins)
        for k in list(reads) + list(writes):
            if isinstance(k, tuple) and k and k[0] == 'B':
                la = self.bank_last.setdefault(k, {})
                for e2, d in la.items():
                    if e2 != eng:
                        deps.append(d)
        seen = set(); best = {}; dd = []
        for d in deps:
            if id(d) in seen or d is ins:
                continue
            seen.add(id(d))
            if d.isdma:
                dd.append(d)
                continue
            if (not dma) and d.eng == 'pe' and eng == 'pe':
                continue
            b = best.get(d.eng)
            if b is None or d.idx > b.idx:
                best[d.eng] = d
        dd.extend(best.values())
        for d in dd:
            d.marked = True
        ins.deps = dd
        for k in reads:
            self.readers.setdefault(k, []).append(ins)
        for k in writes:
            self.lastw[k] = ins
            self.readers[k] = []
        ins.idx = len(self.streams[eng])
        self.streams[eng].append(ins)
        for k in list(reads) + list(writes):
            if isinstance(k, tuple) and k and k[0] == 'B':
                self.bank_last.setdefault(k, {})[eng] = ins
        return ins

    def barrier(self):
        lasts = []
        for e in ENGS:
            for ins in reversed(self.streams[e]):
                if not ins.isdma:
                    lasts.append(ins)
                    break
        pend = []
        for e in ENGS:
            pend.extend(self.dma_hist[e][-NDS:])
        for e in ENGS:
            n = self.add(e, lambda eng: eng.nop())
            for d in lasts + pend:
                if d is n or d in n.deps:
                    continue
                if d.eng == 'pe' and e == 'pe' and not d.isdma:
                    continue
                n.deps.append(d)
                d.marked = True
        self.lastw = {}
        self.readers = {}
        self.bank_last = {}

    def finalize(self):
        for e in ENGS:
            c = 0
            for ins in self.streams[e]:
                if ins.isdma:
                    continue
                if ins.marked:
                    c += 1
                    ins.sem = ('eng', e); ins.semval = c
            assert c < 60000, (e, c)

    def sem_keys(self):
        ks = [('eng', e) for e in ENGS]
        for e in ENGS:
            for j in range(min(NDS, self.ndma[e])):
                ks.append(('dma', e, j))
        return ks

    def emit(self, nc, sems):
        engmap = {'pe': 'tensor', 'act': 'scalar', 'dve': 'vector', 'pool': 'gpsimd', 'sp': 'sync'}
        with nc.Block() as block:
            for e in ENGS:
                stream = self.streams[e]

                def body(eng, stream=stream):
                    known = {}
                    for ins in stream:
                        need = {}
                        for d in ins.deps:
                            if d.semval > need.get(d.sem, 0):
                                need[d.sem] = d.semval
                        for sk, sv in need.items():
                            if known.get(sk, 0) >= sv:
                                continue
                            eng.wait_ge(sems[sk], sv)
                            known[sk] = sv
                        bi = ins.fn(eng)
                        if ins.isdma:
                            bi.then_inc(sems[ins.sem], 16)
                        elif ins.marked:
                            bi.then_inc(sems[ins.sem], 1)
                getattr(block, engmap[e])(body)


def host_constants():
    c = {}
    c['ident'] = np.eye(128, dtype=np.float32).astype(NPBF)
    kk = np.arange(128)[:, None]; qq = np.arange(128)[None, :]
    c['maskbias'] = np.where(kk <= qq, 0.0, -BIGK).astype(np.float32).astype(NPBF)
    kind = np.zeros((8, S), np.float32)
    for n in range(8):
        kind[n, n * 256:(n + 1) * 256] = BIGK
    c['kind'] = kind.astype(NPBF)
    pos = (np.arange(NT)[None, :] * 128 + np.arange(128)[:, None]).astype(np.float64)
    moba_inv = np.power(np.float64(500000.0), -np.arange(8, dtype=np.float64) * 2.0 / 16.0)
    ang = pos[:, :, None] * moba_inv[None, None, :]
    cm = np.cos(ang); sm = np.sin(ang)
    c['csM'] = np.stack([np.concatenate([cm, cm], -1), np.concatenate([sm, sm], -1)], axis=2).astype(np.float32)\
        .reshape(128, NT * 32)
    ret_inv = 1.0 / np.power(np.float64(10000.0), np.linspace(0.0, 1.0, 32, dtype=np.float64))
    angr = pos[:, :, None] * ret_inv[None, None, :]
    cr = np.cos(angr); sr = np.sin(angr)
    ksc = 64.0 ** -0.5
    cr2 = np.concatenate([cr, cr], -1); sr2 = np.concatenate([sr, sr], -1)
    c['csR'] = np.stack([np.stack([cr2, sr2], axis=2), np.stack([cr2 * ksc, sr2 * ksc], axis=2)], axis=2)\
        .astype(np.float32).reshape(128, NT * 256)
    hh = np.arange(4, dtype=np.float64)
    log_g = np.log(1.0 - np.power(2.0, -5.0 - hh))
    idx = np.arange(128, dtype=np.float64)
    qdec = np.exp(log_g[None, :] * (idx[:, None] + 1.0))
    kdec = np.exp(log_g[None, :] * (127.0 - idx[:, None]))
    c['dec'] = np.concatenate([qdec, kdec], axis=1).astype(np.float32)
    diff = idx[None, :] - idx[:, None]
    mt = np.where(diff[:, None, :] >= 0, np.exp(log_g[None, :, None] * np.maximum(diff[:, None, :], 0.0)), 0.0)
    c['maskT'] = mt.astype(np.float32).reshape(128, 4 * 128)
    c['cd'] = [float(np.exp(log_g[h] * 128.0)) for h in range(4)]
    return c


_CONST = host_constants()


def build_program(level=5, ngroups=8, dumps=()):
    nc = bass.Bass("TRN2", target_bir_lowering=False)
    P = Prog()
    REG = {}

    def din(name, shape, dt=F32):
        return nc.dram_tensor(name, list(shape), dt, kind="ExternalInput").ap()

    x_d = din("x", [S, D]); mem_d = din("mem", [NMEM, D])
    g_d = {n: din(n, [1, D]) for n in ["g_pre_mix", "g_post_mix", "g_pre_cross", "g_mem", "g_post_cross",
                                       "g_pre_ffn", "g_post_ffn"]}
    w_in = din("w_in", [D, 3072]); w_out = din("w_out", [D, D]); w_cq = din("w_cq", [D, D])
    w_ckv = din("w_ckv", [D, 2 * D]); w_co = din("w_co", [D, D])
    w_gu = din("w_gate_up", [D, 2 * DFF]); w_down = din("w_down", [DFF, D])
    c_ident = din("c_ident", [128, 128], BF16); c_maskbias = din("c_maskbias", [128, 128], BF16)
    c_kind = din("c_kind", [8, S], BF16)
    c_csM = din("c_csM", [128, NT * 32]); c_csR = din("c_csR", [128, NT * 256])
    c_dec = din("c_dec", [128, 8]); c_maskT = din("c_maskT", [128, 512])
    out_d = nc.dram_tensor("out", [S, D], F32, kind="ExternalOutput").ap()
    x2s = nc.dram_tensor("x2s", [S, D], F32, kind="Internal").ap()
    wb_in = nc.dram_tensor("wb_in", [D, 3072], BF16, kind="Internal").ap()
    wb_out = nc.dram_tensor("wb_out", [D, D], BF16, kind="Internal").ap()
    wb_cq = nc.dram_tensor("wb_cq", [D, D], BF16, kind="Internal").ap()
    wb_co = nc.dram_tensor("wb_co", [D, D], BF16, kind="Internal").ap()
    wb_ckv = nc.dram_tensor("wb_ckv", [D, 2 * D], BF16, kind="Internal").ap()
    wb_gu = nc.dram_tensor("wb_gu", [D, 2 * DFF], BF16, kind="Internal").ap()
    wb_dn = nc.dram_tensor("wb_dn", [DFF, D], BF16, kind="Internal").ap()
    CD = _CONST['cd']

    es = ExitStack()
    arena = es.enter_context(nc.sbuf_tensor("arena", [128, ARENA_BYTES // 2], BF16))
    stats = es.enter_context(nc.sbuf_tensor("stats", [128, 512], F32))
    bpairs = [es.enter_context(nc.psum_tensor(f"bpair{i}", [128, 1024], F32)) for i in range(4)]

    def V(off, nbytes, dt=BF16):
        assert off % 4 == 0 and off + nbytes <= ARENA_BYTES, (off, nbytes)
        a = arena[:, off // 2:(off + nbytes) // 2]
        if dt == F32:
            a = a.bitcast(F32)
        return a

    def Bk(i):
        return bpairs[i // 2][:, (i % 2) * 512:(i % 2 + 1) * 512]

    def Bb(i):
        return bpairs[i // 2][:, (i % 2) * 512:(i % 2 + 1) * 512].bitcast(BF16)

    def Bk2(p):
        return bpairs[p][:]

    BK = lambda i: ('B', i)

    def dma(out, in_, reads, writes):
        P.add('sp', lambda e: e.dma_start(out=out, in_=in_), reads, writes, dma=True)

    def mm(out, lhsT, rhs, start, stop, reads, writes):
        P.add('pe', lambda e: e.matmul(out, lhsT=lhsT, rhs=rhs, start=start, stop=stop), reads, writes)

    def act(out, in_, func, reads, writes, scale=1.0, accum_out=None):
        if accum_out is None:
            P.add('act', lambda e: e.activation(out=out, in_=in_, func=func, scale=scale), reads, writes)
        else:
            P.add('act', lambda e: e.activation(out=out, in_=in_, func=func, scale=scale, accum_out=accum_out),
                  reads, writes)

    def acopy(out, in_, reads, writes):
        P.add('act', lambda e: e.copy(out=out, in_=in_), reads, writes)

    def tt(eng, out, in0, in1, op, reads, writes):
        P.add(eng, lambda e: e.tensor_tensor(out=out, in0=in0, in1=in1, op=op), reads, writes)

    def ts(eng, out, in0, s1, s2, op0, op1, reads, writes):
        if op1 is None:
            P.add(eng, lambda e: e.tensor_scalar(out=out, in0=in0, scalar1=s1, scalar2=None, op0=op0), reads, writes)
        else:
            P.add(eng, lambda e: e.tensor_scalar(out=out, in0=in0, scalar1=s1, scalar2=s2, op0=op0, op1=op1),
                  reads, writes)

    def stt(out, in0, scalar, in1, op0, op1, reads, writes):
        P.add('dve', lambda e: e.scalar_tensor_tensor(out=out, in0=in0, scalar=scalar, in1=in1, op0=op0, op1=op1),
              reads, writes)

    def tcopy(eng, out, in_, reads, writes):
        P.add(eng, lambda e: e.tensor_copy(out=out, in_=in_), reads, writes)

    def memset(eng, ap, val, writes):
        P.add(eng, lambda e: e.memset(ap, val), (), writes)

    def finish():
        outk = []
        for name in dumps:
            ap, dt, shape = REG[name]
            d = nc.dram_tensor("dbg_" + name, list(shape), dt, kind="ExternalOutput").ap()
            dma(d, ap, [], [('dbg', name)])
            outk.append(('dbg', name))
        P.add('sp', lambda e: e.nop(), reads=[('out', T) for T in range(NT)] + outk)
        P.finalize()
        sems = {kk: es.enter_context(nc.semaphore("s_" + "_".join(map(str, kk)))) for kk in P.sem_keys()}
        P.emit(nc, sems)
        es.close()
        return nc

    stat_ctr = [0]

    def newstat():
        i = stat_ctr[0] % 508
        stat_ctr[0] += 1
        return stats[:, i:i + 1], ('st', stat_ctr[0])

    memset('pool', stats[:, 508:509], -0.5, ['mhalf'])
    MHALF = stats[:, 508:509]

    def rstd_from_ss(ss_ap, ss_key, mult, add):
        v, vk = newstat()
        ts('dve', v, ss_ap, mult, add, ALU.mult, ALU.add, [ss_key], [vk])
        r, rk = newstat()
        tt('pool', r, v, MHALF, ALU.pow, [vk, 'mhalf'], [rk])
        return r, rk

    O_MIX = 0
    O_HT = 32768
    O_WCO = 65536
    O_MEMT = 81920
    O_KCT = O_MEMT + 4096
    O_VC = O_KCT + 4096
    O_IDENT = 94336
    O_GAIN = 94592
    O_WG = 106880
    O_STAGE = 123264
    O_MOBA = 139648
    O_RET = 160256
    O_MC = 182784
    O_TMP = 194304

    mixtok = V(O_MIX, 32768).rearrange("p (t f) -> p t f", t=NT)
    hT = V(O_HT, 32768).rearrange("p (c t) -> p c t", c=8)
    wout_sb = V(O_HT, 16384).rearrange("p (c n) -> p c n", c=8)
    wcq_sb = V(O_HT + 16384, 16384).rearrange("p (c n) -> p c n", c=8)
    wco_sb = V(O_WCO, 16384).rearrange("p (c n) -> p c n", c=8)
    memT = V(O_MEMT, 4096).rearrange("p (c t) -> p c t", c=8)
    kcT = V(O_KCT, 4096).rearrange("p (c t) -> p c t", c=8)
    vc = V(O_VC, 4128).rearrange("p (m h d) -> p m h d", m=2, h=4)
    ident = V(O_IDENT, 256)
    gain = [V(O_GAIN + i * 4096, 4096, F32) for i in range(3)]
    wg = [V(O_WG + i * 8192, 8192).rearrange("p (c n) -> p c n", c=8) for i in range(2)]
    wg += [V(O_WCO + i * 8192, 8192).rearrange("p (c n) -> p c n", c=8) for i in range(2)]
    wslot = lambda g: (g % 2) if g < 4 else 2 + ((g - 4) % 2)
    stage = [V(O_STAGE + i * 8192, 8192, F32) for i in range(2)]

    stage_ctr = [0]
    cast_ctr = [0]
    CAST_SEQ = [['pool']]

    def load_piece(dst, srcs, shape, wkeys, extra_reads=()):
        s = stage_ctr[0] % 2
        stage_ctr[0] += 1
        n = int(np.prod(shape[1:]))
        assert n <= 2048
        for (sub, src_ap, seg) in srcs:
            dma(sub(stage[s]), src_ap, [], [('stage', s, seg)])
        stv = stage[s][:, 0:n]
        if len(shape) == 3:
            stv = stv.rearrange("p (a b) -> p a b", a=shape[1])
        elif len(shape) == 4:
            stv = stv.rearrange("p (a b c) -> p a b c", a=shape[1], b=shape[2])
        ce = CAST_SEQ[0][cast_ctr[0] % len(CAST_SEQ[0])]; cast_ctr[0] += 1
        if ce == 'act':
            acopy(dst, stv, [('stage', s, k) for k in range(4)] + list(extra_reads), wkeys)
        else:
            tcopy(ce, dst, stv, [('stage', s, k) for k in range(4)] + list(extra_reads), wkeys)

    def load_plain(dst3, w_ap, r0, nr, c0, ncol, wkeys, extra_reads=()):
        src = w_ap.rearrange("(c p) n -> p c n", p=128)[:, r0:r0 + nr, c0:c0 + ncol]
        load_piece(dst3, [(lambda st: st[:, 0:nr * ncol].rearrange("p (a b) -> p a b", a=nr), src, 0)],
                   [128, nr, ncol], wkeys, extra_reads)

    conv_jobs = []
    for (nm, src, dstb, ncols) in (("ckv", w_ckv, wb_ckv, 2 * D), ("co", w_co, wb_co, D), ("out", w_out, wb_out, D),
                                   ("cq", w_cq, wb_cq, D), ("gu", w_gu, wb_gu, 2 * DFF)):
        for c0 in range(0, ncols, 512):
            conv_jobs.append((nm, c0 // 512, dstb[:, c0:c0 + 512], src[:, c0:c0 + 512]))
    for r0 in range(0, DFF, 512):
        r1 = min(r0 + 512, DFF)
        conv_jobs.append(("dn", r0 // 512, wb_dn[r0:r1, :], w_down[r0:r1, :]))
    in_jobs = [("in", c0 // 512, wb_in[:, c0:c0 + 512], w_in[:, c0:c0 + 512]) for c0 in range(0, 3072, 512)]
    conv_jobs = in_jobs + conv_jobs
    conv_pos = [0]

    def emit_conv(n=1):
        for _ in range(n):
            if conv_pos[0] >= len(conv_jobs):
                return
            nm, i, o_ap, i_ap = conv_jobs[conv_pos[0]]
            conv_pos[0] += 1
            P.add('pool', lambda e, o_ap=o_ap, i_ap=i_ap: e.dma_start(out=o_ap, in_=i_ap), [], [('cv', nm, i)], dma=True)

    def load_b16(dst3, wb_ap, r0, nr, c0, ncol, wkeys, cvkeys):
        src = wb_ap.rearrange("(c p) n -> p c n", p=128)[:, r0:r0 + nr, c0:c0 + ncol]
        dma(dst3, src, cvkeys, wkeys)

    w_in4 = w_in.rearrange("(c p) (s n) -> p c s n", p=128, n=512)
    w_in3 = w_in.rearrange("(c p) n -> p c n", p=128)

    wb_in3 = wb_in.rearrange("(c p) n -> p c n", p=128)

    def load_group_weights(g):
        slot = wslot(g)
        if g < 4:
            j = g
            segs = [(128 * j, 128, 0), (512 + 128 * j, 128, 128), (1024 + 128 * j, 128, 256)]
        else:
            r = g - 4
            segs = [(1536 + 64 * r, 64, 0), (1792 + 64 * r, 64, 64), (2048 + 128 * r, 128, 128),
                    (2560 + 128 * r, 128, 256)]
        for si, (c0, wd, o0) in enumerate(segs):
            dma(wg[slot][:, :, o0:o0 + wd], wb_in3[:, :, c0:c0 + wd], [('cv', 'in', c0 // 512)], [('wg', slot, si)])

    def load_group_weights_direct(g):
        slot = wslot(g)
        if g < 4:
            segs = [(128 * g, 128, 0), (512 + 128 * g, 128, 128), (1024 + 128 * g, 128, 256)]
        else:
            r = g - 4
            segs = [(1536 + 64 * r, 64, 0), (1792 + 64 * r, 64, 64), (2048 + 128 * r, 128, 128),
                    (2560 + 128 * r, 128, 256)]
        for si, (c0, wd, o0) in enumerate(segs):
            P.add('pool', lambda e, o0=o0, wd=wd, c0=c0: e.dma_start(out=wg[slot][:, :, o0:o0 + wd],
                                                                     in_=w_in3[:, :, c0:c0 + wd]),
                  [], [('wg', slot, si)], dma=True)

    def wgk(slot):
        return [('wg', slot, si) for si in range(4)]

    csM2 = V(O_MC, 2048, F32)
    csM = csM2.rearrange("p (t c d) -> p t c d", t=NT, c=2)
    maskT = V(O_MC + 2048, 2048, F32).rearrange("p (h i) -> p h i", h=4)
    maskbias = V(O_MC + 4096, 256)
    csR2 = V(O_STAGE, 16384, F32)
    csR = csR2.rearrange("p (t a c d) -> p t a c d", t=NT, a=2, c=2)
    dec = stats[:, 500:508]
    dma(ident, c_ident, [], ['ident'])
    dma(csM2, c_csM, [], ['csM'])
    dma(gain[0], g_d["g_pre_mix"].partition_broadcast(128), [], [('gain', 0)])
    dma(gain[1], g_d["g_mem"].partition_broadcast(128), [], [('gain', 1)])

    NXB = 4
    xb = [V(O_RET + i * 4096, 4096, F32) for i in range(NXB)]
    xn = [V(O_RET + 16384 + i * 2048, 2048) for i in range(2)]
    junk = V(O_RET + 20480, 2048)
    A0_KEYS = [('xb', i) for i in range(NXB)] + [('xn', 0), ('xn', 1), 'junk']

    a0_ctr = [0]

    def a0_tile(src_ap, gidx, dstT, dkeys):
        i = a0_ctr[0]; a0_ctr[0] += 1
        s = i % 2
        sx = i % NXB
        dma(xb[sx], src_ap, [], [('xb', sx)])
        ss, ssk = newstat()
        act(junk, xb[sx], AF.Square, [('xb', sx)], [ssk, 'junk'], accum_out=ss)
        r, rk = rstd_from_ss(ss, ssk, 1.0 / D, EPS)
        stt(xn[s], xb[sx], r, gain[gidx], ALU.mult, ALU.mult, [('xb', sx), rk, ('gain', gidx)], [('xn', s)])
        def tail(i=i, s=s, dstT=dstT, dkeys=dkeys):
            b = 2 + (i % 2)
            pT = Bb(b)
            for c in range(8):
                P.add('pe', lambda e, c=c, pT=pT, s=s: e.transpose(out=pT[:, c * 128:(c + 1) * 128],
                                                                   in_=xn[s][:, c * 128:(c + 1) * 128], identity=ident),
                      [('xn', s), 'ident'], [BK(b)])
            acopy(dstT, pT.rearrange("p (c t) -> p c t", c=8), [BK(b)], dkeys)
        a0_pend.append(tail)
        while len(a0_pend) > 1:
            a0_pend.pop(0)()

    a0_pend = []

    load_group_weights_direct(0)
    load_group_weights_direct(4)
    for m in range(int(os.environ.get('DBG_NMEM', '2'))):
        a0_tile(mem_d[m * 128:(m + 1) * 128, :], 1, memT[:, :, m * 128:(m + 1) * 128], [('memT', m)])
    dma(csR2, c_csR, [], ['csR'])
    dma(maskT, c_maskT.rearrange("p (h i) -> p h i", h=4), [], ['maskT'])
    dma(maskbias, c_maskbias, [], ['maskbias'])
    dma(dec, c_dec, [], ['dec'])
    for T in range(int(os.environ.get('DBG_NX', str(NT)))):
        a0_tile(x_d[T * 128:(T + 1) * 128, :], 0, hT[:, :, T * 128:(T + 1) * 128], [('hT', T)])
    while a0_pend:
        a0_pend.pop(0)()

    REG['hT'] = (V(O_HT, 32768), BF16, [128, 16384]); REG['memT'] = (V(O_MEMT, 4096), BF16, [128, 2048])
    if level <= 1:
        P.barrier()
        return finish()

    qTa = V(O_MOBA, 8192).rearrange("p (h t) -> p h t", h=2)
    kTa = V(O_MOBA + 8192, 8192).rearrange("p (h t) -> p h t", h=2)
    va = V(O_MOBA + 16384, 4224).rearrange("p (t h d) -> p t h d", t=NT, h=2)
    rT = V(O_RET, 12288).rearrange("p (a t) -> p a t", a=3)
    rv = V(O_RET + 12288, 4096).rearrange("p (t d) -> p t d", t=NT)
    kd = V(O_RET + 16384, 2048).rearrange("p (t d) -> p t d", t=NT)
    sg = V(O_RET + 18432, 4096).rearrange("p (t d) -> p t d", t=NT)

    to = [O_TMP]

    def talloc(nbytes, dt=BF16):
        o = to[0]
        to[0] += (nbytes + 63) // 64 * 64
        return V(o, nbytes, dt)

    PTb = [talloc(1024) for _ in range(4)]
    qk_tok = [talloc(576).rearrange("p (h d) -> p h d", h=4) for _ in range(4)]
    tallb = [talloc(512, F32).rearrange("p (h c d) -> p h c d", h=4, c=2) for _ in range(2)]
    g8b = [talloc(64, F32).rearrange("p (h d) -> p h d", h=2) for _ in range(2)]
    m8b = [talloc(64, F32).rearrange("p (h d) -> p h d", h=2) for _ in range(2)]
    kmT = talloc(32).rearrange("p (h d) -> p h d", h=2)
    kms = talloc(8, F32)
    recb = [talloc(4, F32) for _ in range(4)]
    tallr = [talloc(1024, F32).rearrange("p (a c d) -> p a c d", a=2, c=2) for _ in range(2)]
    qkr = [talloc(512, F32).rearrange("p (a d) -> p a d", a=2) for _ in range(2)]
    rtok = [talloc(384).rearrange("p (a d) -> p a d", a=3) for _ in range(2)]
    attnb = [talloc(256) for _ in range(2)]
    thb = [talloc(512, F32) for _ in range(2)]
    state_f = talloc(512, F32)
    state_b = [talloc(256) for _ in range(2)]
    assert to[0] <= ARENA_BYTES, to[0]

    memset('pool', qTa[64:128, :, :], 0.0, [('qTb', T) for T in range(NT)])
    memset('pool', kTa[64:128, :, :], 0.0, [('kind', 0), ('kind', 1)])
    for h in range(2):
        dma(kTa[64:72, h, :], c_kind, [], [('kind', h)])
    memset('pool', va[:, :, :, 64:65], 1.0, [('va1',)])
    memset('pool', kmT[0:64, :, :], 0.0, ['kmT'])
    memset('pool', stats[:, 509:510], 0.0, A0_KEYS + ['ret_ok'])
    memset('pool', vc[:, :, :, 256:257], 1.0, [('vc1',)])

    sc_ctr = [0]
    o_ctr = [0]
    pt_ctr = [0]
    rec_ctr = [0]
    pvq = deque()
    PV_LAG = 2
    junk2 = [talloc(256), talloc(128)]
    j2c = [0]

    def proj(u, g, T):
        b = u % 2
        slot = wslot(g)
        for c in range(8):
            mm(Bk(b)[:, 0:384], hT[:, c, T * 128:(T + 1) * 128], wg[slot][:, c, 0:384], c == 0, c == 7,
               [('hT', T)] + wgk(slot), [BK(b)])

    def flush_pv(n_keep):
        while len(pvq) > n_keep:
            pvq.popleft()()

    def m1(u, j, T):
        if T == 0:
            flush_pv(0)
        b = u % 2
        k = T % 2
        pb = Bk(b); pk = BK(b)
        slab = pb[:, 0:256].rearrange("p (h d) -> p h d", h=4)
        qt = qk_tok[T % 4]; qk = T % 4
        acopy(va[:, T, :, 0:64], pb[:, 256:384].rearrange("p (h d) -> p h d", h=2), [pk, ('va1',)], [('va', T)])
        tcopy('dve', qt[:, :, 16:64], slab[:, :, 16:64], [pk], [('qkt', qk, 2)])
        in0 = slab[:, :, 0:16].unsqueeze(2).broadcast_to([128, 4, 2, 16])
        in1 = csM[:, T, :, :].unsqueeze(1).broadcast_to([128, 4, 2, 16])
        ta = tallb[k]
        tt('dve', ta, in0, in1, ALU.mult, [pk, 'csM'], [('tall', k)])
        tt('pool', qt[:, :, 0:8], ta[:, :, 0, 0:8], ta[:, :, 1, 8:16], ALU.subtract, [('tall', k)], [('qkt', qk, 0)])
        tt('pool', qt[:, :, 8:16], ta[:, :, 0, 8:16], ta[:, :, 1, 0:8], ALU.add, [('tall', k)], [('qkt', qk, 1)])

    def m2(u, j, T):
        qt = qk_tok[T % 4]; qk = T % 4
        trb = Bb(2)
        for i in range(4):
            P.add('pe', lambda e, i=i, qt=qt, trb=trb: e.transpose(out=trb[0:64, i * 128:(i + 1) * 128],
                                                                   in_=qt[:, i, 0:64], identity=ident),
                  [('qkt', qk, 0), ('qkt', qk, 1), ('qkt', qk, 2), 'ident'], [BK(2)])
        tcopy('dve', qTa[0:64, :, T * 128:(T + 1) * 128], trb[0:64, 0:256].rearrange("p (h t) -> p h t", h=2),
              [BK(2)], [('qTa', T)])
        tcopy('dve', kTa[0:64, :, T * 128:(T + 1) * 128], trb[0:64, 256:512].rearrange("p (h t) -> p h t", h=2),
              [BK(2)], [('kTa', T)])
        if T % 2 == 1 and T // 2 <= 6:
            bidx = T // 2
            P.add('dve', lambda e, bidx=bidx: e.tensor_reduce(out=kms[0:64, :], in_=kTa[0:64, :, bidx * 256:(bidx + 1) * 256],
                                                              axis=AX.X, op=ALU.add),
                  [('kTa', T - 1), ('kTa', T)], ['kms'])
            ts('dve', kmT[0:64, :, bidx], kms[0:64, :], 1.0 / 256.0, None, ALU.mult, None, ['kms'], ['kmT'])

    def m3(u, j, T):
        if T < 8:
            return
        k = T % 2
        qt = qk_tok[T % 4]; qk = T % 4
        cur = T // 2
        for h in range(2):
            mm(Bk(3)[:, h * 8:(h + 1) * 8], qTa[0:64, h, T * 128:(T + 1) * 128], kmT[0:64, h, :], True, True,
               [('qTa', T), 'kmT'], [BK(3)])
        memset('pool', g8b[k], -1e30, [('g8', k)])
        memset('pool', g8b[k][:, :, cur:cur + 1], 1e30, [('g8', k)])
        tcopy('dve', g8b[k][:, :, 0:cur], Bk(3)[:, 0:16].rearrange("p (h d) -> p h d", h=2)[:, :, 0:cur],
              [BK(3)], [('g8', k)])
        for h in range(2):
            P.add('dve', lambda e, h=h: e.max(out=m8b[k][:, h, :], in_=g8b[k][:, h, :]), [('g8', k)], [('m8', k, h)])
            ts('dve', qt[:, h, 64:72], g8b[k][:, h, :], m8b[k][:, h, 3:4], 1.0, ALU.is_ge, ALU.subtract,
               [('g8', k), ('m8', k, h)], [('qkt', qk, 3 + h)])

    def m4(u, j, T):
        if T < 8:
            return
        qt = qk_tok[T % 4]; qk = T % 4
        b3 = Bb(3)
        for h in range(2):
            P.add('pe', lambda e, h=h, qt=qt, b3=b3: e.transpose(out=b3[0:72, 512 + h * 128:512 + (h + 1) * 128],
                                                                 in_=qt[:, h, 0:72], identity=ident),
                  [('qkt', qk, 0), ('qkt', qk, 1), ('qkt', qk, 2), ('qkt', qk, 3 + h), 'ident'], [BK(3)])
        acopy(qTa[64:72, :, T * 128:(T + 1) * 128], b3[64:72, 512:768].rearrange("p (h t) -> p h t", h=2),
              [BK(3)], [('qTb', T)])

    def m5(u, j, T):
        for h in range(2):
            ob = 6 + (o_ctr[0] % 2); o_ctr[0] += 1
            chunks = list(range(T + 1))
            groups = [chunks[i:i + 4] for i in range(0, len(chunks), 4)]
            for gi, grp in enumerate(groups):
                sb_ = 4 + (sc_ctr[0] % 2); sc_ctr[0] += 1
                pi = pt_ctr[0] % 4; pt_ctr[0] += 1
                Sb = Bk(sb_)
                for s, kc in enumerate(grp):
                    rd = [('kTa', kc), ('qTa', T), ('qTb', T), ('kind', h)]
                    mm(Sb[:, s * 128:(s + 1) * 128], kTa[:, h, kc * 128:(kc + 1) * 128],
                       qTa[:, h, T * 128:(T + 1) * 128], True, kc != T, rd, [BK(sb_)])
                    if kc == T:
                        mm(Sb[:, s * 128:(s + 1) * 128], ident, maskbias, False, True, ['ident', 'maskbias'], [BK(sb_)])
                n = len(grp)
                act(PTb[pi][:, 0:n * 128], Sb[:, 0:n * 128], AF.Exp, [BK(sb_)], [('PT', pi)], scale=0.125)

                def pv(grp=grp, pi=pi, ob=ob, h=h, T=T, last=(gi == len(groups) - 1), j=j):
                    Ob = Bk(ob)
                    for s, kc in enumerate(grp):
                        mm(Ob[:, 0:65], PTb[pi][:, s * 128:(s + 1) * 128], va[:, kc, h, 0:65], kc == 0, kc == T,
                           [('PT', pi), ('va', kc), ('va1',)], [BK(ob)])
                    if last:
                        ri = rec_ctr[0] % 4; rec_ctr[0] += 1
                        P.add('dve', lambda e: e.reciprocal(out=recb[ri][:, 0:1], in_=Ob[:, 64:65]), [BK(ob)], [('rec', ri)])
                        hs = 2 * j + h
                        ts('dve', mixtok[:, T, hs * 64:(hs + 1) * 64], Ob[:, 0:64], recb[ri][:, 0:1], None, ALU.mult, None,
                           [BK(ob), ('rec', ri)], [('mix', T, hs)])
                pvq.append(pv)
                flush_pv(PV_LAG)

    def r1(u, r, T):
        b = u % 2
        k = T % 2
        pb = Bk(b); pk = BK(b)
        acopy(rv[:, T, :], pb[:, 128:256], [pk, 'ret_ok'], [('rv', T)])
        act(thb[k], pb[:, 256:384], AF.Tanh, [pk], [('th', k)], scale=0.5)
        stt(sg[:, T, :], thb[k], 1.0, pb[:, 256:384], ALU.add, ALU.mult, [('th', k), pk, 'ret_ok'], [('sg', T)])
        slab2 = pb[:, 0:128].rearrange("p (a d) -> p a d", a=2)
        tr_ = tallr[k]
        tt('dve', tr_, slab2.unsqueeze(2).broadcast_to([128, 2, 2, 64]), csR[:, T, :, :, :], ALU.mult, [pk, 'csR'],
           [('tallr', k)])
        tt('pool', qkr[k][:, :, 0:32], tr_[:, :, 0, 0:32], tr_[:, :, 1, 32:64], ALU.subtract, [('tallr', k)], [('qkr', k, 0)])
        tt('pool', qkr[k][:, :, 32:64], tr_[:, :, 0, 32:64], tr_[:, :, 1, 0:32], ALU.add, [('tallr', k)], [('qkr', k, 1)])
        qk_ = [('qkr', k, 0), ('qkr', k, 1)]
        tcopy('pool', rtok[k][:, 0:2, :], qkr[k], qk_, [('rtok', k, 0)])
        ts('pool', rtok[k][:, 2, :], qkr[k][:, 0, :], dec[:, r:r + 1], 1.0, ALU.mult, ALU.mult, qk_ + ['dec'], [('rtok', k, 1)])
        ts('pool', kd[:, T, :], qkr[k][:, 1, :], dec[:, 4 + r:5 + r], 1.0, ALU.mult, ALU.mult, qk_ + ['dec', 'ret_ok'], [('kd', T)])

    def r2(u, r, T):
        k = T % 2
        trb = Bb(2)
        for i in range(3):
            P.add('pe', lambda e, i=i, trb=trb: e.transpose(out=trb[0:64, i * 128:(i + 1) * 128], in_=rtok[k][:, i, :],
                                                            identity=ident),
                  [('rtok', k, 0), ('rtok', k, 1), 'ident'], [BK(2)])
        tcopy('dve', rT[0:64, :, T * 128:(T + 1) * 128], trb[0:64, 0:384].rearrange("p (a t) -> p a t", a=3), [BK(2), 'ret_ok'], [('rT', T)])

    def r3(u, r, T):
        k = T % 2
        sb_ = 4 + (sc_ctr[0] % 2); sc_ctr[0] += 1
        tsl = slice(T * 128, (T + 1) * 128)
        mm(Bk(sb_)[:, 0:128], rT[0:64, 1, tsl], rT[0:64, 0, tsl], True, True, [('rT', T)], [BK(sb_)])
        tt('dve', attnb[k], Bk(sb_)[:, 0:128], maskT[:, r, :], ALU.mult, [BK(sb_), 'maskT'], [('attn', k)])
        if T < NT - 1:
            mm(Bk(3)[0:64, 0:128], kd[:, T, :], rv[:, T, :], True, True, [('kd', T), ('rv', T)], [BK(3)])
            if T == 0:
                tcopy('dve', state_f[0:64, :], Bk(3)[0:64, 0:128], [BK(3)], ['state_f'])
            else:
                stt(state_f[0:64, :], state_f[0:64, :], CD[r], Bk(3)[0:64, 0:128], ALU.mult, ALU.add,
                    [BK(3), 'state_f'], ['state_f'])
            tcopy('pool', state_b[(T + 1) % 2][0:64, :], state_f[0:64, :], ['state_f'], [('state_b', (T + 1) % 2)])

    def r4(u, r, T):
        k = T % 2
        ob = 4 + (sc_ctr[0] % 2); sc_ctr[0] += 1
        tsl = slice(T * 128, (T + 1) * 128)
        mm(Bk(ob)[:, 0:128], attnb[k], rv[:, T, :], True, T == 0, [('attn', k), ('rv', T)], [BK(ob)])
        if T > 0:
            mm(Bk(ob)[:, 0:128], rT[0:64, 2, tsl], state_b[T % 2][0:64, :], False, True,
               [('rT', T), ('state_b', T % 2)], [BK(ob)])
        ss, ssk = newstat()
        act(junk2[0], Bk(ob)[:, 0:128], AF.Square, [BK(ob)], [ssk, 'junk2'], accum_out=ss)
        r4_state[(r, T)] = rstd_from_ss(ss, ssk, 4.0 / 128.0, 4.0 * EPS)
        tcopy('dve', outc[k], Bk(ob)[:, 0:128], [BK(ob)], [('outc', k)])

    def r5(u, r, T):
        k = T % 2
        rr, rk = r4_state[(r, T)]
        stt(mixtok[:, T, 512 + r * 128:512 + (r + 1) * 128], outc[k], rr, sg[:, T, :], ALU.mult, ALU.mult,
            [('outc', k), rk, ('sg', T)], [('mix', T, 8 + r)])

    r4_state = {}
    outc = [V(O_MC + 4352 + i * 512, 512, F32) for i in range(2)]

    MST = [None, m1, m2, m3, m4, m5]
    RST = [None, r1, r2, r3, r4, r5]

    PAD = 2
    mu = []
    for g in range(min(ngroups, 4)):
        mu += [(g, T) for T in range(NT)] + [None] * PAD
    ru = [(g, T) for g in range(4, ngroups) for T in range(NT)]
    units = []
    for i in range(max(len(mu), len(ru))):
        units.append(mu[i] if i < len(mu) else None)
        units.append(ru[i] if i < len(ru) else None)
    NU = len(units)
    DEPTH = 6
    HT_KEYS = [('hT', T) for T in range(NT)]

    def phaseC_weight_loads():
        emit_conv(len(conv_jobs))
        for hf in range(2):
            load_b16(wout_sb[:, :, hf * 512:(hf + 1) * 512], wb_out, 0, 8, hf * 512, 512, [('wout',)] + HT_KEYS,
                     [('cv', 'out', hf)])
        for hf in range(2):
            load_b16(wco_sb[:, :, hf * 512:(hf + 1) * 512], wb_co, 0, 8, hf * 512, 512, [('wco',)] + wgk(2) + wgk(3),
                     [('cv', 'co', hf)])
        for hf in range(2):
            load_b16(wcq_sb[:, :, hf * 512:(hf + 1) * 512], wb_cq, 0, 8, hf * 512, 512, [('wcq',)] + HT_KEYS,
                     [('cv', 'cq', hf)])
        for cg in range(2):
            load_b16(wg[cg][:, :, :], wb_ckv, 0, 8, cg * 512, 512, wgk(cg), [('cv', 'ckv', cg)])

    def run_stage(kst, p):
        if not (0 <= p < NU) or units[p] is None:
            return
        g, T = units[p]
        if g < 4:
            if kst < len(MST):
                MST[kst](p, g, T)
        else:
            if kst < len(RST):
                RST[kst](p, g - 4, T)

    LAST_PROJ = max(p for p in range(NU) if units[p] is not None)
    for step in range(NU + DEPTH):
        if step >= 2 and step % 2 == 0:
            emit_conv(1)
        if step == 16:
            load_group_weights(1)
            load_group_weights(5)
        run_stage(1, step - 1)
        if step < NU and units[step] is not None:
            g, T = units[step]
            if T == 0 and g in (1, 2):
                load_group_weights(g + 1)
            if T == 0 and g in (5, 6) and g + 1 < ngroups:
                load_group_weights(g + 1)
            proj(step, g, T)
        if step == LAST_PROJ + 1:
            phaseC_weight_loads()
        for kst in range(DEPTH - 1, 1, -1):
            run_stage(kst, step - kst)
    flush_pv(0)
    REG['mixtok'] = (V(O_MIX, 32768), BF16, [128, 16384])
    REG['qTa'] = (V(O_MOBA, 8192), BF16, [128, 4096]); REG['kTa'] = (V(O_MOBA + 8192, 8192), BF16, [128, 4096])
    REG['va'] = (V(O_MOBA + 16384, 4224), BF16, [128, 2112])
    if level <= 2:
        P.barrier()
        return finish()

    kvb = [0]
    for cg in range(4):
        slot = cg % 2
        if cg >= 2:
            load_b16(wg[slot][:, :, :], wb_ckv, 0, 8, cg * 512, 512, wgk(slot), [('cv', 'ckv', cg)])
        if cg < 2:
            for fl in range(4):
                fc = cg * 4 + fl
                b = kvb[0] % 2; kvb[0] += 1
                for c in range(8):
                    mm(Bk(b)[:, 0:256], wg[slot][:, c, fl * 128:(fl + 1) * 128], memT[:, c, :], c == 0, c == 7,
                       wgk(slot) + [('memT', 0), ('memT', 1)], [BK(b)])
                acopy(kcT[:, fc, :], Bk(b)[:, 0:256], [BK(b)], [('kcT',)])
        else:
            hv = cg - 2
            for m in range(2):
                b = kvb[0] % 2; kvb[0] += 1
                for c in range(8):
                    mm(Bk(b)[:, 0:512], memT[:, c, m * 128:(m + 1) * 128], wg[slot][:, c, :], c == 0, c == 7,
                       wgk(slot) + [('memT', m)], [BK(b)])
                acopy(vc[:, m, 2 * hv:2 * hv + 2, 0:256], Bk(b)[:, 0:512].rearrange("p (h d) -> p h d", h=2),
                      [BK(b), ('vc1',)], [('vc', m)])

    dma(gain[0], g_d["g_post_mix"].partition_broadcast(128), [], [('gain', 0)])
    dma(gain[1], g_d["g_pre_cross"].partition_broadcast(128), [], [('gain', 1)])
    dma(gain[2], g_d["g_post_cross"].partition_broadcast(128), [], [('gain', 2)])

    P.barrier()
    REG['kcT'] = (V(O_KCT, 4096), BF16, [128, 2048]); REG['vc'] = (V(O_VC, 4128), BF16, [128, 2064])
    if level <= 3:
        return finish()

    co = [O_MOBA]

    def calloc(nbytes, dt=BF16):
        o = co[0]
        co[0] += (nbytes + 63) // 64 * 64
        return V(o, nbytes, dt)

    h2T = V(O_WG, 8192).rearrange("p (c t) -> p c t", c=8)
    qcT = V(O_WG + 8192, 8192).rearrange("p (c t) -> p c t", c=8)
    x1 = calloc(16384, F32).rearrange("p (t f) -> p t f", t=4)
    xl = [calloc(4096, F32) for _ in range(4)]
    tmpf = [calloc(4096, F32) for _ in range(2)]
    xn2 = [calloc(2048) for _ in range(2)]
    junkcb = [calloc(2048), calloc(2048)]
    jcc = [0]

    def njunkc():
        i = jcc[0] % 2; jcc[0] += 1
        return junkcb[i], ('junkc', i)
    mixT = [calloc(2048).rearrange("p (c t) -> p c t", c=8) for _ in range(2)]
    PTc = [V(O_STAGE + 2048 + i * 1024, 1024) for i in range(8)]
    oc_tok = calloc(8192).rearrange("p (t f) -> p t f", t=4)
    ocT = [calloc(2048).rearrange("p (c t) -> p c t", c=8) for _ in range(2)]
    recc = [calloc(64, F32) for _ in range(4)]
    assert co[0] <= ARENA_BYTES, co[0]

    wb_ctr = [0]

    def workbank():
        b = 5 + (wb_ctr[0] % 3); wb_ctr[0] += 1
        return b

    def transposes8(src_fn, rkeys, dst, dkeys, eng='act', bank=4):
        pT = Bb(bank)
        for c in range(8):
            P.add('pe', lambda e, c=c: e.transpose(out=pT[:, c * 128:(c + 1) * 128], in_=src_fn(c), identity=ident),
                  list(rkeys) + ['ident'], [BK(bank)])
        if eng == 'act':
            acopy(dst, pT.rearrange("p (c t) -> p c t", c=8), [BK(bank)], dkeys)
        else:
            tcopy(eng, dst, pT.rearrange("p (c t) -> p c t", c=8), [BK(bank)], dkeys)

    def norm_res_two_banks(b0, b1, gidx, res_ap, res_keys, out_ap, out_keys, ti, var_mult, var_add):
        assert b1 == b0 + 1 and b0 % 2 == 0
        keys = [BK(b0), BK(b1)]
        acc = Bk2(b0 // 2)
        ss, ssk = newstat()
        jb, jk = njunkc()
        act(jb, acc, AF.Square, keys, [ssk, jk, (jk, 1)], accum_out=ss)
        r, rk = rstd_from_ss(ss, ssk, var_mult, var_add)
        tk = ('tmpf', ti)
        stt(tmpf[ti], acc, r, gain[gidx], ALU.mult, ALU.mult, keys + [rk, ('gain', gidx)], [tk])
        tt('dve', out_ap, tmpf[ti], res_ap, ALU.add, [tk] + list(res_keys), out_keys)

    wgu_pre = V(0, 90112).rearrange("p (c n) -> p c n", c=8)
    wb_gu3 = wb_gu.rearrange("(c p) n -> p c n", p=128)
    C1P = [(0, 1), (2, 3), (6, 7), (0, 1)]
    C3P = [(2, 3), (0, 1), (6, 7), (2, 3)]
    for s4 in range(4):
        def c1_a(Tl):
            T = s4 * 4 + Tl
            k = Tl % 2
            dma(xl[Tl], x_d[T * 128:(T + 1) * 128, :], [], [('xl', Tl)])
            transposes8(lambda c, T=T: mixtok[:, T, c * 128:(c + 1) * 128], [('mix', T, hs) for hs in range(12)],
                        mixT[k], [('mixT', k)], bank=4 + Tl % 2)

        def c1_b(Tl):
            k = Tl % 2
            bp = C1P[Tl]
            for half in range(2):
                for c in range(8):
                    mm(Bk(bp[half]), mixT[k][:, c, :], wout_sb[:, c, half * 512:(half + 1) * 512], c == 0, c == 7,
                       [('mixT', k), ('wout',)], [BK(bp[half])])

        def c1_ab(Tl):
            if Tl == 0:
                c1_a(0)
            if Tl + 1 < 4:
                c1_a(Tl + 1)
            c1_b(Tl)

        c1s = {}

        def c1_cA(Tl):
            bp = C1P[Tl]
            ss, ssk = newstat()
            jb, jk = njunkc()
            act(jb, Bk2(bp[0] // 2), AF.Square, [BK(bp[0]), BK(bp[1])], [ssk, jk, (jk, 1)], accum_out=ss)
            c1s[('r1', Tl)] = rstd_from_ss(ss, ssk, 1.0 / D, EPS)

        def c1_cB(Tl):
            k = Tl % 2
            bp = C1P[Tl]
            r, rk = c1s[('r1', Tl)]
            tk = ('tmpf', k)
            stt(tmpf[k], Bk2(bp[0] // 2), r, gain[0], ALU.mult, ALU.mult, [BK(bp[0]), BK(bp[1]), rk, ('gain', 0)], [tk])
            tt('dve', x1[:, Tl, :], tmpf[k], xl[Tl], ALU.add, [tk, ('xl', Tl)], [('x1', Tl)])

        def c1_cC(Tl):
            ss, ssk = newstat()
            jb, jk = njunkc()
            act(jb, x1[:, Tl, :], AF.Square, [('x1', Tl)], [ssk, jk, (jk, 1)], accum_out=ss)
            c1s[('r2', Tl)] = rstd_from_ss(ss, ssk, 1.0 / D, EPS)

        def c1_cD(Tl):
            k = Tl % 2
            r, rk = c1s[('r2', Tl)]
            stt(xn2[k], x1[:, Tl, :], r, gain[1], ALU.mult, ALU.mult, [('x1', Tl), rk, ('gain', 1)], [('xn2', k)])

        def c1_e(Tl):
            k = Tl % 2
            transposes8(lambda c, k=k: xn2[k][:, c * 128:(c + 1) * 128], [('xn2', k)],
                        h2T[:, :, Tl * 128:(Tl + 1) * 128], [('h2T', Tl)], bank=4 + Tl % 2)

        c1_ab(0); c1_ab(1); c1_cA(0)
        c1_ab(2); c1_cA(1); c1_cB(0)
        c1_ab(3); c1_cA(2); c1_cB(1); c1_cC(0)
        c1_cA(3); c1_cB(2); c1_cC(1); c1_cD(0)
        c1_cB(3); c1_cC(2); c1_cD(1); c1_e(0)
        c1_cC(3); c1_cD(2); c1_e(1)
        c1_cD(3); c1_e(2)
        c1_e(3)
        if s4 == 3:
            for c in range(4):
                dma(wgu_pre[:, c:c + 1, :], wb_gu3[:, c:c + 1, :], [],
                    [('wgu_pre', c), ('wout',)] + [('mix', T, hs) for T in range(NT) for hs in range(12)])
        H2K = [('h2T', t) for t in range(4)]
        for fc in range(8):
            b = workbank()
            for c in range(8):
                mm(Bk(b), wcq_sb[:, c, fc * 128:(fc + 1) * 128], h2T[:, c, :], c == 0, c == 7, H2K + [('wcq',)], [BK(b)])
            acopy(qcT[:, fc, :], Bk(b), [BK(b)], [('qcT', fc)])
        if s4 == 3:
            dma(wgu_pre[:, 4:5, :], wb_gu3[:, 4:5, :], [], [('wgu_pre', 4), ('wout',), ('wcq',)])
        for h in range(4):
            for m in range(2):
                b = workbank()
                for jj in range(2):
                    mm(Bk(b), kcT[:, 2 * h + jj, m * 128:(m + 1) * 128], qcT[:, 2 * h + jj, :], jj == 0, jj == 1,
                       [('kcT',), ('qcT', 2 * h + jj)], [BK(b)])
                pi = 2 * h + m
                act(PTc[pi], Bk(b), AF.Exp, [BK(b)], [('PTc', pi)], scale=1.0 / 16.0)
        for h in range(4):
            for Tl in range(4):
                b = workbank()
                for m in range(2):
                    mm(Bk(b)[:, 0:257], PTc[2 * h + m][:, Tl * 128:(Tl + 1) * 128], vc[:, m, h, 0:257], m == 0, m == 1,
                       [('PTc', 2 * h + m), ('vc', m), ('vc1',)], [BK(b)])
                ri = (h * 4 + Tl) % 4
                P.add('dve', lambda e, ri=ri, b=b: e.reciprocal(out=recc[ri][:, 0:1], in_=Bk(b)[:, 256:257]), [BK(b)],
                      [('recc', ri)])
                ts('dve', oc_tok[:, Tl, h * 256:(h + 1) * 256], Bk(b)[:, 0:256], recc[ri][:, 0:1], None, ALU.mult, None,
                   [BK(b), ('recc', ri)], [('oc', Tl, h)])
        c3s = {}

        def c3_a(Tl):
            k = Tl % 2
            transposes8(lambda c, Tl=Tl: oc_tok[:, Tl, c * 128:(c + 1) * 128], [('oc', Tl, h) for h in range(4)],
                        ocT[k], [('ocT', k)], bank=4 + Tl % 2)

        def c3_b(Tl):
            k = Tl % 2
            bp = C3P[Tl]
            for half in range(2):
                for c in range(8):
                    mm(Bk(bp[half]), ocT[k][:, c, :], wco_sb[:, c, half * 512:(half + 1) * 512], c == 0, c == 7,
                       [('ocT', k), ('wco',)], [BK(bp[half])])

        def c3_ab(Tl):
            if Tl == 0:
                c3_a(0)
            if Tl + 1 < 4:
                c3_a(Tl + 1)
            c3_b(Tl)

        def c3_cA(Tl):
            bp = C3P[Tl]
            ss, ssk = newstat()
            jb, jk = njunkc()
            act(jb, Bk2(bp[0] // 2), AF.Square, [BK(bp[0]), BK(bp[1])], [ssk, jk, (jk, 1)], accum_out=ss)
            c3s[Tl] = rstd_from_ss(ss, ssk, 1.0 / D, EPS)

        def c3_cB(Tl):
            T = s4 * 4 + Tl
            k = Tl % 2
            bp = C3P[Tl]
            r, rk = c3s[Tl]
            tk = ('tmpf', k)
            stt(tmpf[k], Bk2(bp[0] // 2), r, gain[2], ALU.mult, ALU.mult, [BK(bp[0]), BK(bp[1]), rk, ('gain', 2)], [tk])
            tt('dve', xl[Tl], tmpf[k], x1[:, Tl, :], ALU.add, [tk, ('x1', Tl)], [('xl', Tl)])
            dma(x2s[T * 128:(T + 1) * 128, :], xl[Tl], [('xl', Tl)], [('x2s', T)])

        c3_ab(0); c3_ab(1); c3_cA(0)
        c3_ab(2); c3_cA(1); c3_cB(0)
        c3_ab(3); c3_cA(2); c3_cB(1)
        c3_cA(3); c3_cB(2)
        c3_cB(3)

    P.barrier()
    if level <= 4:
        return finish()

    wgu_sb = V(0, 90112).rearrange("p (c n) -> p c n", c=8)
    wdn_sb = V(139648, 45056).rearrange("p (c n) -> p c n", c=NFC)
    actT = V(O_WG, 11264).rearrange("p (c t) -> p c t", c=NFC)
    h3Tb = [V(O_WG + 11264, 4096).rearrange("p (c t) -> p c t", c=8),
            V(O_STAGE + 12288, 4096).rearrange("p (c t) -> p c t", c=8)]
    do = [184704]

    def dalloc(nbytes, dt=BF16):
        o = do[0]
        do[0] += (nbytes + 63) // 64 * 64
        return V(o, nbytes, dt)

    x2l = [dalloc(4096, F32) for _ in range(4)]
    xn3 = [dalloc(2048) for _ in range(2)]
    sgate = [dalloc(1024, F32) for _ in range(3)]
    assert do[0] <= ARENA_BYTES, do[0]
    junkdb = [V(90112, 2048), V(92160, 2048)]
    jdc = [0]

    def njunkd():
        i = jdc[0] % 2; jdc[0] += 1
        return junkdb[i], ('junkd', i)
    obuf = [V(O_STAGE + i * 4096, 4096, F32) for i in range(2)]
    tmpD = V(O_STAGE + 8192, 4096, F32)

    dma(gain[0], g_d["g_pre_ffn"].partition_broadcast(128), [], [('gain', 0)])
    dma(gain[1], g_d["g_post_ffn"].partition_broadcast(128), [], [('gain', 1)])
    STAGE_KEYS = [('stage', s, k) for s in range(2) for k in range(4)]

    first_stage_overlay = [True]
    first_h3_overlay = [True]

    d1_state = {}

    def d1a(u8, Tl):
        T = u8 * 2 + Tl
        xi = (u8 % 2) * 2 + Tl
        dma(x2l[xi], x2s[T * 128:(T + 1) * 128, :], [('x2s', T)], [('x2l', xi)])
        ss, ssk = newstat()
        jb, jk = njunkd()
        act(jb, x2l[xi], AF.Square, [('x2l', xi)], [ssk, (jk, 0), (jk, 1)], accum_out=ss)
        d1_state[(u8, Tl)] = rstd_from_ss(ss, ssk, 1.0 / D, EPS)

    def d1b(u8, Tl):
        xi = (u8 % 2) * 2 + Tl
        r, rk = d1_state[(u8, Tl)]
        stt(xn3[Tl], x2l[xi], r, gain[0], ALU.mult, ALU.mult, [('x2l', xi), rk, ('gain', 0)], [('xn3', Tl)])

    def d1c(u8, Tl):
        h3T = h3Tb[u8 % 2]
        extra = []
        if u8 % 2 == 1 and first_h3_overlay[0]:
            extra = STAGE_KEYS
            first_h3_overlay[0] = False
        transposes8(lambda c, k=Tl: xn3[k][:, c * 128:(c + 1) * 128], [('xn3', Tl)],
                    h3T[:, :, Tl * 128:(Tl + 1) * 128], [('h3T', u8 % 2, Tl)] + extra, eng='dve')

    def d1(u8, tls=(0, 1)):
        for Tl in tls:
            d1a(u8, Tl); d1b(u8, Tl); d1c(u8, Tl)

    D1_SCHED = {2: [(d1a, 0)], 4: [(d1b, 0)], 6: [(d1c, 0)], 8: [(d1a, 1)], 10: [(d1b, 1)], 12: [(d1c, 1)]}

    d4_state = {}
    d4_pending = {}

    def d4a(u8):
        for Tl in range(2):
            jb, jk = njunkd()
            ss, ssk = newstat()
            act(jb, Bk2(Tl), AF.Square, [BK(2 * Tl), BK(2 * Tl + 1)], [ssk, (jk, 0), (jk, 1)], accum_out=ss)
            d4_state[(u8, Tl)] = rstd_from_ss(ss, ssk, 1.0 / D, EPS)

    def d4b(u8):
        extra = STAGE_KEYS if first_stage_overlay[0] else []
        first_stage_overlay[0] = False
        for Tl in range(2):
            r, rk = d4_state[(u8, Tl)]
            stt(obuf[Tl], Bk2(Tl), r, gain[1], ALU.mult, ALU.mult,
                [BK(2 * Tl), BK(2 * Tl + 1), rk, ('gain', 1)], [('obuf', Tl, 0), ('obuf', Tl, 1)] + extra)

    def d4c(u8):
        for Tl in range(2):
            T = u8 * 2 + Tl
            xi = (u8 % 2) * 2 + Tl
            tt('dve', obuf[Tl], obuf[Tl], x2l[xi], ALU.add, [('obuf', Tl, 0), ('obuf', Tl, 1), ('x2l', xi)],
               [('obuf', Tl, 0), ('obuf', Tl, 1)])
            dma(out_d[T * 128:(T + 1) * 128, :], obuf[Tl], [('obuf', Tl, 0), ('obuf', Tl, 1)], [('out', T)])

    DLAG = 3
    d1(0)
    for gq in range(6):
        nfc = 4 if gq < 5 else 2
        ncol = nfc * 128
        for (kind, base) in (('g', 0), ('u', DFF)):
            c0 = base + gq * 512
            load_b16(wgu_sb[:, 5:8, c0:c0 + ncol], wb_gu, 5, 3, c0, ncol, [('wgu', kind, gq)], [])
        load_b16(wdn_sb[:, gq * 4:gq * 4 + nfc, :], wb_dn, gq * 4, nfc, 0, 1024,
                 [('wdn', rb) for rb in range(gq * 2, gq * 2 + nfc // 2)], [])
    for u8 in range(8):
        h3T = h3Tb[u8 % 2]
        H3K = [('h3T', u8 % 2, 0), ('h3T', u8 % 2, 1)]

        def down(fc):
            for Tl in range(2):
                for half in range(2):
                    b = Tl * 2 + half
                    mm(Bk(b), actT[:, fc, Tl * 128:(Tl + 1) * 128], wdn_sb[:, fc, half * 512:(half + 1) * 512],
                       fc == 0, fc == NFC - 1, [('actT', fc), ('wdn', fc // 2)], [BK(b)])

        for fc in range(NFC):
            b = 5 + (fc % 3)
            for c in range(8):
                mm(Bk(b)[:, 0:256], wgu_sb[:, c, fc * 128:(fc + 1) * 128], h3T[:, c, :], c == 0, c == 7,
                   H3K + [('wgu', 'g', fc // 4)], [BK(b)])
            for c in range(8):
                mm(Bk(b)[:, 256:512], wgu_sb[:, c, DFF + fc * 128:DFF + (fc + 1) * 128], h3T[:, c, :], c == 0, c == 7,
                   H3K + [('wgu', 'u', fc // 4)], [BK(b)])
            k = fc % 3
            act(sgate[k], Bk(b)[:, 0:256], AF.Silu, [BK(b)], [('sgate', k)])
            tt('dve', actT[:, fc, :], sgate[k], Bk(b)[:, 256:512], ALU.mult, [('sgate', k), BK(b)], [('actT', fc)])
            if fc >= DLAG:
                down(fc - DLAG)
            if fc in d4_pending:
                fn_, u_ = d4_pending.pop(fc)
                fn_(u_)
            if u8 + 1 < 8:
                for (fn, Tl) in D1_SCHED.get(fc, ()):
                    fn(u8 + 1, Tl)
        for fc in range(NFC - DLAG, NFC):
            down(fc)
        d4a(u8)
        if u8 == 7:
            d4b(u8); d4c(u8)
        else:
            d4_pending[1] = (d4b, u8)
            d4_pending[2] = (d4c, u8)

    return finish()


_NC_CACHE = {}


def kernel(x, mem, g_pre_mix, w_in, w_out, g_post_mix, g_pre_cross, g_mem, w_cq, w_ckv, w_co,
           g_post_cross, g_pre_ffn, w_gate_up, w_down, g_post_ffn):
    f32 = lambda a: np.ascontiguousarray(np.asarray(a, dtype=np.float32))
    x = f32(x); mem = f32(mem)
    B = x.shape[0]
    if 'nc' not in _NC_CACHE:
        _NC_CACHE['nc'] = build_program()
    nc = _NC_CACHE['nc']
    shared = {
        "g_pre_mix": f32(g_pre_mix).reshape(1, D), "g_post_mix": f32(g_post_mix).reshape(1, D),
        "g_pre_cross": f32(g_pre_cross).reshape(1, D), "g_mem": f32(g_mem).reshape(1, D),
        "g_post_cross": f32(g_post_cross).reshape(1, D), "g_pre_ffn": f32(g_pre_ffn).reshape(1, D),
        "g_post_ffn": f32(g_post_ffn).reshape(1, D),
        "w_in": f32(w_in).reshape(D, 3072), "w_out": f32(w_out).reshape(D, D), "w_cq": f32(w_cq).reshape(D, D),
        "w_ckv": f32(w_ckv).reshape(D, 2 * D), "w_co": f32(w_co).reshape(D, D),
        "w_gate_up": f32(w_gate_up).reshape(D, 2 * DFF), "w_down": f32(w_down).reshape(DFF, D),
        "c_ident": _CONST['ident'], "c_maskbias": _CONST['maskbias'], "c_kind": _CONST['kind'],
        "c_csM": _CONST['csM'], "c_csR": _CONST['csR'],
        "c_dec": _CONST['dec'], "c_maskT": _CONST['maskT'],
    }
    in_maps = []
    for b in range(B):
        m = dict(shared)
        m["x"] = x[b]
        m["mem"] = mem[b]
        in_maps.append(m)
    res = run_bass_kernel_spmd(nc, in_maps, core_ids=list(range(B)))
    return np.stack([np.asarray(r["out"], dtype=np.float32) for r in res.results], axis=0)
```
